# Optimizing a Trainium2 kernel written in Bass

```python
import jax
import jax.numpy as jnp
from jax import lax
import numpy as np

D_MODEL = 1024
BATCH = 2
SEQ = 8192
DEPTH = 1
DEC_BATCH = 128
DEC_SEQ = 8
PAST_LEN = 8192
PAGE_SIZE = 128

MIX_W = D_MODEL
ML_W = MIX_W // 2
MLA_W = MIX_W - ML_W
ML_DK = 128
ML_DV = 128
ML_H = ML_W // ML_DV
MLA_VD = 64
MLA_H = MLA_W // MLA_VD
MLA_NOPE = 64
MLA_ROPE = 32
MLA_SCALE = (MLA_NOPE + MLA_ROPE) ** -0.5
Q_LORA = 3 * D_MODEL // 8
KV_LORA = D_MODEL // 4
ROPE_THETA = 10000.0
ML_CHUNK = 128
Q_BLOCK = 128
EPS = 1e-6
ALPHA = (2.0 * DEPTH) ** 0.25
BETA = (8.0 * DEPTH) ** -0.25
IN_SIZES = (ML_H * ML_DK, ML_H * ML_DK, ML_W, ML_H, ML_H, ML_W, ML_W, Q_LORA, KV_LORA, MLA_ROPE, MLA_W)
IN_OFFS = tuple(int(o) for o in np.cumsum(IN_SIZES)[:-1])
N_IN = int(sum(IN_SIZES))

kernel_name = 'hymba_mlstm_mla_deepnorm_adaln_step'


def _layernorm(x, g, b):
    xf = x.astype(jnp.float32)
    mu = jnp.mean(xf, -1, keepdims=True)
    var = jnp.mean(jnp.square(xf - mu), -1, keepdims=True)
    return ((xf - mu) * lax.rsqrt(var + EPS) * g + b).astype(x.dtype)


def _rmsnorm(x, g):
    xf = x.astype(jnp.float32)
    return (xf * lax.rsqrt(jnp.mean(xf * xf, -1, keepdims=True) + EPS) * g).astype(x.dtype)


def _rope(x, pos):
    half = MLA_ROPE // 2
    inv = ROPE_THETA ** (-jnp.arange(half, dtype=jnp.float32) / half)
    ang = pos.astype(jnp.float32)[:, None] * inv[None, :]
    shape = (x.shape[1],) + (1,) * (x.ndim - 3) + (half,)
    cos = jnp.cos(ang).reshape(shape)
    sin = jnp.sin(ang).reshape(shape)
    xf = x.astype(jnp.float32)
    x1, x2 = xf[..., :half], xf[..., half:]
    return jnp.concatenate([x1 * cos - x2 * sin, x2 * cos + x1 * sin], -1).astype(x.dtype)


def _front(x, c, pos, w_ada, b_ada, w_in, ml_b_i, ml_b_f, mla_q_norm, mla_kv_norm, mla_w_uq, mla_w_uk):
    B, S, _ = x.shape
    f32 = jnp.float32
    shift, scale, gate = jnp.split(c @ w_ada + b_ada, 3, axis=-1)
    h = x * (1.0 + scale[:, None, :]) + shift[:, None, :]
    mq, mk, mv, mi, mf, mo, mz, cq, ckv, kr, az = jnp.split(h @ w_in, IN_OFFS, axis=-1)

    def heads(t, d):
        return jnp.moveaxis(t.reshape(B, S, ML_H, d), 2, 1).astype(f32)

    q = heads(mq, ML_DK)
    k = heads(mk, ML_DK) * (ML_DK ** -0.5)
    v = heads(mv, ML_DV)
    ig = jnp.moveaxis((mi + ml_b_i).astype(f32), 2, 1)
    lf = jax.nn.log_sigmoid(jnp.moveaxis((mf + ml_b_f).astype(f32), 2, 1))
    cq = _rmsnorm(cq, mla_q_norm)
    qh = (cq @ mla_w_uq).reshape(B, S, MLA_H, MLA_NOPE + MLA_ROPE)
    q_rope = _rope(qh[..., MLA_NOPE:], pos)
    q_lat = jnp.einsum('bshn,chn->bshc', qh[..., :MLA_NOPE],
                       mla_w_uk.reshape(KV_LORA, MLA_H, MLA_NOPE))
    ckv = _rmsnorm(ckv, mla_kv_norm)
    kr = _rope(kr, pos)
    return gate, (q, k, v, ig, lf), mo, mz, q_lat, q_rope, ckv, kr, az


def _mlstm_chunk(carry, inp):
    C, n, m = carry
    q, k, v, ig, lf = inp
    L = q.shape[2]
    F = jnp.cumsum(lf, axis=-1)
    causal = jnp.tril(jnp.ones((L, L), bool))
    D = jnp.where(causal, F[..., :, None] - F[..., None, :] + ig[..., None, :], -jnp.inf)
    m_inter = F + m[..., None]
    m_t = jnp.maximum(m_inter, jnp.max(D, -1))
    W = jnp.exp(D - m_t[..., None])
    a = jnp.exp(m_inter - m_t)
    Sq = jnp.einsum('bhtd,bhsd->bhts', q, k) * W
    num = jnp.einsum('bhts,bhsv->bhtv', Sq, v) + a[..., None] * jnp.einsum('bhtd,bhdv->bhtv', q, C)
    den = jnp.sum(Sq, -1) + a * jnp.einsum('bhtd,bhd->bht', q, n)
    h = num / jnp.maximum(jnp.abs(den), jnp.exp(-m_t))[..., None]
    wl = W[..., -1, :]
    al = a[..., -1]
    C_new = al[..., None, None] * C + jnp.einsum('bhs,bhsd,bhsv->bhdv', wl, k, v)
    n_new = al[..., None] * n + jnp.einsum('bhs,bhsd->bhd', wl, k)
    return (C_new, n_new, m_t[..., -1]), h


def _mlstm(mlin, C0, n0, m0, chunk):
    q = mlin[0]
    B, H, S, _ = q.shape
    nc = S // chunk

    def to_chunks(t):
        return jnp.moveaxis(t.reshape((B, H, nc, chunk) + t.shape[3:]), 2, 0)

    (C, n, m), h = lax.scan(_mlstm_chunk, (C0, n0, m0), tuple(to_chunks(t) for t in mlin))
    h = jnp.moveaxis(h, 0, 2).reshape(B, H, S, ML_DV)
    return h, C, n, m


def _mla_scores(q_lat, q_rope, ckv, kr):
    s = jnp.einsum('bqhc,bkc->bhqk', q_lat, ckv) + jnp.einsum('bqhr,bkr->bhqk', q_rope, kr)
    return s.astype(jnp.float32) * MLA_SCALE


def _mla_prompt(q_lat, q_rope, ckv, kr):
    B, S = q_lat.shape[:2]
    qb = min(Q_BLOCK, S)
    nb = S // qb

    def blk(t):
        return jnp.moveaxis(t.reshape((B, nb, qb) + t.shape[2:]), 1, 0)

    kpos = jnp.arange(S)

    def one(args):
        ql, qr, i = args
        s = _mla_scores(ql, qr, ckv, kr)
        qpos = i * qb + jnp.arange(qb)
        s = jnp.where(kpos[None, :] <= qpos[:, None], s, -jnp.inf)
        p = jax.nn.softmax(s, axis=-1).astype(ckv.dtype)
        return jnp.einsum('bhqk,bkc->bqhc', p, ckv)

    o = lax.map(one, (blk(q_lat), blk(q_rope), jnp.arange(nb)))
    return jnp.moveaxis(o, 0, 1).reshape(B, S, MLA_H, KV_LORA)


def _mla_sample(q_lat, q_rope, ckv, kr, pool_ckv, pool_kr, page_table):
    B, S = q_lat.shape[:2]
    past_ckv = pool_ckv[page_table].reshape(B, -1, KV_LORA)
    past_kr = pool_kr[page_table].reshape(B, -1, MLA_ROPE)
    P = past_ckv.shape[1]
    s_past = _mla_scores(q_lat, q_rope, past_ckv, past_kr)
    s_new = _mla_scores(q_lat, q_rope, ckv, kr)
    s_new = jnp.where(jnp.tril(jnp.ones((S, S), bool)), s_new, -jnp.inf)
    p = jax.nn.softmax(jnp.concatenate([s_past, s_new], -1), axis=-1)
    o = (jnp.einsum('bhqk,bkc->bqhc', p[..., :P].astype(past_ckv.dtype), past_ckv)
         + jnp.einsum('bhqk,bkc->bqhc', p[..., P:].astype(ckv.dtype), ckv))
    return o


def _back(x, gate, h_ml, mo, mz, o_lat, az, ml_gn, mla_w_uv, w_out, ln_g, ln_b):
    B, S, _ = x.shape
    hm = jnp.moveaxis(h_ml, 1, 2)
    hm = hm * jax.nn.sigmoid(mo.astype(jnp.float32)).reshape(B, S, ML_H, ML_DV)
    mu = jnp.mean(hm, -1, keepdims=True)
    var = jnp.mean(jnp.square(hm - mu), -1, keepdims=True)
    hm = ((hm - mu) * lax.rsqrt(var + EPS)).reshape(B, S, ML_W) * ml_gn
    y_ml = hm.astype(x.dtype) * jax.nn.silu(mz)
    o_mla = jnp.einsum('bshc,chv->bshv', o_lat, mla_w_uv.reshape(KV_LORA, MLA_H, MLA_VD)).reshape(B, S, MLA_W)
    y_mla = o_mla * jax.nn.silu(az)
    out = jnp.concatenate([y_ml, y_mla], -1) @ w_out
    return _layernorm(ALPHA * x + gate[:, None, :] * out, ln_g, ln_b)


def setup_inputs(seed: int = 0) -> dict:
    key = jax.random.key(seed)
    ks = jax.random.split(key, 32)
    f32 = jnp.float32

    def nrm(k, shape, s=1.0):
        return jax.random.normal(k, shape, f32) * s

    n_pages = PAST_LEN // PAGE_SIZE
    n_used = DEC_BATCH * n_pages
    n_pool = n_used + max(1, n_used // 4)
    page_table = jax.random.permutation(ks[0], n_pool)[:n_used].reshape(DEC_BATCH, n_pages).astype(jnp.int32)
    Dm = D_MODEL
    return {
        'x_prompt': nrm(ks[1], (BATCH, SEQ, Dm)),
        'x_sample': nrm(ks[2], (DEC_BATCH, DEC_SEQ, Dm)),
        'c_prompt': nrm(ks[3], (BATCH, Dm)),
        'c_sample': nrm(ks[4], (DEC_BATCH, Dm)),
        'cache_ckv': nrm(ks[5], (DEPTH, n_pool, PAGE_SIZE, KV_LORA)),
        'cache_krope': nrm(ks[6], (DEPTH, n_pool, PAGE_SIZE, MLA_ROPE)),
        'state_C': nrm(ks[7], (DEPTH, DEC_BATCH, ML_H, ML_DK, ML_DV), 0.1),
        'state_n': jnp.abs(nrm(ks[8], (DEPTH, DEC_BATCH, ML_H, ML_DK), 0.5)),
        'state_m': nrm(ks[9], (DEPTH, DEC_BATCH, ML_H)),
        'page_table': page_table,
        'w_ada': nrm(ks[10], (DEPTH, Dm, 3 * Dm), 0.5 * Dm ** -0.5),
        'b_ada': nrm(ks[11], (DEPTH, 3 * Dm), 0.02),
        'w_in': nrm(ks[12], (DEPTH, Dm, N_IN), Dm ** -0.5),
        'ml_b_i': nrm(ks[13], (DEPTH, ML_H), 0.1),
        'ml_b_f': jnp.broadcast_to(jnp.linspace(3.0, 6.0, ML_H, dtype=f32), (DEPTH, ML_H)) + nrm(ks[14], (DEPTH, ML_H), 0.01),
        'ml_gn': 1.0 + nrm(ks[15], (DEPTH, ML_W), 0.02),
        'mla_q_norm': 1.0 + nrm(ks[16], (DEPTH, Q_LORA), 0.02),
        'mla_kv_norm': 1.0 + nrm(ks[17], (DEPTH, KV_LORA), 0.02),
        'mla_w_uq': nrm(ks[18], (DEPTH, Q_LORA, MLA_H * (MLA_NOPE + MLA_ROPE)), Q_LORA ** -0.5),
        'mla_w_uk': nrm(ks[19], (DEPTH, KV_LORA, MLA_H * MLA_NOPE), KV_LORA ** -0.5),
        'mla_w_uv': nrm(ks[20], (DEPTH, KV_LORA, MLA_H * MLA_VD), KV_LORA ** -0.5),
        'w_out': nrm(ks[21], (DEPTH, MIX_W, Dm), BETA * MIX_W ** -0.5),
        'ln_g': 1.0 + nrm(ks[22], (DEPTH, Dm), 0.02),
        'ln_b': nrm(ks[23], (DEPTH, Dm), 0.02),
    }


def reference(x_prompt, x_sample, c_prompt, c_sample, cache_ckv, cache_krope, state_C, state_n, state_m,
              page_table, w_ada, b_ada, w_in, ml_b_i, ml_b_f, ml_gn, mla_q_norm, mla_kv_norm,
              mla_w_uq, mla_w_uk, mla_w_uv, w_out, ln_g, ln_b):
    f32 = jnp.float32
    B, S, _ = x_prompt.shape
    DB, DS, _ = x_sample.shape
    past = page_table.shape[1] * PAGE_SIZE
    pos_p = jnp.arange(S)
    pos_s = past + jnp.arange(DS)
    xp, xs = x_prompt, x_sample
    ckv_p_l, kr_p_l, Cp_l, np_l, mp_l = [], [], [], [], []
    ckv_s_l, kr_s_l, Cs_l, ns_l, ms_l = [], [], [], [], []
    for l in range(DEPTH):
        fw = (w_ada[l], b_ada[l], w_in[l], ml_b_i[l], ml_b_f[l], mla_q_norm[l], mla_kv_norm[l], mla_w_uq[l], mla_w_uk[l])
        bw = (ml_gn[l], mla_w_uv[l], w_out[l], ln_g[l], ln_b[l])
        gate, mlin, mo, mz, q_lat, q_rope, ckv, kr, az = _front(xp, c_prompt, pos_p, *fw)
        C0 = jnp.zeros((B, ML_H, ML_DK, ML_DV), f32)
        n0 = jnp.zeros((B, ML_H, ML_DK), f32)
        m0 = jnp.full((B, ML_H), -jnp.inf, f32)
        h_ml, Cp, n_p, mp = _mlstm(mlin, C0, n0, m0, min(ML_CHUNK, S))
        o_lat = _mla_prompt(q_lat, q_rope, ckv, kr)
        yp = _back(xp, gate, h_ml, mo, mz, o_lat, az, *bw)
        ckv_p_l.append(ckv)
        kr_p_l.append(kr)
        Cp_l.append(Cp.astype(state_C.dtype))
        np_l.append(n_p.astype(state_n.dtype))
        mp_l.append(mp.astype(state_m.dtype))
        gate, mlin, mo, mz, q_lat, q_rope, ckv, kr, az = _front(xs, c_sample, pos_s, *fw)
        h_ml, Cs, n_s, ms = _mlstm(mlin, state_C[l].astype(f32), state_n[l].astype(f32), state_m[l].astype(f32), DS)
        o_lat = _mla_sample(q_lat, q_rope, ckv, kr, cache_ckv[l], cache_krope[l], page_table)
        ys = _back(xs, gate, h_ml, mo, mz, o_lat, az, *bw)
        ckv_s_l.append(ckv.astype(cache_ckv.dtype))
        kr_s_l.append(kr.astype(cache_krope.dtype))
        Cs_l.append(Cs.astype(state_C.dtype))
        ns_l.append(n_s.astype(state_n.dtype))
        ms_l.append(ms.astype(state_m.dtype))
        xp, xs = yp, ys
    ckv_prompt = jnp.stack(ckv_p_l)
    krope_prompt = jnp.stack(kr_p_l)
    C_prompt = jnp.stack(Cp_l)
    n_prompt = jnp.stack(np_l)
    m_prompt = jnp.stack(mp_l)
    ckv_sample = jnp.stack(ckv_s_l)
    krope_sample = jnp.stack(kr_s_l)
    C_sample = jnp.stack(Cs_l)
    n_sample = jnp.stack(ns_l)
    m_sample = jnp.stack(ms_l)
    return (xp, xs, ckv_prompt, krope_prompt, C_prompt, n_prompt, m_prompt,
            ckv_sample, krope_sample, C_sample, n_sample, m_sample)
```

```python
import contextlib
import numpy as np
import concourse.bass as bass
import concourse.mybir as mybir
from concourse.bass_utils import run_bass_kernel_spmd

F32 = mybir.dt.float32
BF16 = mybir.dt.bfloat16
I32 = mybir.dt.int32
AF = mybir.ActivationFunctionType
ALU = mybir.AluOpType
AX = mybir.AxisListType

ENGS = ("pe", "act", "dve", "pool", "sp")

D = 1024
KC = 8
SEQ = 8192
NBLK = 64
NOWN = 16
DK = 128
EPS = 1e-6
ALPHA = 2.0 ** 0.25
MLA_SCALE = 96.0 ** -0.5
NPOOL = 10240
NEG = -1.0e30


class Sched:
    def __init__(self, nc, stack):
        self.nc = nc
        self.stack = stack
        self.sems = {}
        self.count = {}
        self.ops = {e: [] for e in ENGS}
        self.waited = {e: {} for e in ENGS}
        self.writer = {}
        self.readers = {}
        self.tagmap = {}
        self.maxtags = 0

    def _sem(self, key):
        if key not in self.sems:
            self.sems[key] = self.stack.enter_context(self.nc.semaphore("s_" + key.replace(":", "_")))
            self.count[key] = 0
        return self.sems[key]

    def _deps(self, reads, writes):
        deps = {}

        def add(tok):
            if tok is None:
                return
            k, v = tok
            if deps.get(k, 0) < v:
                deps[k] = v

        for r in reads:
            add(self.writer.get(r))
        for w in writes:
            add(self.writer.get(w))
            for t in self.readers.get(w, ()):
                add(t)
        return deps

    def _commit(self, tok, reads, writes):
        for r in reads:
            self.readers.setdefault(r, []).append(tok)
        for w in writes:
            self.writer[w] = tok
            self.readers[w] = []

    def _waits(self, q, deps, n=None):
        waits = []
        for k, v in deps.items():
            if n is not None and k == q:
                if q == "pe" or v < n - 2:
                    continue
            if self.waited[q].get(k, 0) >= v:
                continue
            self.waited[q][k] = v
            waits.append((k, v))
        return waits

    def op(self, eng, fn, reads=(), writes=()):
        self._sem(eng)
        deps = self._deps(reads, writes)
        n = self.count[eng] + 1
        waits = self._waits(eng, deps, n)
        self.count[eng] = n
        self.ops[eng].append((waits, fn, (eng, 1)))
        self._commit((eng, n), reads, writes)

    def raw(self, q, tag, fn, reads=(), writes=()):
        if tag not in self.tagmap:
            self.tagmap[tag] = "t%d" % len(self.tagmap)
            self.maxtags = max(self.maxtags, len(self.tagmap))
        key = "dma:" + self.tagmap[tag]
        self._sem(key)
        deps = self._deps(reads, writes)
        waits = self._waits(q, deps)
        self.count[key] += 16
        tok = (key, self.count[key])
        self.ops[q].append((waits, fn, (key, 16)))
        self._commit(tok, reads, writes)

    def dma(self, q, tag, out, in_, reads=(), writes=(), **kw):
        self.raw(q, tag, lambda e, out=out, in_=in_, kw=kw: e.dma_start(out=out, in_=in_, **kw), reads, writes)

    def barrier(self):
        for e in ENGS:
            waits = []
            for k, v in self.count.items():
                if v == 0 or k == e:
                    continue
                if self.waited[e].get(k, 0) >= v:
                    continue
                self.waited[e][k] = v
                waits.append((k, v))
            if waits:
                self.ops[e].append((waits, None, None))

    def emit(self):
        self.barrier()
        nc = self.nc
        ops = self.ops
        sems = self.sems
        self.ops = {e: [] for e in ENGS}
        self.tagmap = {}

        def run(e, lst):
            for waits, fn, inc in lst:
                for k, v in waits:
                    e.wait_ge(sems[k], v)
                if fn is not None:
                    fn(e).then_inc(sems[inc[0]], inc[1])

        with nc.Block() as block:
            @block.tensor
            def _(e):
                run(e, ops["pe"])

            @block.scalar
            def _(e):
                run(e, ops["act"])

            @block.vector
            def _(e):
                run(e, ops["dve"])

            @block.gpsimd
            def _(e):
                run(e, ops["pool"])

            @block.sync
            def _(e):
                run(e, ops["sp"])


IN_SPECS = {
    "x_all": ([SEQ, D], F32), "x_own": ([NOWN * 128, D], F32), "x_smp": ([128, D], F32),
    "cT": ([D, 17], F32), "w_ada": ([D, 3 * D], F32), "b_adaT": ([128, 24], F32), "b_gate": ([D], F32),
    "w_all": ([D, 1352], F32), "w_own": ([D, 2944], F32),
    "rope_all": ([SEQ, 64], F32), "rope_own": ([NOWN * 128, 64], F32), "rope_smp": ([128, 64], F32),
    "ropeT_own": ([128, 2, NOWN * 128], F32), "ropeT_smp": ([128, 2, 128], F32),
    "ident": ([128, 128], F32), "tstrict": ([64, 64], F32), "segmask": ([128, 512], F32),
    "sel": ([128, 4], F32), "maskA": ([128, 4, 128], F32), "caus": ([128, 128], F32),
    "caus_s": ([128, 128], F32), "bmask": ([128, 16, 128], F32), "rmask": ([128, 16], F32),
    "nmask": ([128, 16, 64], F32), "i4": ([4, 4], F32), "seg8": ([4, 128], F32),
    "b_if_bc": ([64, 8], F32), "b_if_col": ([4, 2], F32), "gkv": ([256], F32), "gqT": ([128, 3], F32),
    "ml_gn": ([512], F32), "ln_g": ([D], F32), "ln_b": ([D], F32),
    "w_uq": ([384, 8, 96], F32), "w_uqp": ([384, 8, 96], F32), "w_ukp": ([256, 8, 96], F32),
    "w_ukT": ([64, 8, 256], F32), "w_uv": ([256, 512], F32), "w_out": ([D, D], F32),
    "state_C": ([16, 4, 128, 128], F32), "state_nT": ([128, 16, 4], F32), "state_mT": ([4, 16], F32),
    "ptab": ([128, 16], I32), "cache_ckv": ([NPOOL, 128, 256], F32), "cache_kr": ([NPOOL, 128, 32], F32),
}
OUT_SPECS = {
    "o_y": ([NOWN * 128, D], F32), "o_ckv": ([NOWN * 128, 256], F32), "o_kr": ([NOWN * 128, 32], F32),
    "o_C": ([128, 4, 129], F32), "o_m": ([1, 4], F32),
    "o_ys": ([128, D], F32), "o_ckvs": ([128, 256], F32), "o_krs": ([128, 32], F32),
    "o_Cs": ([16, 128, 4, 128], F32), "o_ns": ([128, 16, 4], F32), "o_ms": ([4, 16], F32),
}

STAGE = 99


def build(stage=STAGE):
    nc = bass.Bass("TRN2", target_bir_lowering=False)
    dr = {}
    used_in = set()
    for k, (shp, dt) in IN_SPECS.items():
        dr[k] = (k, shp, dt)
    dram_cache = {}

    def DIN(name):
        if name not in dram_cache:
            _, shp, dt = dr[name]
            dram_cache[name] = nc.dram_tensor(name, shp, dt, kind="ExternalInput").ap()
            used_in.add(name)
        return dram_cache[name]

    outs = {k: nc.dram_tensor(k, shp, dt, kind="ExternalOutput").ap() for k, (shp, dt) in OUT_SPECS.items()}
    kscr = nc.dram_tensor("kscr", [NBLK, 128, 512], BF16, kind="Internal").ap()
    vscr = nc.dram_tensor("vscr", [NBLK, 128, 4, 129], BF16, kind="Internal").ap()

    with contextlib.ExitStack() as st:
        S = Sched(nc, st)
        op, dma = S.op, S.dma

        def T(stack, name, shape, dt):
            return stack.enter_context(nc.sbuf_tensor("sb_" + name, shape, dt))

        PS = [st.enter_context(nc.psum_tensor(f"ps{i}", [128, 512], F32)) for i in range(8)]

        def psf(i):
            return PS[i][:]

        def psb(i):
            return PS[i][:].bitcast(BF16)

        def bank(i):
            return f"ps{i}"

        def mm(out, lhsT, rhs, start, stop, reads, writes):
            op("pe", lambda e: e.matmul(out, lhsT, rhs, start=start, stop=stop, skip_group_check=True), reads, writes)

        def tr(out, in_, idn, reads, writes):
            op("pe", lambda e: e.transpose(out, in_, idn), reads, writes)

        def act(out, in_, func, reads, writes, scale=1.0, bias=0.0):
            op("act", lambda e: e.activation(out, in_, func, bias=bias, scale=scale), reads, writes)

        def tt(eng, out, in0, in1, alu, reads, writes):
            op(eng, lambda e: e.tensor_tensor(out, in0, in1, alu), reads, writes)

        def ts(eng, out, in0, s1, s2, op0, op1, reads, writes):
            if s2 is None:
                op(eng, lambda e: e.tensor_scalar(out, in0, s1, None, op0), reads, writes)
            else:
                op(eng, lambda e: e.tensor_scalar(out, in0, s1, s2, op0, op1), reads, writes)

        def stt(out, in0, sc, in1, op0, op1, reads, writes):
            op("dve", lambda e: e.scalar_tensor_tensor(out, in0, sc, in1, op0, op1), reads, writes)

        def cp(eng, out, in_, reads, writes):
            if eng == "act":
                op("act", lambda e: e.activation(out, in_, AF.Identity), reads, writes)
            else:
                op(eng, lambda e: e.tensor_copy(out, in_), reads, writes)

        def red(out, in_, alu, reads, writes, axis=AX.X):
            op("dve", lambda e: e.tensor_reduce(out, in_, axis, alu), reads, writes)

        def rsqrt_pool(out, in_, mhalf, reads, writes):
            op("act", lambda e: e.activation(out, in_, AF.Ln, bias=0.0, scale=1.0), reads, writes)
            op("act", lambda e: e.activation(out, out, AF.Exp, bias=0.0, scale=-0.5), list(writes), writes)

        def cast_w(stack, name, src, kc, n, tagq="pool"):
            t = T(stack, name, [128, kc, n], BF16)
            v = src.rearrange("(k p) n -> p k n", p=128)
            c0 = 0
            while c0 < n:
                c1 = min(n, c0 + 2048)
                dma("pool", name, t[:, :, c0:c1], v[:, :, c0:c1], writes=[name])
                c0 = c1
            return t

        ident = T(st, "ident", [128, 128], BF16)
        identf = T(st, "identf", [128, 128], F32)
        onesb = T(st, "onesb", [128, 128], BF16)
        onesf = T(st, "onesf", [128, 128], F32)
        mhalf = T(st, "mhalf", [128, 8], F32)
        mod = T(st, "mod", [128, 16, 17], F32)
        gate_p = T(st, "gate_p", [128, D], F32)
        gate_s = T(st, "gate_s", [128, D], F32)
        gkv_bc = T(st, "gkv_bc", [128, 256], F32)
        sel = T(st, "sel", [128, 4], F32)

        dma("pool", "c_ident", ident[:], DIN("ident"), writes=["ident"])
        dma("sp", "c_identf", identf[:], DIN("ident"), writes=["identf"])
        op("pool", lambda e: e.memset(onesb[:], 1.0), writes=["onesb"])
        op("pool", lambda e: e.memset(onesf[:], 1.0), writes=["onesf"])
        op("pool", lambda e: e.memset(mhalf[:], -0.5), writes=["mhalf"])
        dma("sp", "c_gkv", gkv_bc[:], DIN("gkv").partition_broadcast(128), writes=["gkv_bc"])
        dma("sp", "c_sel", sel[:], DIN("sel"), writes=["sel"])

        with contextlib.ExitStack() as ph:
            cT_bf = T(ph, "cT_bf", [128, KC, 17], BF16)
            dma("pool", "cT", cT_bf[:], DIN("cT").rearrange("(k p) n -> p k n", p=128), writes=["cT_bf"])
            badaT = T(ph, "badaT", [128, 24], F32)
            dma("sp", "badaT", badaT[:], DIN("b_adaT"), writes=["badaT"])
            bgate = T(ph, "bgate", [128, D], F32)
            dma("sp", "bgate", bgate[:], DIN("b_gate").partition_broadcast(128), writes=["bgate"])
            crep_p = T(ph, "crep_p", [128, KC, 128], BF16)
            crep_s = T(ph, "crep_s", [128, KC, 128], BF16)
            cp("dve", crep_p[:], cT_bf[:, :, 0:1].to_broadcast([128, KC, 128]), ["cT_bf"], ["crep_p"])
            cp("dve", crep_s[:].rearrange("p k (b s) -> p k b s", s=8),
               cT_bf[:, :, 1:17].unsqueeze(3).to_broadcast([128, KC, 16, 8]), ["cT_bf"], ["crep_s"])
            wa = [T(ph, f"wa{i}", [128, KC, 1024], BF16) for i in range(2)]
            wav = DIN("w_ada").rearrange("(k p) n -> p k n", p=128)
            for piece in range(3):
                w = wa[piece % 2]
                wn = f"wa{piece % 2}"
                dma("pool", wn, w[:], wav[:, :, piece * 1024:(piece + 1) * 1024], writes=[wn])
                if piece < 2:
                    for nch in range(8):
                        col = (piece * 8 + nch) * 17
                        for kc in range(KC):
                            mm(psf(0)[:, col:col + 17], w[:, kc, nch * 128:(nch + 1) * 128], cT_bf[:, kc, :],
                               kc == 0, kc == KC - 1, [wn, "cT_bf"], [bank(0)])
                else:
                    for gi, (crep, gdst, gname) in enumerate(((crep_p, gate_p, "gate_p"), (crep_s, gate_s, "gate_s"))):
                        for half in range(2):
                            b = 1 + gi * 2 + half
                            for kc in range(KC):
                                mm(psf(b), crep[:, kc, :], w[:, kc, half * 512:(half + 1) * 512],
                                   kc == 0, kc == KC - 1, [wn, "crep_p", "crep_s"], [bank(b)])
                            tt("dve", gdst[:, half * 512:(half + 1) * 512], psf(b), bgate[:, half * 512:(half + 1) * 512],
                               ALU.add, [bank(b), "bgate"], [gname])
            tt("dve", mod[:], psf(0)[:, 0:272].rearrange("p (c n) -> p c n", n=17),
               badaT[:, 0:16].unsqueeze(2).to_broadcast([128, 16, 17]), ALU.add, [bank(0), "badaT"], ["mod"])
            ts("dve", mod[:, 8:16, :], mod[:, 8:16, :], 1.0, None, ALU.add, None, ["mod"], ["mod"])
            S.emit()

        def make_hT(xbt, xbn, hTt, hTn, pbank, sample):
            for kc in range(KC):
                tr(psb(pbank)[:, kc * 128:(kc + 1) * 128], xbt[:, kc * 128:(kc + 1) * 128], ident[:],
                   [xbn, "ident"], [bank(pbank)])
            if not sample:
                tt("dve", hTt[:], psb(pbank).rearrange("p (k t) -> p k t", t=128),
                   mod[:, 8:16, 0:1].to_broadcast([128, KC, 128]), ALU.mult, [bank(pbank), "mod"], [hTn])
                tt("dve", hTt[:], hTt[:], mod[:, 0:8, 0:1].to_broadcast([128, KC, 128]), ALU.add,
                   [hTn, "mod"], [hTn])
            else:
                v4 = lambda a: a.rearrange("p k (b s) -> p k b s", s=8)
                tt("dve", v4(hTt[:]), v4(psb(pbank).rearrange("p (k t) -> p k t", t=128)),
                   mod[:, 8:16, 1:17].unsqueeze(3).to_broadcast([128, KC, 16, 8]), ALU.mult,
                   [bank(pbank), "mod"], [hTn])
                tt("pool", v4(hTt[:]), v4(hTt[:]),
                   mod[:, 0:8, 1:17].unsqueeze(3).to_broadcast([128, KC, 16, 8]), ALU.add,
                   [hTn, "mod"], [hTn])

        def tokgroup(pb, hTt, hTn, W, Wn, c0, n):
            for kc in range(KC):
                mm(psf(pb)[:, 0:n], hTt[:, kc, :], W[:, kc, c0:c0 + n], kc == 0, kc == KC - 1,
                   [hTn, Wn], [bank(pb)])

        def featgroup(pb, col, hTt, hTn, W, Wn, c0, m):
            for kc in range(KC):
                mm(psf(pb)[0:m, col:col + 128], W[:, kc, c0:c0 + m], hTt[:, kc, :], kc == 0, kc == KC - 1,
                   [hTn, Wn], [bank(pb)])

        def misc_post(pb, ropet, ropen, ckvn, ckvnn, krr, krrn, tmp):
            sq, ss, rstd, t1, t2 = tmp
            act(sq[:, 0:256], psf(pb)[:, 0:256], AF.Square, [bank(pb)], ["m_sq"])
            red(ss[:, 0:1], sq[:, 0:256], ALU.add, ["m_sq"], ["m_ss"])
            ts("dve", ss[:, 1:2], ss[:, 0:1], 1.0 / 256.0, EPS, ALU.mult, ALU.add, ["m_ss"], ["m_ss2"])
            rsqrt_pool(rstd[:, 0:1], ss[:, 1:2], mhalf[:, 0:1], ["m_ss2", "mhalf"], ["m_rstd"])
            stt(ckvn[:], psf(pb)[:, 0:256], rstd[:, 0:1], gkv_bc[:], ALU.mult, ALU.mult,
                [bank(pb), "m_rstd", "gkv_bc"], [ckvnn])
            tt("dve", t1[:], psf(pb)[:, 256:288], ropet[:, 0:32], ALU.mult, [bank(pb), ropen], ["m_t1"])
            tt("dve", t2[:], psf(pb)[:, 288:320], ropet[:, 32:64], ALU.mult, [bank(pb), ropen], ["m_t2"])
            tt("pool", krr[:], t1[:], t2[:], ALU.add, ["m_t1", "m_t2"], [krrn])

        def misc_tmp(stack, pfx):
            return (T(stack, pfx + "sq", [128, 256], F32), T(stack, pfx + "ss", [128, 2], F32),
                    T(stack, pfx + "rstd", [128, 1], F32), T(stack, pfx + "t1", [128, 32], F32),
                    T(stack, pfx + "t2", [128, 32], F32))

        def mlstm_intra(rn, tp, qT, kT, vaug, wlc, ec, maskT, x2_list, tmp):
            st_, xs, dn, hm = tmp
            for h in range(4):
                mm(psf(3)[:, h * 128:(h + 1) * 128], kT[:, h, :], qT[:, h, :], True, True,
                   [rn["kT"], rn["qT"]], [bank(3)])
            for h in range(4):
                stt(st_[:, h, :], psf(3)[:, h * 128:(h + 1) * 128], wlc[:, h:h + 1], maskT, ALU.mult, ALU.mult,
                    [bank(3), rn["wl"], rn["mask"]], [tp + "st"])
            for h in range(4):
                col = slice((h % 2) * 129, (h % 2) * 129 + 129)
                lst = x2_list(h)
                mm(psf(4 + h // 2)[:, col], st_[:, h, :], vaug[:, h, :], True, False,
                   [tp + "st", rn["vaug"]], [bank(4 + h // 2)])
                for n, (lh, rh, rd) in enumerate(lst):
                    mm(psf(4 + h // 2)[:, col], lh, rh, False, n == len(lst) - 1, rd, [bank(4 + h // 2)])
            for hb in range(2):
                cp("act", xs[:, 2 * hb:2 * hb + 2, :], psf(4 + hb)[:, 0:258].rearrange("p (h v) -> p h v", v=129),
                   [bank(4 + hb)], [tp + "xs"])
            act(dn[:, 0:4], xs[:, :, 128], AF.Abs, [tp + "xs"], [tp + "dn"])
            tt("dve", dn[:, 0:4], dn[:, 0:4], ec, ALU.max, [tp + "dn", rn["e"]], [tp + "dn"])
            op("dve", lambda e: e.reciprocal(dn[:, 4:8], dn[:, 0:4]), [tp + "dn"], [tp + "dn2"])
            tt("dve", hm[:], xs[:, :, 0:128], dn[:, 4:8].unsqueeze(2).to_broadcast([128, 4, 128]), ALU.mult,
               [tp + "xs", tp + "dn2"], [tp + "hm"])
            return hm

        def intra_tmp(stack, pfx):
            return (T(stack, pfx + "st", [128, 4, 128], BF16), T(stack, pfx + "xs", [128, 4, 129], F32),
                    T(stack, pfx + "dn", [128, 8], F32), T(stack, pfx + "hm", [128, 4, 128], F32))

        def ml_post(rn, tp, hm, sig_mo, silu_mz, gn_bc, ydst, yname, tmp):
            sqh, st4 = tmp
            v3 = lambda a: a.rearrange("p (h v) -> p h v", v=128)
            tt("dve", hm[:], hm[:], v3(sig_mo), ALU.mult, [tp + "hm", rn["sig_mo"]], [tp + "hm"])
            red(st4[:, 0:4], hm[:], ALU.add, [tp + "hm"], [tp + "g_s1"])
            act(sqh[:], hm[:], AF.Square, [tp + "hm"], [tp + "sqh"])
            red(st4[:, 4:8], sqh[:], ALU.add, [tp + "sqh"], [tp + "g_s2"])
            ts("dve", st4[:, 8:12], st4[:, 0:4], 1.0 / 128.0, None, ALU.mult, None, [tp + "g_s1"], [tp + "g_mean"])
            tt("dve", st4[:, 12:16], st4[:, 8:12], st4[:, 8:12], ALU.mult, [tp + "g_mean"], [tp + "g_msq"])
            stt(st4[:, 16:20], st4[:, 4:8], 1.0 / 128.0, st4[:, 12:16], ALU.mult, ALU.subtract,
                [tp + "g_s2", tp + "g_msq"], [tp + "g_var"])
            ts("dve", st4[:, 16:20], st4[:, 16:20], EPS, None, ALU.add, None, [tp + "g_var"], [tp + "g_var"])
            rsqrt_pool(st4[:, 20:24], st4[:, 16:20], None, [tp + "g_var"], [tp + "g_rstd"])
            tt("dve", hm[:], hm[:], st4[:, 8:12].unsqueeze(2).to_broadcast([128, 4, 128]), ALU.subtract,
               [tp + "hm", tp + "g_mean"], [tp + "hm"])
            tt("dve", hm[:], hm[:], st4[:, 20:24].unsqueeze(2).to_broadcast([128, 4, 128]), ALU.mult,
               [tp + "hm", tp + "g_rstd"], [tp + "hm"])
            tt("dve", hm[:], hm[:], v3(gn_bc), ALU.mult, [tp + "hm", "gn_bc"], [tp + "hm"])
            tt("dve", v3(ydst), hm[:], v3(silu_mz), ALU.mult, [tp + "hm", rn["silu_mz"]], [yname])

        def ln_steps(ipfx, pfx, pbs, xf, gate, lng, lnb, odst, oname, tmp):
            z, sqz, st1 = tmp
            for half in range(2):
                hs = slice(half * 512, (half + 1) * 512)
                tt("dve", z[:, hs], psf(pbs[half]), gate[:, hs], ALU.mult, [bank(pbs[half]), "gate"], [pfx + "z"])
            yield
            stt(z[:], xf, ALPHA, z[:], ALU.mult, ALU.add, [ipfx + "xf", pfx + "z"], [pfx + "z"])
            red(st1[:, 0:1], z[:], ALU.add, [pfx + "z"], [pfx + "l_s1"])
            act(sqz[:], z[:], AF.Square, [pfx + "z"], [pfx + "sqz"])
            red(st1[:, 1:2], sqz[:], ALU.add, [pfx + "sqz"], [pfx + "l_s2"])
            ts("dve", st1[:, 2:3], st1[:, 0:1], 1.0 / D, None, ALU.mult, None, [pfx + "l_s1"], [pfx + "l_mean"])
            tt("dve", st1[:, 3:4], st1[:, 2:3], st1[:, 2:3], ALU.mult, [pfx + "l_mean"], [pfx + "l_msq"])
            stt(st1[:, 4:5], st1[:, 1:2], 1.0 / D, st1[:, 3:4], ALU.mult, ALU.subtract,
                [pfx + "l_s2", pfx + "l_msq"], [pfx + "l_var"])
            ts("dve", st1[:, 4:5], st1[:, 4:5], EPS, None, ALU.add, None, [pfx + "l_var"], [pfx + "l_var"])
            rsqrt_pool(st1[:, 5:6], st1[:, 4:5], None, [pfx + "l_var"], [pfx + "l_rstd"])
            stt(z[:], z[:], st1[:, 2:3], lng[:], ALU.subtract, ALU.mult, [pfx + "z", pfx + "l_mean", "lng"], [pfx + "z"])
            stt(z[:], z[:], st1[:, 5:6], lnb[:], ALU.mult, ALU.add, [pfx + "z", pfx + "l_rstd", "lnb"], [pfx + "z"])
            dma("sp", pfx + "yout", odst, z[:], reads=[pfx + "z"], writes=[oname])
            yield

        def ln_out(*args):
            for _ in ln_steps(*args):
                pass

        if stage >= 3:
          sst = contextlib.ExitStack()
          ys_ml = T(sst, "ys_ml", [128, 512], BF16)
          silu_azT = T(sst, "silu_azT", [128, 8, 128], BF16)
          qlatT = T(sst, "qlatT", [128, 2, 8, 128], BF16)
          qropeT = T(sst, "qropeT", [128, 8, 128], BF16)
          ckvnb_s = T(sst, "ckvnb_s", [128, 256], BF16)
          ckvnT_s = T(sst, "ckvnT_s", [128, 2, 128], BF16)
          krT_s = T(sst, "krT_s", [128, 128], BF16)
          xf_s = T(sst, "xf_s", [128, D], F32)
          qr4 = T(sst, "qr4", [128, 4, 8, 128], BF16)
          op("pool", lambda e: e.memset(qr4[:], 0.0), writes=["qr4"])
          op("pool", lambda e: e.memset(krT_s[:], 0.0), writes=["krT_s"])
          qT = T(sst, "s_qT", [128, 4, 128], BF16)
          kT = T(sst, "s_kT", [128, 4, 128], BF16)
          ktok = T(sst, "s_ktok", [128, 4, 128], BF16)
          vaug = T(sst, "s_vaug", [128, 4, 129], BF16)
          gs = T(sst, "s_gs", [128, 8], F32)
          sig_mo = T(sst, "s_sig_mo", [128, 512], BF16)
          silu_mz = T(sst, "s_silu_mz", [128, 512], BF16)
          with contextlib.ExitStack() as ph:
            WOWN = DIN("w_own")
            Wa = cast_w(ph, "sWa", DIN("w_all"), KC, 1352)
            Wq5 = cast_w(ph, "sWq5", WOWN[:, 0:1024], KC, 1024)
            Wcq = cast_w(ph, "sWcq", WOWN[:, 1024:1408], KC, 384)
            Wg5 = cast_w(ph, "sWg5", WOWN[:, 1408:2944], KC, 1536)
            gqT = T(ph, "s_gqT", [128, 3], F32)
            dma("sp", "s_gqT", gqT[:], DIN("gqT"), writes=["s_gqT"])
            wq_st = T(ph, "s_wq_st", [128, 3, 768], F32)
            wuq = T(ph, "s_wuq", [128, 3, 768], BF16)
            wuqp = T(ph, "s_wuqp", [128, 3, 768], BF16)
            for nm, dst, dn in (("w_uq", wuq, "s_wuq"), ("w_uqp", wuqp, "s_wuqp")):
                dma("sp", "s_wq_st", wq_st[:], DIN(nm).rearrange("(c p) h n -> p c (h n)", p=128), writes=["s_wq_st"])
                for cc in range(3):
                    ts("dve", dst[:, cc, :], wq_st[:, cc, :], gqT[:, cc:cc + 1], None, ALU.mult, None,
                       ["s_wq_st", "s_gqT"], [dn])
            wukT = T(ph, "s_wukT", [64, 8, 256], BF16)
            dma("pool", "s_wukT", wukT[:], DIN("w_ukT"), writes=["s_wukT"])
            mhq = T(ph, "s_mhq", [128, 128], F32)
            op("pool", lambda e: e.memset(mhq[:], -0.5), writes=["s_mhq"])
            xb = T(ph, "s_xb", [128, D], BF16)
            hT = T(ph, "s_hT", [128, KC, 128], BF16)
            rope = T(ph, "s_rope", [128, 64], F32)
            ropeT = T(ph, "s_ropeT", [128, 2, 128], F32)
            ckvn = T(ph, "s_ckvn", [128, 256], F32)
            krr = T(ph, "s_krr", [128, 32], F32)
            krb = T(ph, "s_krb", [128, 32], BF16)
            sqc = T(ph, "s_sqc", [128, 3, 128], BF16)
            rq = T(ph, "s_rq", [128, 128], F32)
            rq2 = T(ph, "s_rq2", [128, 128], F32)
            cqn = T(ph, "s_cqn", [128, 3, 128], BF16)
            qnT = T(ph, "s_qnT", [64, 8, 128], BF16)
            tq1 = T(ph, "s_tq1", [32, 8, 128], F32)
            tq2 = T(ph, "s_tq2", [32, 8, 128], F32)
            mt = misc_tmp(ph, "sp")
            op("pool", lambda e: e.memset(vaug[:], 1.0), writes=["s_vaug"])
            dma("pool", "s_xb", xb[:], DIN("x_smp"), writes=["s_xb"])
            dma("sp", "s_xf", xf_s[:], DIN("x_smp"), writes=["s_xf"])
            dma("sp", "s_rope", rope[:], DIN("rope_smp"), writes=["s_rope"])
            dma("sp", "s_ropeT", ropeT[:], DIN("ropeT_smp"), writes=["s_ropeT"])
            make_hT(xb, "s_xb", hT, "s_hT", 0, True)
            v3 = lambda a: a.rearrange("p (h t) -> p h t", t=128)
            for h in range(4):
                featgroup(2, h * 128, hT, "s_hT", Wq5, "sWq5", h * 128, 128)
            for h in range(4):
                featgroup(3, h * 128, hT, "s_hT", Wq5, "sWq5", 512 + h * 128, 128)
            cp("act", qT[:].rearrange("p h t -> p (h t)"), psf(2), [bank(2)], ["s_qT"])
            act(kT[:].rearrange("p h t -> p (h t)"), psf(3), AF.Identity, [bank(3)], ["s_kT"], scale=DK ** -0.5)
            tokgroup(4, hT, "s_hT", Wa, "sWa", 0, 512)
            act(ktok[:].rearrange("p h t -> p (h t)"), psf(4), AF.Identity, [bank(4)], ["s_ktok"], scale=DK ** -0.5)
            tokgroup(5, hT, "s_hT", Wa, "sWa", 512, 512)
            cp("dve", vaug[:, :, 0:128], psf(5).rearrange("p (h v) -> p h v", v=128), [bank(5)], ["s_vaug"])
            tokgroup(6, hT, "s_hT", Wa, "sWa", 1024, 328)
            misc_post(6, rope, "s_rope", ckvn, "s_ckvn", krr, "s_krr", mt)
            cp("act", gs[:], psf(6)[:, 320:328], [bank(6)], ["s_gs"])
            dma("sp", "o_ckvs", outs["o_ckvs"], ckvn[:], reads=["s_ckvn"], writes=["out_ckvs"])
            dma("sp", "o_krs", outs["o_krs"], krr[:], reads=["s_krr"], writes=["out_krs"])
            cp("pool", ckvnb_s[:], ckvn[:], ["s_ckvn"], ["ckvnb_s"])
            cp("pool", krb[:], krr[:], ["s_krr"], ["s_krb"])
            for cc in range(2):
                tr(psb(7)[:, cc * 128:(cc + 1) * 128], ckvnb_s[:, cc * 128:(cc + 1) * 128], ident[:],
                   ["ckvnb_s", "ident"], [bank(7)])
            tr(psb(7)[0:32, 256:384], krb[:], ident[:], ["s_krb", "ident"], [bank(7)])
            cp("act", ckvnT_s[:].rearrange("p c t -> p (c t)"), psb(7)[:, 0:256], [bank(7)], ["ckvnT_s"])
            cp("act", krT_s[0:32, :], psb(7)[0:32, 256:384], [bank(7)], ["krT_s"])
            tokgroup(2, hT, "s_hT", Wg5, "sWg5", 0, 512)
            act(sig_mo[:], psf(2), AF.Sigmoid, [bank(2)], ["s_sig_mo"])
            tokgroup(3, hT, "s_hT", Wg5, "sWg5", 512, 512)
            act(silu_mz[:], psf(3), AF.Silu, [bank(3)], ["s_silu_mz"])
            for h in range(8):
                featgroup(4 + h // 4, (h % 4) * 128, hT, "s_hT", Wg5, "sWg5", 1024 + h * 64, 64)
            for hb in range(2):
                act(silu_azT[0:64, 4 * hb:4 * hb + 4, :], v3(psf(4 + hb)[0:64, :]), AF.Silu, [bank(4 + hb)], ["silu_azT"])
            for cc in range(3):
                featgroup(6, cc * 128, hT, "s_hT", Wcq, "sWcq", cc * 128, 128)
            act(sqc[:], psf(6)[:, 0:384].rearrange("p (c t) -> p c t", t=128), AF.Square, [bank(6)], ["s_sqc"])
            for cc in range(3):
                mm(psf(7)[:, 0:128], onesb[:], sqc[:, cc, :], cc == 0, cc == 2, ["onesb", "s_sqc"], [bank(7)])
            ts("dve", rq[:], psf(7)[:, 0:128], 1.0 / 384.0, EPS, ALU.mult, ALU.add, [bank(7)], ["s_rq"])
            rsqrt_pool(rq2[:], rq[:], mhq[:], ["s_rq", "s_mhq"], ["s_rq2"])
            tt("dve", cqn[:], psf(6)[:, 0:384].rearrange("p (c t) -> p c t", t=128),
               rq2[:].unsqueeze(1).to_broadcast([128, 3, 128]), ALU.mult, [bank(6), "s_rq2"], ["s_cqn"])
            for h in range(8):
                col = slice((h % 4) * 128, (h % 4 + 1) * 128)
                for cc in range(3):
                    mm(psf(2 + h // 4)[0:64, col], wuq[:, cc, h * 96:h * 96 + 64], cqn[:, cc, :], cc == 0, cc == 2,
                       ["s_wuq", "s_cqn"], [bank(2 + h // 4)])
                for cc in range(3):
                    mm(psf(4 + h // 4)[0:32, col], wuq[:, cc, h * 96 + 64:h * 96 + 96], cqn[:, cc, :], cc == 0, cc == 2,
                       ["s_wuq", "s_cqn"], [bank(4 + h // 4)])
                for cc in range(3):
                    mm(psf(6 + h // 4)[0:32, col], wuqp[:, cc, h * 96 + 64:h * 96 + 96], cqn[:, cc, :], cc == 0, cc == 2,
                       ["s_wuqp", "s_cqn"], [bank(6 + h // 4)])
            for hb in range(2):
                hs = slice(4 * hb, 4 * hb + 4)
                cp("act", qnT[:, hs, :], v3(psf(2 + hb)[0:64, :]), [bank(2 + hb)], ["s_qnT"])
                tt("dve", tq1[:, hs, :], v3(psf(4 + hb)[0:32, :]), ropeT[0:32, 0:1, :].to_broadcast([32, 4, 128]), ALU.mult,
                   [bank(4 + hb), "s_ropeT"], ["s_tq1"])
                tt("dve", tq2[:, hs, :], v3(psf(6 + hb)[0:32, :]), ropeT[0:32, 1:2, :].to_broadcast([32, 4, 128]), ALU.mult,
                   [bank(6 + hb), "s_ropeT"], ["s_tq2"])
            tt("pool", qropeT[0:32, :, :], tq1[:], tq2[:], ALU.add, ["s_tq1", "s_tq2"], ["qropeT"])
            for i4 in range(4):
                dma("sp", "qr4", qr4[i4 * 32:(i4 + 1) * 32, i4, :, :], qropeT[0:32, :, :], reads=["qropeT", "qr4"], writes=["qr4"])
            for cc in range(2):
                for h in range(8):
                    pb = 2 + cc * 2 + h // 4
                    mm(psf(pb)[:, (h % 4) * 128:(h % 4 + 1) * 128], wukT[:, h, cc * 128:(cc + 1) * 128], qnT[:, h, :],
                       True, True, ["s_wukT", "s_qnT"], [bank(pb)])
                for hb in range(2):
                    cp("act" if hb == 0 else "dve", qlatT[:, cc, 4 * hb:4 * hb + 4, :], v3(psf(2 + cc * 2 + hb)),
                       [bank(2 + cc * 2 + hb)], ["qlatT"])
            S.emit()
          with contextlib.ExitStack() as ph:
            gn_bc = T(ph, "s_gn_bc", [128, 512], F32)
            dma("sp", "s_gn_bc", gn_bc[:], DIN("ml_gn").partition_broadcast(128), writes=["gn_bc"])
            caus_s = T(ph, "s_caus", [128, 128], F32)
            dma("sp", "s_caus", caus_s[:], DIN("caus_s"), writes=["s_caus"])
            bmask = T(ph, "s_bmask", [128, 16, 128], BF16)
            dma("pool", "s_bmask", bmask[:], DIN("bmask"), writes=["s_bmask"])
            rmask = T(ph, "s_rmask", [128, 16], F32)
            dma("sp", "s_rmask", rmask[:], DIN("rmask"), writes=["s_rmask"])
            seg8 = T(ph, "s_seg8", [4, 128], F32)
            dma("sp", "s_seg8", seg8[:], DIN("seg8"), writes=["s_seg8"])
            bcol = T(ph, "s_bcol", [4, 2], F32)
            dma("sp", "s_bcol", bcol[:], DIN("b_if_col"), writes=["s_bcol"])
            i4t = T(ph, "s_i4", [4, 4], F32)
            dma("sp", "s_i4", i4t[:], DIN("i4"), writes=["s_i4"])
            m0T = T(ph, "s_m0T", [4, 16], F32)
            dma("sp", "s_m0T", m0T[:], DIN("state_mT"), writes=["s_m0T"])
            n0T = T(ph, "s_n0T", [128, 16, 4], F32)
            dma("sp", "s_n0T", n0T[:], DIN("state_nT"), writes=["s_n0T"])
            kts = T(ph, "s_kts", [128, 4, 128], BF16)
            it = intra_tmp(ph, "s_")
            gt = (T(ph, "s_sqh", [128, 4, 128], F32), T(ph, "s_st4", [128, 24], F32))
            gf = T(ph, "s_gf", [4, 12, 128], F32)
            g16 = T(ph, "s_g16", [4, 8, 16], F32)
            aldg = T(ph, "s_aldg", [4, 4, 16], F32)
            albc_s = T(ph, "s_albc", [128, 4, 16], F32)
            cols = T(ph, "s_cols", [128, 12], F32)
            tr(psf(0)[0:4, 0:128], gs[:, 0:4], identf[:], ["s_gs", "identf"], [bank(0)])
            tr(psf(0)[0:4, 128:256], gs[:, 4:8], identf[:], ["s_gs", "identf"], [bank(0)])
            ts("dve", gf[:, 0, :], psf(0)[0:4, 0:128], bcol[:, 0:1], None, ALU.add, None, [bank(0), "s_bcol"], ["s_ig"])
            ts("dve", gf[:, 1, :], psf(0)[0:4, 128:256], bcol[:, 1:2], None, ALU.add, None, [bank(0), "s_bcol"], ["s_lf"])
            act(gf[:, 1, :], gf[:, 1, :], AF.Exp, ["s_lf"], ["s_lf"], scale=-1.0)
            act(gf[:, 1, :], gf[:, 1, :], AF.Ln, ["s_lf"], ["s_lf"], bias=1.0)
            op("dve", lambda e: e.tensor_tensor_scan(gf[:, 2, :], seg8[:], gf[:, 1, :], 0.0, ALU.mult, ALU.add),
               ["s_seg8", "s_lf"], ["s_L"])
            tt("dve", gf[:, 3, :], gf[:, 0, :], gf[:, 2, :], ALU.add, ["s_ig", "s_L"], ["s_G"])
            b8 = lambda a: a.rearrange("p (b s) -> p b s", s=8)
            red(g16[:, 0, :], b8(gf[:, 3, :]), ALU.max, ["s_G"], ["s_gmax"])
            tt("dve", g16[:, 1, :], g16[:, 0, :], m0T[:], ALU.max, ["s_gmax", "s_m0T"], ["s_R"])
            bcR = g16[:, 1, :].unsqueeze(2).to_broadcast([4, 16, 8])
            tt("dve", b8(gf[:, 4, :]), b8(gf[:, 3, :]), bcR, ALU.subtract, ["s_G", "s_R"], ["s_wl"])
            act(gf[:, 4, :], gf[:, 4, :], AF.Exp, ["s_wl"], ["s_wl"])
            tt("dve", b8(gf[:, 5, :]), b8(gf[:, 2, :]), bcR, ALU.subtract, ["s_L", "s_R"], ["s_e"])
            act(gf[:, 5, :], gf[:, 5, :], AF.Exp, ["s_e"], ["s_e"])
            tt("dve", g16[:, 2, :], m0T[:], g16[:, 1, :], ALU.subtract, ["s_m0T", "s_R"], ["s_al"])
            act(g16[:, 2, :], g16[:, 2, :], AF.Exp, ["s_al"], ["s_al"])
            tt("dve", g16[:, 3, :], g16[:, 1, :], b8(gf[:, 2, :])[:, :, 7], ALU.subtract, ["s_R", "s_L"], ["s_mnew"])
            dma("sp", "o_ms", outs["o_ms"], g16[:, 3, :], reads=["s_mnew"], writes=["out_ms"])
            cp("dve", b8(gf[:, 6, :]), g16[:, 2, :].unsqueeze(2).to_broadcast([4, 16, 8]), ["s_al"], ["s_alx"])
            for n, (src, rn) in enumerate(((4, "s_wl"), (5, "s_e"), (6, "s_alx"))):
                tr(psf(1)[:, n * 4:(n + 1) * 4], gf[:, src, :], identf[0:4, 0:4], [rn, "identf"], [bank(1)])
            cp("dve", cols[:], psf(1)[:, 0:12], [bank(1)], ["s_cols", "s_wlc", "s_ec", "s_alc"])
            wlc_s, ec_s, alc_s = cols[:, 0:4], cols[:, 4:8], cols[:, 8:12]
            tt("dve", aldg[:], g16[:, 2, :].unsqueeze(1).to_broadcast([4, 4, 16]),
               i4t[:].unsqueeze(2).to_broadcast([4, 4, 16]), ALU.mult, ["s_al", "s_i4"], ["s_aldg"])
            mm(psf(0)[:, 256:320], onesf[0:4, :], aldg[:].rearrange("p h b -> p (h b)"), True, True,
               ["onesf", "s_aldg"], [bank(0)])
            cp("dve", albc_s[:], psf(0)[:, 256:320].rearrange("p (h b) -> p h b", b=16), [bank(0)], ["s_albc"])
            C0all = T(ph, "s_C0all", [128, 16, 4, 129], F32)
            C0bf = T(ph, "s_C0bf", [128, 16, 4, 129], BF16)
            vm = T(ph, "s_vm", [128, 16, 516], BF16)
            qm = [T(ph, f"s_qm{i}", [128, 16, 128], BF16) for i in range(2)]
            Cn = [T(ph, f"s_Cn{i}", [128, 4, 129], F32) for i in range(2)]
            nnew = T(ph, "s_nnew", [128, 16, 4], F32)
            sC = DIN("state_C")
            c0regs = [f"s_C0a{q}" for q in range(4)]
            for q in range(4):
                dma("sp", c0regs[q], C0all[:, 4 * q:4 * q + 4, :, 0:128], sC[4 * q:4 * q + 4].rearrange("b h k v -> k b h v"),
                    writes=[c0regs[q]])
            cp("dve", C0all[:, :, :, 128], n0T[:, :, :], ["s_n0T"] + c0regs, c0regs)
            for b in range(16):
                tt("dve", C0bf[:, b, :, :], C0all[:, b, :, :], albc_s[:, :, b:b + 1].to_broadcast([128, 4, 129]), ALU.mult,
                   [c0regs[b // 4], "s_albc"], ["s_C0bf"])
            tt("pool", kts[:], ktok[:], cols[:, 0:4].unsqueeze(2).to_broadcast([128, 4, 128]), ALU.mult,
               ["s_ktok", "s_cols"], ["s_kts"])
            tt("pool", vm[:], vaug[:].rearrange("p h v -> p (h v)").unsqueeze(1).to_broadcast([128, 16, 516]),
               rmask[:].unsqueeze(2).to_broadcast([128, 16, 516]), ALU.mult, ["s_vaug", "s_rmask"], ["s_vm"])

            def x2_s(h):
                i = h % 2
                tt("pool", qm[i][:], qT[:, h:h + 1, :].to_broadcast([128, 16, 128]), bmask[:], ALU.mult,
                   ["s_qT", "s_bmask"], [f"s_qm{i}"])
                return [(qm[i][:, b, :], C0bf[:, b, h, :], [f"s_qm{i}", "s_C0bf"]) for b in range(16)]

            rn_s = dict(qT="s_qT", kT="s_kT", vaug="s_vaug", wl="s_cols", e="s_cols", mask="s_caus",
                        sig_mo="s_sig_mo", silu_mz="s_silu_mz")
            hm = mlstm_intra(rn_s, "s_", qT, kT, vaug, wlc_s, ec_s, caus_s[:], x2_s, it)
            ml_post(rn_s, "s_", hm, sig_mo[:], silu_mz[:], gn_bc[:], ys_ml[:], "s_y", gt)
            for b in range(16):
                i = b % 2
                for h in range(4):
                    pb = 2 + 2 * i + h // 2
                    col = slice((h % 2) * 129, (h % 2) * 129 + 129)
                    mm(psf(pb)[:, col], kts[:, h, :], vm[:, b, h * 129:(h + 1) * 129], True, True,
                       ["s_kts", "s_vm"], [bank(pb)])
                tt("pool", Cn[i][:], C0all[:, b, :, :], albc_s[:, :, b:b + 1].to_broadcast([128, 4, 129]), ALU.mult,
                   [c0regs[b // 4], "s_albc"], [f"s_Cn{i}"])
                for hh in range(2):
                    pb = 2 + 2 * i + hh
                    tt("dve", Cn[i][:, 2 * hh:2 * hh + 2, :], Cn[i][:, 2 * hh:2 * hh + 2, :],
                       psf(pb)[:, 0:258].rearrange("p (h v) -> p h v", v=129), ALU.add, [f"s_Cn{i}", bank(pb)], [f"s_Cn{i}"])
                dma("sp", f"o_Cs{i}", outs["o_Cs"][b], Cn[i][:, :, 0:128], reads=[f"s_Cn{i}"], writes=["out_Cs"])
                cp("act", nnew[:, b, :], Cn[i][:, :, 128], [f"s_Cn{i}"], ["s_nnew"])
            dma("sp", "o_ns", outs["o_ns"], nnew[:], reads=["s_nnew"], writes=["out_ns"])
            S.emit()
          omT = T(sst, "b_omT", [64, 8, 128], F32)
          with contextlib.ExitStack() as ph:
            wuv_s = cast_w(ph, "b_wuv", DIN("w_uv"), 2, 512)
            nmask = T(ph, "b_nmask", [128, 16, 64], BF16)
            dma("pool", "b_nmask", nmask[:], DIN("nmask"), writes=["b_nmask"])
            ptab = T(ph, "b_ptab", [128, 16], I32)
            dma("sp", "b_ptab", ptab[:], DIN("ptab"), writes=["b_ptab"])
            idx = T(ph, "b_idx", [128, 16], I32)
            idx8 = T(ph, "b_idx8", [128, 16, 8], I32)
            ts("dve", idx[0:64, :], ptab[0:64, :], 2.0, None, ALU.mult, None, ["b_ptab"], ["b_idx"])
            ts("dve", idx[64:128, :], ptab[64:128, :], 2.0, 1.0, ALU.mult, ALU.add, ["b_ptab"], ["b_idx"])
            for tc in range(8):
                ts("dve", idx8[:, :, tc], idx[:], 8.0, float(tc), ALU.mult, ALU.add, ["b_idx"], ["b_idx8"])
            ckv_view = DIN("cache_ckv").rearrange("p (a t) c -> (p a) (t c)", t=8)
            kr_view = DIN("cache_kr").rearrange("p (a t) r -> (p a) (t r)", a=2)
            NG = 3
            cg = [T(ph, f"b_cg{i}", [128, 64 * 256], BF16) for i in range(NG)]
            krg = [T(ph, f"b_krg{i}", [128, 2048], BF16) for i in range(NG)]
            ckT = [T(ph, f"b_ckT{i}", [128, 1024], BF16) for i in range(2)]
            krTt = [T(ph, f"b_krTt{i}", [128, 128], BF16) for i in range(2)]
            PTs = [T(ph, f"b_PTs{i}", [128, 65, 64], BF16) for i in range(2)]
            rs = T(ph, "b_rs", [128, 64], F32)
            onT = T(ph, "b_onT", [128, 2, 64], BF16)

            def gather(b):
                i = b % NG
                for tc in range(8):
                    S.raw("pool", f"b_cg{i}",
                          lambda e, i=i, tc=tc, b=b: e.indirect_dma_start(
                              out=cg[i][:, tc * 2048:(tc + 1) * 2048], out_offset=None, in_=ckv_view,
                              in_offset=bass.IndirectOffsetOnAxis(ap=idx8[:, b, tc:tc + 1], axis=0)),
                          reads=["b_idx8"], writes=[f"b_cg{i}"])
                S.raw("pool", f"b_krg{i}",
                      lambda e, i=i, b=b: e.indirect_dma_start(
                          out=krg[i][:], out_offset=None, in_=kr_view,
                          in_offset=bass.IndirectOffsetOnAxis(ap=idx[:, b:b + 1], axis=0)),
                      reads=["b_idx"], writes=[f"b_krg{i}"])

            def score_steps(b):
                i = b % NG
                P = PTs[b % 2]
                pn = f"b_PTs{b % 2}"
                qcols = slice(b * 8, (b + 1) * 8)

                def sb_T(tg):
                    k2 = tg % 2
                    for j in range(4):
                        t = 4 * tg + j
                        for cc in range(2):
                            tr(psb(k2)[:, (j * 2 + cc) * 128:(j * 2 + cc + 1) * 128],
                               cg[i][:, t * 256 + cc * 128:t * 256 + (cc + 1) * 128], ident[:],
                               [f"b_cg{i}", "ident"], [bank(k2)])
                    tr(psb(2 + k2)[:, 0:128], krg[i][:, tg * 128:(tg + 1) * 128], ident[:],
                       [f"b_krg{i}", "ident"], [bank(2 + k2)])
                    cp("act" if tg % 2 == 0 else "dve", ckT[k2][:], psb(k2), [bank(k2)], [f"b_ckT{k2}"])
                    cp("dve" if tg % 2 == 0 else "act", krTt[k2][:, 0:128], psb(2 + k2)[:, 0:128], [bank(2 + k2)],
                       [f"b_krTt{k2}"])

                def sb_S(tg):
                    k2 = tg % 2
                    sbk = 4 + (tg // 2) % 2
                    for j in range(4):
                        t = 4 * tg + j
                        oc = slice((t % 8) * 64, (t % 8 + 1) * 64)
                        mm(psf(sbk)[:, oc], ckT[k2][:, (j * 2) * 128:(j * 2 + 1) * 128], qlatT[:, 0, :, qcols],
                           True, False, [f"b_ckT{k2}", "qlatT"], [bank(sbk)])
                        mm(psf(sbk)[:, oc], ckT[k2][:, (j * 2 + 1) * 128:(j * 2 + 2) * 128], qlatT[:, 1, :, qcols],
                           False, False, [f"b_ckT{k2}", "qlatT"], [bank(sbk)])
                        mm(psf(sbk)[:, oc], krTt[k2][:, 0:128], qr4[:, j, :, qcols],
                           False, True, [f"b_krTt{k2}", "qr4"], [bank(sbk)])
                    if tg % 2 == 1:
                        t0 = 4 * (tg - 1)
                        act(P[:, t0:t0 + 8, :].rearrange("p t q -> p (t q)"), psf(sbk), AF.Exp, [bank(sbk)], [pn],
                            scale=MLA_SCALE)

                sb_T(0)
                yield
                for tg in range(16):
                    if tg + 1 < 16:
                        sb_T(tg + 1)
                    sb_S(tg)
                    yield
                mm(psf(4)[:, 0:64], ckvnT_s[:, 0, :], qlatT[:, 0, :, qcols], True, False, ["ckvnT_s", "qlatT"], [bank(4)])
                mm(psf(4)[:, 0:64], ckvnT_s[:, 1, :], qlatT[:, 1, :, qcols], False, False, ["ckvnT_s", "qlatT"], [bank(4)])
                mm(psf(4)[:, 0:64], krT_s[:, :], qr4[:, 0, :, qcols], False, True, ["krT_s", "qr4"], [bank(4)])
                act(P[:, 64, :], psf(4)[:, 0:64], AF.Exp, [bank(4)], [pn], scale=MLA_SCALE)
                tt("dve", P[:, 64, :], P[:, 64, :], nmask[:, b, :], ALU.mult, [pn, "b_nmask"], [pn])
                yield

            def pv_steps(b):
                i = b % NG
                P = PTs[b % 2]
                pn = f"b_PTs{b % 2}"
                qcols = slice(b * 8, (b + 1) * 8)
                n = 0
                for cc in range(3):
                    for t in range(65):
                        if cc < 2:
                            lh = cg[i][:, t * 256 + cc * 128:t * 256 + (cc + 1) * 128] if t < 64 else ckvnb_s[:, cc * 128:(cc + 1) * 128]
                            mm(psf(6)[:, cc * 64:(cc + 1) * 64], lh, P[:, t, :], t == 0, t == 64,
                               [f"b_cg{i}", "ckvnb_s", pn], [bank(6)])
                        else:
                            mm(psf(6)[:, 128:192], onesb[:], P[:, t, :], t == 0, t == 64, ["onesb", pn], [bank(6)])
                        n += 1
                        if n % 12 == 0:
                            yield
                op("dve", lambda e: e.reciprocal(rs[:], psf(6)[:, 128:192]), [bank(6)], ["b_rs"])
                tt("dve", onT[:], psf(6)[:, 0:128].rearrange("p (c q) -> p c q", q=64),
                   rs[:].unsqueeze(1).to_broadcast([128, 2, 64]), ALU.mult, [bank(6), "b_rs"], ["b_onT"])
                yield
                for h in range(8):
                    for cc in range(2):
                        mm(psf(7)[0:64, h * 8:(h + 1) * 8], wuv_s[:, cc, h * 64:(h + 1) * 64], onT[:, cc, h * 8:(h + 1) * 8],
                           cc == 0, cc == 1, ["b_wuv", "b_onT"], [bank(7)])
                cp("act", omT[:, :, qcols], psf(7)[0:64, 0:64].rearrange("p (h s) -> p h s", s=8), [bank(7)], ["b_omT"])
                yield

            def drain(*gens):
                live = list(gens)
                while live:
                    for gobj in list(live):
                        try:
                            next(gobj)
                        except StopIteration:
                            live.remove(gobj)

            gather(0)
            gather(1)
            drain(score_steps(0))
            for b in range(16):
                if b + 2 < 16:
                    gather(b + 2)
                if b + 1 < 16:
                    drain(score_steps(b + 1), pv_steps(b))
                else:
                    drain(pv_steps(b))
            S.emit()
          with contextlib.ExitStack() as ph:
            wout_ml = cast_w(ph, "b_wout_ml", DIN("w_out")[0:512, :], 4, 1024)
            wout_mla = T(ph, "b_wout_mla", [64, 8, 1024], BF16)
            dma("pool", "b_wout_mla", wout_mla[:], DIN("w_out")[512:1024, :].rearrange("(h v) n -> v h n", v=64),
                writes=["b_wout_mla"])
            lng = T(ph, "b_lng", [128, D], F32)
            lnb = T(ph, "b_lnb", [128, D], F32)
            dma("sp", "b_lng", lng[:], DIN("ln_g").partition_broadcast(128), writes=["lng"])
            dma("sp", "b_lnb", lnb[:], DIN("ln_b").partition_broadcast(128), writes=["lnb"])
            ymlaT = T(ph, "b_ymlaT", [64, 8, 128], BF16)
            yT_ml = T(ph, "b_yT_ml", [128, 4, 128], BF16)
            lt = (T(ph, "b_z", [128, D], F32), T(ph, "b_sqz", [128, D], F32), T(ph, "b_st1", [128, 8], F32))
            tt("dve", ymlaT[:], omT[:], silu_azT[0:64, :, :], ALU.mult, ["b_omT", "silu_azT"], ["b_ymlaT"])
            for fc in range(4):
                tr(psb(0)[:, fc * 128:(fc + 1) * 128], ys_ml[:, fc * 128:(fc + 1) * 128], ident[:], ["s_y", "ident"], [bank(0)])
            cp("act", yT_ml[:].rearrange("p k t -> p (k t)"), psb(0)[:, 0:512], [bank(0)], ["b_yT_ml"])
            for half in range(2):
                hs = slice(half * 512, (half + 1) * 512)
                for fc in range(4):
                    mm(psf(2 + half), yT_ml[:, fc, :], wout_ml[:, fc, hs], fc == 0, False, ["b_yT_ml", "b_wout_ml"], [bank(2 + half)])
                for h in range(8):
                    mm(psf(2 + half), ymlaT[:, h, :], wout_mla[:, h, hs], False, h == 7, ["b_ymlaT", "b_wout_mla"], [bank(2 + half)])
            ln_out("b_", "b_", (2, 3), xf_s[:], gate_s, lng, lnb, outs["o_ys"], "out_ys", lt)
            S.emit()
          sst.close()
        wl_own = T(st, "wl_own", [128, 4, NOWN], F32)
        e_own = T(st, "e_own", [128, 4, NOWN], F32)
        al_own = T(st, "al_own", [128, 4, NOWN], F32)
        wlT = T(st, "wlT", [128, 4, NBLK], F32)
        eT = T(st, "eT", [128, 4, NBLK], F32)
        albc = T(st, "albc", [128, 4, NBLK], F32)
        attn = T(st, "attn", [128, NOWN, 512], BF16)
        Cown = T(st, "Cown", [128, NOWN, 4, 129], BF16)
        st14 = contextlib.ExitStack()
        ckvT_all = T(st14, "ckvT_all", [128, 2, SEQ], BF16)
        KRp = T(st14, "KRp", [128, NBLK, 96], BF16)
        GT = T(st14, "GT", [128, NBLK, 8], F32)
        op("pool", lambda e: e.memset(KRp[:], 0.0), writes=["KRp"])
        with contextlib.ExitStack() as ph:
            Wa = cast_w(ph, "Wa", DIN("w_all"), KC, 1352)
            xb = [T(ph, f"xb{i}", [128, D], BF16) for i in range(4)]
            hT = [T(ph, f"hT{i}", [128, KC, 128], BF16) for i in range(2)]
            rope = [T(ph, f"rope{i}", [128, 64], F32) for i in range(4)]
            ktok = [T(ph, f"ktok{i}", [128, 512], BF16) for i in range(2)]
            vaug = [T(ph, f"vaug{i}", [128, 4, 129], BF16) for i in range(2)]
            ckvn = [T(ph, f"ckvn{i}", [128, 256], F32) for i in range(2)]
            ckvb = [T(ph, f"ckvb{i}", [128, 256], BF16) for i in range(2)]
            krr = [T(ph, f"krr{i}", [128, 32], F32) for i in range(2)]
            mt = misc_tmp(ph, "p1")
            for i in range(2):
                op("pool", lambda e, i=i: e.memset(vaug[i][:], 1.0), writes=[f"vaug{i}"])
            xall = DIN("x_all")
            ropeall = DIN("rope_all")
            def p1_L(blk):
                b4 = blk % 4
                rows = slice(blk * 128, (blk + 1) * 128)
                dma("pool", f"xb{b4}", xb[b4][:], xall[rows, :], writes=[f"xb{b4}"])
                dma("sp", f"rope{b4}", rope[b4][:], ropeall[rows, :], writes=[f"rope{b4}"])

            def p1_T(blk):
                b = blk % 2
                b4 = blk % 4
                make_hT(xb[b4], f"xb{b4}", hT[b], f"hT{b}", b, False)

            def p1_M(blk):
                b = blk % 2
                tokgroup(2, hT[b], f"hT{b}", Wa, "Wa", 0, 512)
                act(ktok[b][:], psf(2), AF.Identity, [bank(2)], [f"ktok{b}"], scale=DK ** -0.5)
                tokgroup(3, hT[b], f"hT{b}", Wa, "Wa", 512, 512)
                cp("dve", vaug[b][:, :, 0:128], psf(3).rearrange("p (h v) -> p h v", v=128), [bank(3)], [f"vaug{b}"])
                tokgroup(4 + b, hT[b], f"hT{b}", Wa, "Wa", 1024, 328)
                dma("sp", f"spk{b}", kscr[blk], ktok[b][:], reads=[f"ktok{b}"], writes=[f"kscr{blk}"])
                dma("sp", f"spv{b}", vscr[blk], vaug[b][:], reads=[f"vaug{b}"], writes=[f"vscr{blk}"])
                misc_post(4 + b, rope[blk % 4], f"rope{blk % 4}", ckvn[b], f"ckvn{b}", krr[b], f"krr{b}", mt)
                cp("act", GT[:, blk, :], psf(4 + b)[:, 320:328], [bank(4 + b)], ["GT"])
                cp("pool", ckvb[b][:], ckvn[b][:], [f"ckvn{b}"], [f"ckvb{b}"])
                cp("pool", KRp[:, blk, 64:96], krr[b][:], [f"krr{b}"], ["KRp"])

            def p1_CT(blk):
                b = blk % 2
                rows = slice(blk * 128, (blk + 1) * 128)
                for cc in range(2):
                    tr(psb(6 + b)[:, cc * 128:(cc + 1) * 128], ckvb[b][:, cc * 128:(cc + 1) * 128], ident[:],
                       [f"ckvb{b}", "ident"], [bank(6 + b)])
                cp("act", ckvT_all[:, :, rows], psb(6 + b)[:, 0:256].rearrange("p (c t) -> p c t", t=128),
                   [bank(6 + b)], ["ckvT_all"])

            for blk in range(3):
                p1_L(blk)
            p1_T(0)
            for blk in range(NBLK):
                if blk + 3 < NBLK:
                    p1_L(blk + 3)
                if blk + 1 < NBLK:
                    p1_T(blk + 1)
                p1_M(blk)
                if blk >= 1:
                    p1_CT(blk - 1)
            p1_CT(NBLK - 1)
            S.emit()

        with contextlib.ExitStack() as ph:
            tstrict = T(ph, "tstrict", [64, 64], F32)
            segm = T(ph, "segm", [64, 512], F32)
            bif = T(ph, "bif", [64, 8], F32)
            dma("sp", "tstrict", tstrict[:], DIN("tstrict"), writes=["tstrict"])
            dma("sp", "segm", segm[:], DIN("segmask")[0:64, :], writes=["segm"])
            dma("sp", "bif", bif[:], DIN("b_if_bc"), writes=["bif"])
            ig = T(ph, "ig", [64, 4, 128], F32)
            lf = T(ph, "lf", [64, 4, 128], F32)
            Ll = T(ph, "Ll", [64, 4, 128], F32)
            Lg = T(ph, "Lg", [64, 4, 128], F32)
            G = T(ph, "G", [64, 4, 128], F32)
            wl = T(ph, "wl", [64, 4, 128], F32)
            ee = T(ph, "ee", [64, 4, 128], F32)
            sm = T(ph, "sm", [64, 8, 4], F32)
            smT = T(ph, "smT", [4, 4, 64], F32)
            aldiag = T(ph, "aldiag", [64, 4, 64], F32)
            for j in range(8):
                pb = 0 if j < 4 else 1
                tr(psf(pb)[0:64, (j % 4) * 128:(j % 4 + 1) * 128], GT[:, :, j], identf[:], ["GT", "identf"], [bank(pb)])
            v3 = lambda a: a.rearrange("p (h s) -> p h s", s=128)
            tt("dve", ig[:], v3(psf(0)[0:64, :]), bif[:, 0:4].unsqueeze(2).to_broadcast([64, 4, 128]), ALU.add,
               [bank(0), "bif"], ["ig"])
            tt("dve", lf[:], v3(psf(1)[0:64, :]), bif[:, 4:8].unsqueeze(2).to_broadcast([64, 4, 128]), ALU.add,
               [bank(1), "bif"], ["lf"])
            act(lf[:], lf[:], AF.Exp, ["lf"], ["lf"], scale=-1.0)
            act(lf[:], lf[:], AF.Ln, ["lf"], ["lf"], bias=1.0)
            op("dve", lambda e: e.tensor_tensor_scan(Ll[:].rearrange("p h s -> p (h s)"), segm[:],
                                                       lf[:].rearrange("p h s -> p (h s)"), 0.0, ALU.mult, ALU.add),
               ["segm", "lf"], ["Ll"])
            cp("dve", sm[:, 0, :], Ll[:, :, 127], ["Ll"], ["sm0"])
            mm(psf(2)[0:64, 0:4], tstrict[:], sm[:, 0, :], True, True, ["tstrict", "sm0"], [bank(2)])
            cp("dve", sm[:, 1, :], psf(2)[0:64, 0:4], [bank(2)], ["sm1"])
            tt("dve", Lg[:], Ll[:], sm[:, 1, :].unsqueeze(2).to_broadcast([64, 4, 128]), ALU.add, ["Ll", "sm1"], ["Lg"])
            tt("dve", G[:], ig[:], Lg[:], ALU.add, ["ig", "Lg"], ["G"])
            red(sm[:, 2, :], G[:], ALU.max, ["G"], ["sm2"])
            tr(psf(3)[0:4, 0:64], sm[:, 2, :], identf[0:64, 0:64], ["sm2", "identf"], [bank(3)])
            cp("dve", smT[:, 0, :], psf(3)[0:4, 0:64], [bank(3)], ["smT0"])
            op("dve", lambda e: e.tensor_tensor_scan(smT[:, 1, :], smT[:, 0, :], smT[:, 0, :], NEG, ALU.max, ALU.max),
               ["smT0"], ["smT1"])
            op("dve", lambda e: e.memset(smT[:, 2, 0:1], NEG), [], ["smT2a"])
            cp("dve", smT[:, 2, 1:64], smT[:, 1, 0:63], ["smT1", "smT2a"], ["smT2"])
            tr(psf(4)[0:64, 0:4], smT[:, 1, :], identf[0:4, 0:4], ["smT1", "identf"], [bank(4)])
            tr(psf(4)[0:64, 4:8], smT[:, 2, :], identf[0:4, 0:4], ["smT2", "identf"], [bank(4)])
            cp("dve", sm[:, 3:5, :], psf(4)[0:64, 0:8].rearrange("p (a h) -> p a h", h=4), [bank(4)], ["sm34"])
            bcR = sm[:, 3, :].unsqueeze(2).to_broadcast([64, 4, 128])
            tt("dve", wl[:], G[:], bcR, ALU.subtract, ["G", "sm34"], ["wl"])
            act(wl[:], wl[:], AF.Exp, ["wl"], ["wl"])
            tt("dve", ee[:], Lg[:], bcR, ALU.subtract, ["Lg", "sm34"], ["ee"])
            act(ee[:], ee[:], AF.Exp, ["ee"], ["ee"])
            tt("dve", sm[:, 5, :], sm[:, 4, :], sm[:, 3, :], ALU.subtract, ["sm34"], ["sm5"])
            act(sm[:, 5, :], sm[:, 5, :], AF.Exp, ["sm5"], ["sm5"])
            tt("dve", sm[:, 6, :], sm[:, 3, :], Lg[:, :, 127], ALU.subtract, ["sm34", "Lg"], ["sm6"])
            dma("sp", "o_m", outs["o_m"], sm[63:64, 6, :], reads=["sm6"], writes=["o_m"])
            for h in range(4):
                tr(psf(5)[:, h * 64:(h + 1) * 64], wl[:, h, :], identf[0:64, 0:64], ["wl", "identf"], [bank(5)])
                tr(psf(6)[:, h * 64:(h + 1) * 64], ee[:, h, :], identf[0:64, 0:64], ["ee", "identf"], [bank(6)])
            cp("dve", wlT[:], psf(5)[:, 0:256].rearrange("p (h c) -> p h c", c=64), [bank(5)], ["wlT"])
            cp("act", eT[:], psf(6)[:, 0:256].rearrange("p (h c) -> p h c", c=64), [bank(6)], ["eT"])
            tt("dve", aldiag[:], sm[:, 5, :].unsqueeze(2).to_broadcast([64, 4, 64]),
               identf[0:64, 0:64].unsqueeze(1).to_broadcast([64, 4, 64]), ALU.mult, ["sm5", "identf"], ["aldiag"])
            mm(psf(7)[:, 0:256], onesf[0:64, :], aldiag[:].rearrange("p h c -> p (h c)"), True, True,
               ["onesf", "aldiag"], [bank(7)])
            cp("dve", albc[:], psf(7)[:, 0:256].rearrange("p (h c) -> p h c", c=64), [bank(7)], ["albc"])
            for src, dst, dn in ((wlT, wl_own, "wl_own"), (eT, e_own, "e_own"), (albc, al_own, "al_own")):
                sv = src[:].rearrange("p h (g i) -> p h g i", i=4)
                ts("dve", dst[:], sv[:, :, :, 0], sel[:, 0:1], None, ALU.mult, None, ["wlT", "eT", "albc", "sel"], [dn])
                for i in range(1, 4):
                    stt(dst[:], sv[:, :, :, i], sel[:, i:i + 1], dst[:], ALU.mult, ALU.add, ["wlT", "eT", "albc", "sel", dn], [dn])
            S.emit()

        if stage >= 2:
          with contextlib.ExitStack() as ph:
            WOWN = DIN("w_own")
            qaT = T(ph, "qaT", [128, 8, NOWN * 128], BF16)
            wukp = cast_w(ph, "wukp", DIN("w_ukp").rearrange("c h n -> c (h n)"), 2, 768)
            wuv = cast_w(ph, "wuv", DIN("w_uv"), 2, 512)
            maskA = T(ph, "maskA", [128, 512], BF16)
            dma("pool", "maskA", maskA[:], DIN("maskA").rearrange("p i q -> p (i q)"), writes=["maskA"])
            ph_outer = ph
            ph = contextlib.ExitStack()
            Wcq = cast_w(ph, "Wcq", WOWN[:, 1024:1408], KC, 384)
            gqT = T(ph, "gqT", [128, 3], F32)
            dma("sp", "gqT", gqT[:], DIN("gqT"), writes=["gqT"])
            wq_st = T(ph, "wq_st", [128, 3, 768], F32)
            wuq = T(ph, "wuq", [128, 3, 768], BF16)
            wuqp = T(ph, "wuqp", [128, 3, 768], BF16)
            for nm, dst, dn in (("w_uq", wuq, "wuq"), ("w_uqp", wuqp, "wuqp")):
                dma("sp", "wq_st", wq_st[:], DIN(nm).rearrange("(c p) h n -> p c (h n)", p=128), writes=["wq_st"])
                for cc in range(3):
                    ts("dve", dst[:, cc, :], wq_st[:, cc, :], gqT[:, cc:cc + 1], None, ALU.mult, None,
                       ["wq_st", "gqT"], [dn])
            xb = [T(ph, f"q_xb{i}", [128, D], BF16) for i in range(4)]
            hT = [T(ph, f"q_hT{i}", [128, KC, 128], BF16) for i in range(3)]
            ropeT = [T(ph, f"q_ropeT{i}", [128, 2, 128], F32) for i in range(4)]
            sqc = T(ph, "sqc", [128, 3, 128], BF16)
            rq = T(ph, "rq", [128, 128], F32)
            rq2 = T(ph, "rq2", [128, 128], F32)
            cqn = [T(ph, f"cqn{i}", [128, 3, 128], BF16) for i in range(2)]
            tq1 = T(ph, "tq1", [128, 4, 128], F32)
            tq2 = T(ph, "tq2", [128, 4, 128], F32)
            xown = DIN("x_own")
            ropeTo = DIN("ropeT_own")
            v3 = lambda a: a.rearrange("p (h t) -> p h t", t=128)

            def q_L(g):
                b4 = g % 4
                rows = slice(g * 128, (g + 1) * 128)
                dma("pool", f"q_xb{b4}", xb[b4][:], xown[rows, :], writes=[f"q_xb{b4}"])
                dma("sp", f"q_ropeT{b4}", ropeT[b4][:], ropeTo[:, :, rows], writes=[f"q_ropeT{b4}"])

            def q_T(g):
                make_hT(xb[g % 4], f"q_xb{g % 4}", hT[g % 3], f"q_hT{g % 3}", g % 2, False)

            def q_A(g):
                b = g % 2
                for cc in range(3):
                    featgroup(2, cc * 128, hT[g % 3], f"q_hT{g % 3}", Wcq, "Wcq", cc * 128, 128)
                act(sqc[:], psf(2)[:, 0:384].rearrange("p (c t) -> p c t", t=128), AF.Square, [bank(2)], ["sqc"])
                for cc in range(3):
                    mm(psf(3)[:, 0:128], onesb[:], sqc[:, cc, :], cc == 0, cc == 2, ["onesb", "sqc"], [bank(3)])
                ts("dve", rq[:], psf(3)[:, 0:128], 1.0 / 384.0, EPS, ALU.mult, ALU.add, [bank(3)], ["rq"])
                rsqrt_pool(rq2[:], rq[:], None, ["rq"], ["rq2"])
                tt("dve", cqn[b][:], psf(2)[:, 0:384].rearrange("p (c t) -> p c t", t=128),
                   rq2[:].unsqueeze(1).to_broadcast([128, 3, 128]), ALU.mult, [bank(2), "rq2"], [f"cqn{b}"])

            def q_B(g):
                b = g % 2
                b4 = g % 4
                rows = slice(g * 128, (g + 1) * 128)
                for h in range(8):
                    col = slice((h % 4) * 128, (h % 4 + 1) * 128)
                    for cc in range(3):
                        mm(psf(4 + h // 4)[0:96, col], wuq[:, cc, h * 96:(h + 1) * 96], cqn[b][:, cc, :], cc == 0, cc == 2,
                           ["wuq", f"cqn{b}"], [bank(4 + h // 4)])
                    for cc in range(3):
                        mm(psf(6 + h // 4)[0:96, col], wuqp[:, cc, h * 96:(h + 1) * 96], cqn[b][:, cc, :], cc == 0, cc == 2,
                           ["wuqp", f"cqn{b}"], [bank(6 + h // 4)])
                for hb in range(2):
                    cp("act", qaT[0:64, 4 * hb:4 * hb + 4, rows], v3(psf(4 + hb)[0:64, :]), [bank(4 + hb)], ["qaT"])
                    tt("dve", tq1[64:96], v3(psf(4 + hb)[64:96, :]),
                       ropeT[b4][64:96, 0:1, :].to_broadcast([32, 4, 128]), ALU.mult,
                       [bank(4 + hb), f"q_ropeT{b4}"], ["tq1"])
                    tt("dve", tq2[64:96], v3(psf(6 + hb)[64:96, :]),
                       ropeT[b4][64:96, 1:2, :].to_broadcast([32, 4, 128]), ALU.mult,
                       [bank(6 + hb), f"q_ropeT{b4}"], ["tq2"])
                    tt("pool", qaT[64:96, 4 * hb:4 * hb + 4, rows], tq1[64:96], tq2[64:96], ALU.add,
                       ["tq1", "tq2"], ["qaT"])

            for g in range(3):
                q_L(g)
            q_T(0)
            q_T(1)
            q_A(0)
            for g in range(NOWN):
                if g + 3 < NOWN:
                    q_L(g + 3)
                if g + 2 < NOWN:
                    q_T(g + 2)
                if g + 1 < NOWN:
                    q_A(g + 1)
                q_B(g)
            S.emit()
            ph.close()
            ph = ph_outer
            KT = T(ph, "KT", [128, 2, SEQ], BF16)
            Vt = T(ph, "Vt", [128, NBLK, 2, 65], BF16)
            Cst = T(ph, "Cst", [128, 4, 129], F32)
            Cown32 = T(ph, "Cown32", [128, 4, 129], F32)
            kin = [T(ph, f"kin{i}", [128, 4, 128], BF16) for i in range(2)]
            vin = [T(ph, f"vin{i}", [128, 4, 129], BF16) for i in range(2)]
            kt = [T(ph, f"kt{i}", [128, 4, 128], BF16) for i in range(2)]
            op("pool", lambda e: e.memset(Cst[:], 0.0), writes=["Cst"])

            def rec_steps():
                for c in range(NBLK):
                    b = c % 2
                    g, i = c // 4, c % 4
                    dma("sp", f"kin{b}", kin[b][:].rearrange("p h d -> p (h d)"), kscr[c], reads=[f"kscr{c}"], writes=[f"kin{b}"])
                    dma("sp", f"vin{b}", vin[b][:], vscr[c], reads=[f"vscr{c}"], writes=[f"vin{b}"])
                    tt("pool", kt[b][:], kin[b][:], wlT[:, :, c:c + 1].to_broadcast([128, 4, 128]), ALU.mult,
                       [f"kin{b}", "wlT"], [f"kt{b}"])
                    yield
                    for h in range(4):
                        pb = h // 2
                        col = slice((h % 2) * 129, (h % 2) * 129 + 129)
                        mm(psf(pb)[:, col], kt[b][:, h, :], vin[b][:, h, :], True, True, [f"kt{b}", f"vin{b}"], [bank(pb)])
                    if i == 0:
                        ts("dve", Cown32[:], Cst[:], sel[:, 0:1], None, ALU.mult, None, ["Cst", "sel"], ["Cown32"])
                    else:
                        stt(Cown32[:], Cst[:], sel[:, i:i + 1], Cown32[:], ALU.mult, ALU.add, ["Cst", "sel", "Cown32"], ["Cown32"])
                    if i == 3:
                        tt("pool", Cown[:, g, :, :], Cown32[:], al_own[:, :, g:g + 1].to_broadcast([128, 4, 129]), ALU.mult,
                           ["Cown32", "al_own"], ["Cown"])
                    for h in range(4):
                        pb = h // 2
                        col = slice((h % 2) * 129, (h % 2) * 129 + 129)
                        stt(Cst[:, h, :], Cst[:, h, :], albc[:, h, c:c + 1], psf(pb)[:, col], ALU.mult, ALU.add,
                            ["Cst", "albc", bank(pb)], ["Cst"])
                    yield
                dma("sp", "o_C", outs["o_C"], Cst[:], reads=["Cst"], writes=["o_C"])
                yield

            rec = rec_steps()

            def rec_tick():
                try:
                    next(rec)
                except StopIteration:
                    pass

            PT = [T(ph, f"PT{i}", [128, 512], BF16) for i in range(3)]
            rinv = T(ph, "rinv", [128, 2], F32)
            op("pool", lambda e: e.memset(Vt[:], 1.0), writes=["Vt"])
            cnt = 0
            for p in range(4):
                for kb4 in range(16):
                    for hh in range(2):
                        h = 2 * p + hh
                        pb = 2 + (kb4 % 2) * 2 + hh
                        keys = slice(kb4 * 512, (kb4 + 1) * 512)
                        for cc in range(2):
                            mm(psf(pb)[0:96, :], wukp[:, cc, h * 96:(h + 1) * 96], ckvT_all[:, cc, keys], cc == 0, False,
                               ["wukp", "ckvT_all"], [bank(pb)])
                        for i in range(4):
                            mm(psf(pb)[0:96, i * 128:(i + 1) * 128], KRp[:, kb4 * 4 + i, :], ident[:], False, i == 3,
                               ["KRp", "ident"], [bank(pb)])
                        cp("act" if hh == 0 else "dve", KT[0:96, hh, keys], psf(pb)[0:96, :], [bank(pb)], ["KT"])
                for kb in range(NBLK):
                    pb = 6 + (kb // 4) % 2
                    for cc in range(2):
                        mm(psf(pb)[:, (kb % 4) * 128:(kb % 4 + 1) * 128], ckvT_all[:, cc, kb * 128:(kb + 1) * 128],
                           wuv[:, cc, p * 128:(p + 1) * 128], cc == 0, cc == 1, ["ckvT_all", "wuv"], [bank(pb)])
                    if kb % 4 == 3:
                        cp("dve", Vt[:, kb - 3:kb + 1, :, 0:64],
                           psf(pb)[:, 0:512].rearrange("p (k h v) -> p k h v", h=2, v=64), [bank(pb)], ["Vt"])
                items = [(g, hh, kg) for g in range(NOWN) for hh in range(2) for kg in range(g + 1)]

                def emit_S(n):
                    g, hh, kg = items[n]
                    h = 2 * p + hh
                    sbk = 2 + n % 4
                    for i in range(4):
                        kb = 4 * kg + i
                        mm(psf(sbk)[:, i * 128:(i + 1) * 128], KT[0:96, hh, kb * 128:(kb + 1) * 128],
                           qaT[0:96, h, g * 128:(g + 1) * 128], True, True, ["KT", "qaT"], [bank(sbk)])

                emit_S(0)
                for n, (g, hh, kg) in enumerate(items):
                    if n % 8 == 4:
                        rec_tick()
                    ob = 6 + g % 2
                    sbk = 2 + n % 4
                    pt = PT[n % 3]
                    ptn = f"PT{n % 3}"
                    act(pt[:], psf(sbk), AF.Exp, [bank(sbk)], [ptn], scale=MLA_SCALE)
                    if kg == g:
                        tt("dve", pt[:], pt[:], maskA[:], ALU.mult, [ptn, "maskA"], [ptn])
                    if n + 1 < len(items):
                        emit_S(n + 1)
                    for i in range(4):
                        kb = 4 * kg + i
                        mm(psf(ob)[:, hh * 65:(hh + 1) * 65], pt[:, i * 128:(i + 1) * 128], Vt[:, kb, hh, :],
                           kg == 0 and i == 0, kg == g and i == 3, [ptn, "Vt"], [bank(ob)])
                    if hh == 1 and kg == g:
                        ov = psf(ob)[:, 0:130].rearrange("p (h v) -> p h v", v=65)
                        op("dve", lambda e, ov=ov: e.reciprocal(rinv[:], ov[:, :, 64]), [bank(ob)], ["rinv"])
                        tt("dve", attn[:, g, p * 128:(p + 1) * 128].rearrange("p (h v) -> p h v", v=64), ov[:, :, 0:64],
                           rinv[:].unsqueeze(2).to_broadcast([128, 2, 64]), ALU.mult, [bank(ob), "rinv"], ["attn"])
            for _ in range(200):
                rec_tick()
            S.emit()
        st14.close()
        if stage >= 2:
          with contextlib.ExitStack() as ph:
            WOWN = DIN("w_own")
            WALL = DIN("w_all")
            Wq5 = cast_w(ph, "Wq5", WOWN[:, 0:1024], KC, 1024)
            Wg5 = cast_w(ph, "Wg5", WOWN[:, 1408:2944], KC, 1536)
            Wv5 = cast_w(ph, "Wv5", WALL[:, 512:1352], KC, 840)
            wout = cast_w(ph, "wout", DIN("w_out"), KC, 1024)
            gn_bc = T(ph, "gn_bc", [128, 512], F32)
            lng = T(ph, "lng", [128, D], F32)
            lnb = T(ph, "lnb", [128, D], F32)
            caus = T(ph, "caus", [128, 128], F32)
            dma("sp", "gn_bc", gn_bc[:], DIN("ml_gn").partition_broadcast(128), writes=["gn_bc"])
            dma("sp", "lng", lng[:], DIN("ln_g").partition_broadcast(128), writes=["lng"])
            dma("sp", "lnb", lnb[:], DIN("ln_b").partition_broadcast(128), writes=["lnb"])
            dma("sp", "caus", caus[:], DIN("caus"), writes=["caus"])
            xb = [T(ph, f"o_xb{i}", [128, D], BF16) for i in range(4)]
            xf = [T(ph, f"o_xf{i}", [128, D], F32) for i in range(2)]
            hT = [T(ph, f"o_hT{i}", [128, KC, 128], BF16) for i in range(3)]
            rope = [T(ph, f"o_rope{i}", [128, 64], F32) for i in range(4)]
            qT = [T(ph, f"o_qT{i}", [128, 4, 128], BF16) for i in range(2)]
            kT = [T(ph, f"o_kT{i}", [128, 4, 128], BF16) for i in range(2)]
            vaug = [T(ph, f"o_vaug{i}", [128, 4, 129], BF16) for i in range(2)]
            ckvn = [T(ph, f"o_ckvn{i}", [128, 256], F32) for i in range(2)]
            krr = [T(ph, f"o_krr{i}", [128, 32], F32) for i in range(2)]
            sig_mo = [T(ph, f"o_sig_mo{i}", [128, 512], BF16) for i in range(2)]
            silu_mz = [T(ph, f"o_silu_mz{i}", [128, 512], BF16) for i in range(2)]
            silu_az = [T(ph, f"o_silu_az{i}", [128, 512], BF16) for i in range(2)]
            y = T(ph, "o_y", [128, D], BF16)
            yT = T(ph, "o_yT", [128, KC, 128], BF16)
            mt = misc_tmp(ph, "p5")
            it = intra_tmp(ph, "o_")
            gt = (T(ph, "o_sqh", [128, 4, 128], F32), T(ph, "o_st4", [128, 24], F32))
            lt = (T(ph, "o_z", [128, D], F32), T(ph, "o_sqz", [128, D], F32), T(ph, "o_st1", [128, 8], F32))
            for i in range(2):
                op("pool", lambda e, i=i: e.memset(vaug[i][:], 1.0), writes=[f"o_vaug{i}"])
            xown = DIN("x_own")
            ropeo = DIN("rope_own")

            def p5_L(g):
                b4 = g % 4
                rows = slice(g * 128, (g + 1) * 128)
                dma("pool", f"o_xb{b4}", xb[b4][:], xown[rows, :], writes=[f"o_xb{b4}"])
                dma("sp", f"o_rope{b4}", rope[b4][:], ropeo[rows, :], writes=[f"o_rope{b4}"])

            def p5_T(g):
                make_hT(xb[g % 4], f"o_xb{g % 4}", hT[g % 3], f"o_hT{g % 3}", g % 2, False)

            def p5_M(g):
                b = g % 2
                rows = slice(g * 128, (g + 1) * 128)
                hn = f"o_hT{g % 3}"
                dma("sp", f"o_xf{b}", xf[b][:], xown[rows, :], writes=[f"o{b}_xf"])
                for h in range(4):
                    featgroup(2, h * 128, hT[g % 3], hn, Wq5, "Wq5", h * 128, 128)
                cp("act", qT[b][:].rearrange("p h t -> p (h t)"), psf(2), [bank(2)], [f"o_qT{b}"])
                for h in range(4):
                    featgroup(3, h * 128, hT[g % 3], hn, Wq5, "Wq5", 512 + h * 128, 128)
                act(kT[b][:].rearrange("p h t -> p (h t)"), psf(3), AF.Identity, [bank(3)], [f"o_kT{b}"], scale=DK ** -0.5)
                tokgroup(4, hT[g % 3], hn, Wv5, "Wv5", 0, 512)
                cp("act", vaug[b][:, :, 0:128], psf(4).rearrange("p (h v) -> p h v", v=128), [bank(4)], [f"o_vaug{b}"])
                tokgroup(5, hT[g % 3], hn, Wv5, "Wv5", 512, 328)
                misc_post(5, rope[g % 4], f"o_rope{g % 4}", ckvn[b], f"o_ckvn{b}", krr[b], f"o_krr{b}", mt)
                dma("sp", f"o_ckv{b}", outs["o_ckv"][rows, :], ckvn[b][:], reads=[f"o_ckvn{b}"], writes=["out_ckv"])
                dma("sp", f"o_kr{b}", outs["o_kr"][rows, :], krr[b][:], reads=[f"o_krr{b}"], writes=["out_kr"])
                tokgroup(6, hT[g % 3], hn, Wg5, "Wg5", 0, 512)
                act(sig_mo[b][:], psf(6), AF.Sigmoid, [bank(6)], [f"o_sig_mo{b}"])
                tokgroup(7, hT[g % 3], hn, Wg5, "Wg5", 512, 512)
                act(silu_mz[b][:], psf(7), AF.Silu, [bank(7)], [f"o_silu_mz{b}"])
                tokgroup(2, hT[g % 3], hn, Wg5, "Wg5", 1024, 512)
                act(silu_az[b][:], psf(2), AF.Silu, [bank(2)], [f"o_silu_az{b}"])

            def p5_mid(g):
                b = g % 2
                rn = dict(qT=f"o_qT{b}", kT=f"o_kT{b}", vaug=f"o_vaug{b}", wl="wl_own", e="e_own", mask="caus",
                          sig_mo=f"o_sig_mo{b}", silu_mz=f"o_silu_mz{b}")
                hm = mlstm_intra(rn, "o_", qT[b], kT[b], vaug[b], wl_own[:, :, g], e_own[:, :, g], caus[:],
                                 lambda h: [(qT[b][:, h, :], Cown[:, g, h, :], [f"o_qT{b}", "Cown"])], it)
                ml_post(rn, "o_", hm, sig_mo[b][:], silu_mz[b][:], gn_bc[:], y[:, 0:512], "o_y", gt)
                tt("pool", y[:, 512:1024], attn[:, g, :], silu_az[b][:], ALU.mult, ["attn", f"o_silu_az{b}"], ["o_y"])

            def p5_tail(g):
                b = g % 2
                rows = slice(g * 128, (g + 1) * 128)
                for fc in range(KC):
                    tr(psb(2)[:, fc * 128:(fc + 1) * 128], y[:, fc * 128:(fc + 1) * 128], ident[:],
                       ["o_y", "ident"], [bank(2)])
                cp("act", yT[:].rearrange("p k t -> p (k t)"), psb(2), [bank(2)], ["o_yT"])
                for half in range(2):
                    for fc in range(KC):
                        mm(psf(3 + half), yT[:, fc, :], wout[:, fc, half * 512:(half + 1) * 512], fc == 0, fc == KC - 1,
                           ["o_yT", "wout"], [bank(3 + half)])
                gen = ln_steps(f"o{b}_", "o_", (3, 4), xf[b][:], gate_p, lng, lnb, outs["o_y"][rows, :], "out_y", lt)
                next(gen)
                return gen

            def finish(gen):
                if gen is not None:
                    for _ in gen:
                        pass

            for g in range(3):
                p5_L(g)
            p5_T(0)
            p5_T(1)
            p5_M(0)
            pending = None
            for g in range(NOWN):
                if g + 3 < NOWN:
                    p5_L(g + 3)
                if g + 2 < NOWN:
                    p5_T(g + 2)
                p5_mid(g)
                finish(pending)
                if g + 1 < NOWN:
                    p5_M(g + 1)
                pending = p5_tail(g)
            finish(pending)
            S.emit()
        S.emit()
    return nc, sorted(used_in)


def _rope_tables(pos):
    half = 16
    inv = (10000.0 ** (-np.arange(half, dtype=np.float32) / half)).astype(np.float32)
    ang = pos.astype(np.float32)[:, None] * inv[None, :]
    cos = np.cos(ang).astype(np.float32)
    sin = np.sin(ang).astype(np.float32)
    tok = np.concatenate([cos, cos, -sin, sin], axis=1).astype(np.float32)
    return tok


def _host_inputs(inp):
    f32 = np.float32
    w_in = np.asarray(inp["w_in"][0], f32)
    offs = np.cumsum([0, 512, 512, 512, 4, 4, 512, 512, 384, 256, 32, 512])
    q_, k_, v_, i_, f_, mo_, mz_, cq_, ckv_, kr_, az_ = [slice(int(offs[n]), int(offs[n + 1])) for n in range(11)]
    krw = w_in[:, kr_]
    krperm = np.concatenate([krw[:, 16:32], krw[:, 0:16]], axis=1)
    w_all = np.ascontiguousarray(np.concatenate([w_in[:, k_], w_in[:, v_], w_in[:, ckv_], krw, krperm,
                                                 w_in[:, i_], w_in[:, f_]], axis=1))
    w_own = np.ascontiguousarray(np.concatenate([w_in[:, q_], w_in[:, k_], w_in[:, cq_], w_in[:, mo_],
                                                 w_in[:, mz_], w_in[:, az_]], axis=1))
    b_ada = np.asarray(inp["b_ada"][0], f32)
    b_adaT = np.ascontiguousarray(b_ada.reshape(24, 128).T)
    b_gate = np.ascontiguousarray(b_ada[2048:3072])
    w_uq = np.asarray(inp["mla_w_uq"][0], f32).reshape(384, 8, 96)
    w_uqp = np.zeros_like(w_uq)
    w_uqp[:, :, 64:80] = w_uq[:, :, 80:96]
    w_uqp[:, :, 80:96] = w_uq[:, :, 64:80]
    w_uk = np.asarray(inp["mla_w_uk"][0], f32).reshape(256, 8, 64)
    w_ukp = np.zeros((256, 8, 96), f32)
    w_ukp[:, :, 0:64] = w_uk
    w_ukT = np.ascontiguousarray(w_uk.transpose(2, 1, 0))
    b_if = np.concatenate([np.asarray(inp["ml_b_i"][0], f32), np.asarray(inp["ml_b_f"][0], f32)])
    rope_all = _rope_tables(np.arange(SEQ))
    rope_smp = np.ascontiguousarray(np.tile(_rope_tables(SEQ + np.arange(8)), (16, 1)))
    tri = np.tril(np.ones((128, 128), f32))
    caus = np.ascontiguousarray(tri.T)
    tok = np.arange(128)
    same_b = (tok[:, None] // 8) == (tok[None, :] // 8)
    caus_s = (same_b & (tok[:, None] <= tok[None, :])).astype(f32)
    bmask = np.zeros((128, 16, 128), f32)
    rmask = np.zeros((128, 16), f32)
    nmask = np.zeros((128, 16, 8, 8), f32)
    for b in range(16):
        bmask[:, b, b * 8:(b + 1) * 8] = 1.0
        rmask[b * 8:(b + 1) * 8, b] = 1.0
        for sk in range(8):
            nmask[b * 8 + sk, b, :, sk:] = 1.0
    segmask = np.ones((128, 512), f32)
    segmask[:, 0::128] = 0.0
    seg8 = np.ones((4, 128), f32)
    seg8[:, 0::8] = 0.0
    common = {
        "w_ada": np.asarray(inp["w_ada"][0], f32), "b_adaT": b_adaT, "b_gate": b_gate,
        "w_all": w_all, "w_own": w_own, "rope_all": rope_all, "rope_smp": rope_smp,
        "ident": np.eye(128, dtype=f32), "tstrict": np.triu(np.ones((64, 64), f32), 1), "segmask": segmask,
        "caus": caus, "caus_s": caus_s, "bmask": bmask, "rmask": rmask,
        "nmask": np.ascontiguousarray(nmask.reshape(128, 16, 64)), "i4": np.eye(4, dtype=f32), "seg8": seg8,
        "b_if_bc": np.ascontiguousarray(np.tile(b_if[None, :], (64, 1))),
        "b_if_col": np.ascontiguousarray(b_if.reshape(2, 4).T),
        "gkv": np.asarray(inp["mla_kv_norm"][0], f32),
        "gqT": np.ascontiguousarray(np.asarray(inp["mla_q_norm"][0], f32).reshape(3, 128).T),
        "ml_gn": np.asarray(inp["ml_gn"][0], f32), "ln_g": np.asarray(inp["ln_g"][0], f32),
        "ln_b": np.asarray(inp["ln_b"][0], f32),
        "w_uq": np.ascontiguousarray(w_uq), "w_uqp": w_uqp, "w_ukp": w_ukp, "w_ukT": w_ukT,
        "w_uv": np.asarray(inp["mla_w_uv"][0], f32), "w_out": np.asarray(inp["w_out"][0], f32),
        "cache_ckv": np.asarray(inp["cache_ckv"][0], f32), "cache_kr": np.asarray(inp["cache_krope"][0], f32),
    }
    ropeT_smp = np.zeros((128, 2, 128), f32)
    ropeT_smp[0:32, 0, :] = rope_smp[:, 0:32].T
    ropeT_smp[0:32, 1, :] = rope_smp[:, 32:64].T
    ropeT_smp[64:96] = ropeT_smp[0:32]
    common["ropeT_smp"] = ropeT_smp
    per = []
    xp = np.asarray(inp["x_prompt"], f32)
    xs = np.asarray(inp["x_sample"], f32)
    for j in range(8):
        b, jj = j // 4, j % 4
        own = [4 * g + jj for g in range(NOWN)]
        xb = xp[b].reshape(NBLK, 128, D)
        ropeo = rope_all.reshape(NBLK, 128, 64)[own].reshape(NOWN * 128, 64)
        ropeT_own = np.zeros((128, 2, NOWN * 128), f32)
        ropeT_own[64:96, 0, :] = ropeo[:, 0:32].T
        ropeT_own[64:96, 1, :] = ropeo[:, 32:64].T
        maskA = np.zeros((128, 4, 128), f32)
        for i in range(4):
            if i < jj:
                maskA[:, i, :] = 1.0
            elif i == jj:
                maskA[:, i, :] = caus
        selv = np.zeros((128, 4), f32)
        selv[:, jj] = 1.0
        sb = slice(16 * j, 16 * j + 16)
        cT = np.concatenate([np.asarray(inp["c_prompt"], f32)[b][:, None], np.asarray(inp["c_sample"], f32)[sb].T], axis=1)
        pt = np.asarray(inp["page_table"])[sb].astype(np.int32)
        ptab = np.ascontiguousarray(np.concatenate([pt.T, pt.T], axis=0))
        d = dict(common)
        d.update({
            "x_all": np.ascontiguousarray(xp[b]), "x_own": np.ascontiguousarray(xb[own].reshape(NOWN * 128, D)),
            "x_smp": np.ascontiguousarray(xs[sb].reshape(128, D)), "cT": np.ascontiguousarray(cT),
            "rope_own": np.ascontiguousarray(ropeo), "ropeT_own": ropeT_own, "sel": selv, "maskA": maskA,
            "state_C": np.ascontiguousarray(np.asarray(inp["state_C"][0], f32)[sb]),
            "state_nT": np.ascontiguousarray(np.asarray(inp["state_n"][0], f32)[sb].transpose(2, 0, 1)),
            "state_mT": np.ascontiguousarray(np.asarray(inp["state_m"][0], f32)[sb].T),
            "ptab": ptab,
        })
        per.append(d)
    return per


_CACHE = {}


def kernel(**inp):
    if "nc" not in _CACHE:
        _CACHE["nc"] = build()
    nc, used = _CACHE["nc"]
    per = _host_inputs(inp)
    in_maps = [{k: d[k] for k in used} for d in per]
    res = run_bass_kernel_spmd(nc, in_maps, core_ids=list(range(8)))
    R = res.results
    f32 = np.float32
    y_p = np.zeros((2, SEQ, D), f32)
    ckv_p = np.zeros((1, 2, SEQ, 256), f32)
    kr_p = np.zeros((1, 2, SEQ, 32), f32)
    C_p = np.zeros((1, 2, 4, 128, 128), f32)
    n_p = np.zeros((1, 2, 4, 128), f32)
    m_p = np.zeros((1, 2, 4), f32)
    y_s = np.zeros((128, 8, D), f32)
    ckv_s = np.zeros((1, 128, 8, 256), f32)
    kr_s = np.zeros((1, 128, 8, 32), f32)
    C_s = np.zeros((1, 128, 4, 128, 128), f32)
    n_s = np.zeros((1, 128, 4, 128), f32)
    m_s = np.zeros((1, 128, 4), f32)
    for j in range(8):
        b, jj = j // 4, j % 4
        r = R[j]
        own = [4 * g + jj for g in range(NOWN)]
        y_p[b].reshape(NBLK, 128, D)[own] = r["o_y"].reshape(NOWN, 128, D)
        ckv_p[0, b].reshape(NBLK, 128, 256)[own] = r["o_ckv"].reshape(NOWN, 128, 256)
        kr_p[0, b].reshape(NBLK, 128, 32)[own] = r["o_kr"].reshape(NOWN, 128, 32)
        if jj == 0:
            oc = r["o_C"]
            C_p[0, b] = oc[:, :, 0:128].transpose(1, 0, 2)
            n_p[0, b] = oc[:, :, 128].T
            m_p[0, b] = r["o_m"][0]
        sb = slice(16 * j, 16 * j + 16)
        y_s[sb] = r["o_ys"].reshape(16, 8, D)
        ckv_s[0, sb] = r["o_ckvs"].reshape(16, 8, 256)
        kr_s[0, sb] = r["o_krs"].reshape(16, 8, 32)
        C_s[0, sb] = r["o_Cs"].transpose(0, 2, 1, 3)
        n_s[0, sb] = r["o_ns"].transpose(1, 2, 0)
        m_s[0, sb] = r["o_ms"].T
    return (y_p, y_s, ckv_p, kr_p, C_p, n_p, m_p, ckv_s, kr_s, C_s, n_s, m_s)
```

```python
import contextlib
import numpy as np
import concourse.bass as bass
import concourse.mybir as mybir
from concourse.bass_utils import run_bass_kernel_spmd

F32 = mybir.dt.float32
BF16 = mybir.dt.bfloat16
I32 = mybir.dt.int32
AF = mybir.ActivationFunctionType
ALU = mybir.AluOpType
AX = mybir.AxisListType

ENGS = ("pe", "act", "dve", "pool", "sp")

D = 1024
KC = 8
SEQ = 8192
NBLK = 64
NOWN = 16
DK = 128
EPS = 1e-6
ALPHA = 2.0 ** 0.25
MLA_SCALE = 96.0 ** -0.5
NPOOL = 10240
NEG = -1.0e30


class Sched:
    def __init__(self, nc, stack):
        self.nc = nc
        self.stack = stack
        self.sems = {}
        self.count = {}
        self.ops = {e: [] for e in ENGS}
        self.waited = {e: {} for e in ENGS}
        self.writer = {}
        self.readers = {}
        self.tagmap = {}
        self.maxtags = 0

    def _sem(self, key):
        if key not in self.sems:
            self.sems[key] = self.stack.enter_context(self.nc.semaphore("s_" + key.replace(":", "_")))
            self.count[key] = 0
        return self.sems[key]

    def _deps(self, reads, writes):
        deps = {}

        def add(tok):
            if tok is None:
                return
            k, v = tok
            if deps.get(k, 0) < v:
                deps[k] = v

        for r in reads:
            add(self.writer.get(r))
        for w in writes:
            add(self.writer.get(w))
            for t in self.readers.get(w, ()):
                add(t)
        return deps

    def _commit(self, tok, reads, writes):
        for r in reads:
            self.readers.setdefault(r, []).append(tok)
        for w in writes:
            self.writer[w] = tok
            self.readers[w] = []

    def _waits(self, q, deps, n=None):
        waits = []
        for k, v in deps.items():
            if n is not None and k == q:
                if q == "pe" or v < n - 2:
                    continue
            if self.waited[q].get(k, 0) >= v:
                continue
            self.waited[q][k] = v
            waits.append((k, v))
        return waits

    def op(self, eng, fn, reads=(), writes=()):
        self._sem(eng)
        deps = self._deps(reads, writes)
        n = self.count[eng] + 1
        waits = self._waits(eng, deps, n)
        self.count[eng] = n
        self.ops[eng].append((waits, fn, (eng, 1)))
        self._commit((eng, n), reads, writes)

    def raw(self, q, tag, fn, reads=(), writes=(), nowaw=False):
        if tag not in self.tagmap:
            self.tagmap[tag] = "t%d" % len(self.tagmap)
            self.maxtags = max(self.maxtags, len(self.tagmap))
        key = "dma:" + self.tagmap[tag]
        self._sem(key)
        deps = self._deps(reads, writes)
        if nowaw:
            deps.pop(key, None)
        waits = self._waits(q, deps)
        self.count[key] += 16
        tok = (key, self.count[key])
        self.ops[q].append((waits, fn, (key, 16)))
        self._commit(tok, reads, writes)

    def dma(self, q, tag, out, in_, reads=(), writes=(), nowaw=False, **kw):
        self.raw(q, tag, lambda e, out=out, in_=in_, kw=kw: e.dma_start(out=out, in_=in_, **kw), reads, writes, nowaw)

    def barrier(self):
        for e in ENGS:
            waits = []
            for k, v in self.count.items():
                if v == 0 or k == e:
                    continue
                if self.waited[e].get(k, 0) >= v:
                    continue
                self.waited[e][k] = v
                waits.append((k, v))
            if waits:
                self.ops[e].append((waits, None, None))

    def emit(self):
        self.barrier()
        nc = self.nc
        ops = self.ops
        sems = self.sems
        self.ops = {e: [] for e in ENGS}
        self.tagmap = {}

        def run(e, lst):
            for waits, fn, inc in lst:
                for k, v in waits:
                    e.wait_ge(sems[k], v)
                if fn is not None:
                    fn(e).then_inc(sems[inc[0]], inc[1])

        with nc.Block() as block:
            @block.tensor
            def _(e):
                run(e, ops["pe"])

            @block.scalar
            def _(e):
                run(e, ops["act"])

            @block.vector
            def _(e):
                run(e, ops["dve"])

            @block.gpsimd
            def _(e):
                run(e, ops["pool"])

            @block.sync
            def _(e):
                run(e, ops["sp"])


IN_SPECS = {
    "x_all": ([SEQ, D], F32), "x_own": ([NOWN * 128, D], F32), "x_smp": ([128, D], F32),
    "cT": ([D, 17], F32), "w_ada": ([D, 3 * D], F32), "b_adaT": ([128, 24], F32), "b_gate": ([D], F32),
    "w_all": ([D, 1352], F32), "w_own": ([D, 2944], F32),
    "rope_all": ([SEQ, 64], F32), "rope_own": ([NOWN * 128, 64], F32), "rope_smp": ([128, 64], F32),
    "ropeT_own": ([128, 2, NOWN * 128], F32), "ropeT_smp": ([128, 2, 128], F32),
    "ident": ([128, 128], F32), "tstrict": ([64, 64], F32), "segmask": ([128, 512], F32),
    "sel": ([128, 4], F32), "maskA": ([128, 4, 128], F32), "caus": ([128, 128], F32),
    "caus_s": ([128, 128], F32), "bmask": ([128, 16, 128], F32), "rmask": ([128, 16], F32),
    "nmask": ([128, 16, 64], F32), "i4": ([4, 4], F32), "seg8": ([4, 128], F32),
    "b_if_bc": ([64, 8], F32), "b_if_col": ([4, 2], F32), "gkv": ([256], F32), "gqT": ([128, 3], F32),
    "ml_gn": ([512], F32), "ln_g": ([D], F32), "ln_b": ([D], F32),
    "w_uq": ([384, 8, 96], F32), "w_uqp": ([384, 8, 96], F32), "w_ukp": ([256, 8, 96], F32),
    "w_ukT": ([64, 8, 256], F32), "w_uv": ([256, 512], F32), "w_out": ([D, D], F32),
    "state_C": ([16, 4, 128, 128], F32), "state_nT": ([128, 16, 4], F32), "state_mT": ([4, 16], F32),
    "ptab": ([128, 16], I32), "cache_ckv": ([NPOOL, 128, 256], F32), "cache_kr": ([NPOOL, 128, 32], F32),
}
OUT_SPECS = {
    "o_y": ([NOWN * 128, D], F32), "o_ckv": ([NOWN * 128, 256], F32), "o_kr": ([NOWN * 128, 32], F32),
    "o_C": ([128, 4, 129], F32), "o_m": ([1, 4], F32),
    "o_ys": ([128, D], F32), "o_ckvs": ([128, 256], F32), "o_krs": ([128, 32], F32),
    "o_Cs": ([16, 128, 4, 128], F32), "o_ns": ([128, 16, 4], F32), "o_ms": ([4, 16], F32),
}

STAGE = 99


def build(stage=STAGE):
    nc = bass.Bass("TRN2", target_bir_lowering=False)
    dr = {}
    used_in = set()
    for k, (shp, dt) in IN_SPECS.items():
        dr[k] = (k, shp, dt)
    dram_cache = {}

    def DIN(name):
        if name not in dram_cache:
            _, shp, dt = dr[name]
            dram_cache[name] = nc.dram_tensor(name, shp, dt, kind="ExternalInput").ap()
            used_in.add(name)
        return dram_cache[name]

    outs = {k: nc.dram_tensor(k, shp, dt, kind="ExternalOutput").ap() for k, (shp, dt) in OUT_SPECS.items()}
    kscr = nc.dram_tensor("kscr", [NBLK, 128, 512], BF16, kind="Internal").ap()
    vscr = nc.dram_tensor("vscr", [NBLK, 128, 4, 129], BF16, kind="Internal").ap()

    with contextlib.ExitStack() as st:
        S = Sched(nc, st)
        op, dma = S.op, S.dma

        def T(stack, name, shape, dt):
            return stack.enter_context(nc.sbuf_tensor("sb_" + name, shape, dt))

        PS = [st.enter_context(nc.psum_tensor(f"ps{i}", [128, 512], F32)) for i in range(8)]

        def psf(i):
            return PS[i][:]

        def psb(i):
            return PS[i][:].bitcast(BF16)

        def bank(i):
            return f"ps{i}"

        def mm(out, lhsT, rhs, start, stop, reads, writes):
            op("pe", lambda e: e.matmul(out, lhsT, rhs, start=start, stop=stop, skip_group_check=True), reads, writes)

        def tr(out, in_, idn, reads, writes):
            op("pe", lambda e: e.transpose(out, in_, idn), reads, writes)

        def act(out, in_, func, reads, writes, scale=1.0, bias=0.0):
            op("act", lambda e: e.activation(out, in_, func, bias=bias, scale=scale), reads, writes)

        def tt(eng, out, in0, in1, alu, reads, writes):
            op(eng, lambda e: e.tensor_tensor(out, in0, in1, alu), reads, writes)

        def ts(eng, out, in0, s1, s2, op0, op1, reads, writes):
            if s2 is None:
                op(eng, lambda e: e.tensor_scalar(out, in0, s1, None, op0), reads, writes)
            else:
                op(eng, lambda e: e.tensor_scalar(out, in0, s1, s2, op0, op1), reads, writes)

        def stt(out, in0, sc, in1, op0, op1, reads, writes):
            op("dve", lambda e: e.scalar_tensor_tensor(out, in0, sc, in1, op0, op1), reads, writes)

        def cp(eng, out, in_, reads, writes):
            if eng == "act":
                op("act", lambda e: e.activation(out, in_, AF.Identity), reads, writes)
            else:
                op(eng, lambda e: e.tensor_copy(out, in_), reads, writes)

        def red(out, in_, alu, reads, writes, axis=AX.X):
            op("dve", lambda e: e.tensor_reduce(out, in_, axis, alu), reads, writes)

        def rsqrt_pool(out, in_, mhalf, reads, writes):
            op("act", lambda e: e.activation(out, in_, AF.Ln, bias=0.0, scale=1.0), reads, writes)
            op("act", lambda e: e.activation(out, out, AF.Exp, bias=0.0, scale=-0.5), list(writes), writes)

        def cast_w(stack, name, src, kc, n, tagq="pool"):
            t = T(stack, name, [128, kc, n], BF16)
            v = src.rearrange("(k p) n -> p k n", p=128)
            c0 = 0
            while c0 < n:
                c1 = min(n, c0 + 2048)
                dma("pool", name, t[:, :, c0:c1], v[:, :, c0:c1], writes=[name], nowaw=True)
                c0 = c1
            return t

        ident = T(st, "ident", [128, 128], BF16)
        identf = T(st, "identf", [128, 128], F32)
        onesb = T(st, "onesb", [128, 128], BF16)
        onesf = T(st, "onesf", [128, 128], F32)
        mhalf = T(st, "mhalf", [128, 8], F32)
        mod = T(st, "mod", [128, 16, 17], F32)
        gate_p = T(st, "gate_p", [128, D], F32)
        gate_s = T(st, "gate_s", [128, D], F32)
        gkv_bc = T(st, "gkv_bc", [128, 256], F32)
        sel = T(st, "sel", [128, 4], F32)

        dma("pool", "c_ident", ident[:], DIN("ident"), writes=["ident"])
        dma("sp", "c_identf", identf[:], DIN("ident"), writes=["identf"])
        op("pool", lambda e: e.memset(onesb[:], 1.0), writes=["onesb"])
        op("pool", lambda e: e.memset(onesf[:], 1.0), writes=["onesf"])
        op("pool", lambda e: e.memset(mhalf[:], -0.5), writes=["mhalf"])
        dma("sp", "c_gkv", gkv_bc[:], DIN("gkv").partition_broadcast(128), writes=["gkv_bc"])
        dma("sp", "c_sel", sel[:], DIN("sel"), writes=["sel"])

        with contextlib.ExitStack() as ph:
            cT_bf = T(ph, "cT_bf", [128, KC, 17], BF16)
            dma("pool", "cT", cT_bf[:], DIN("cT").rearrange("(k p) n -> p k n", p=128), writes=["cT_bf"])
            badaT = T(ph, "badaT", [128, 24], F32)
            dma("sp", "badaT", badaT[:], DIN("b_adaT"), writes=["badaT"])
            bgate = T(ph, "bgate", [128, D], F32)
            dma("sp", "bgate", bgate[:], DIN("b_gate").partition_broadcast(128), writes=["bgate"])
            crep_p = T(ph, "crep_p", [128, KC, 128], BF16)
            crep_s = T(ph, "crep_s", [128, KC, 128], BF16)
            cp("dve", crep_p[:], cT_bf[:, :, 0:1].to_broadcast([128, KC, 128]), ["cT_bf"], ["crep_p"])
            cp("dve", crep_s[:].rearrange("p k (b s) -> p k b s", s=8),
               cT_bf[:, :, 1:17].unsqueeze(3).to_broadcast([128, KC, 16, 8]), ["cT_bf"], ["crep_s"])
            wa = [T(ph, f"wa{i}", [128, KC, 1024], BF16) for i in range(2)]
            wav = DIN("w_ada").rearrange("(k p) n -> p k n", p=128)
            for piece in range(3):
                w = wa[piece % 2]
                wn = f"wa{piece % 2}"
                dma("pool", wn, w[:], wav[:, :, piece * 1024:(piece + 1) * 1024], writes=[wn])
                if piece < 2:
                    for nch in range(8):
                        col = (piece * 8 + nch) * 17
                        for kc in range(KC):
                            mm(psf(0)[:, col:col + 17], w[:, kc, nch * 128:(nch + 1) * 128], cT_bf[:, kc, :],
                               kc == 0, kc == KC - 1, [wn, "cT_bf"], [bank(0)])
                else:
                    for gi, (crep, gdst, gname) in enumerate(((crep_p, gate_p, "gate_p"), (crep_s, gate_s, "gate_s"))):
                        for half in range(2):
                            b = 1 + gi * 2 + half
                            for kc in range(KC):
                                mm(psf(b), crep[:, kc, :], w[:, kc, half * 512:(half + 1) * 512],
                                   kc == 0, kc == KC - 1, [wn, "crep_p", "crep_s"], [bank(b)])
                            tt("dve", gdst[:, half * 512:(half + 1) * 512], psf(b), bgate[:, half * 512:(half + 1) * 512],
                               ALU.add, [bank(b), "bgate"], [gname])
            tt("dve", mod[:], psf(0)[:, 0:272].rearrange("p (c n) -> p c n", n=17),
               badaT[:, 0:16].unsqueeze(2).to_broadcast([128, 16, 17]), ALU.add, [bank(0), "badaT"], ["mod"])
            ts("dve", mod[:, 8:16, :], mod[:, 8:16, :], 1.0, None, ALU.add, None, ["mod"], ["mod"])
            S.emit()

        def make_hT(xbt, xbn, hTt, hTn, pbank, sample):
            for kc in range(KC):
                tr(psb(pbank)[:, kc * 128:(kc + 1) * 128], xbt[:, kc * 128:(kc + 1) * 128], ident[:],
                   [xbn, "ident"], [bank(pbank)])
            if not sample:
                tt("dve", hTt[:], psb(pbank).rearrange("p (k t) -> p k t", t=128),
                   mod[:, 8:16, 0:1].to_broadcast([128, KC, 128]), ALU.mult, [bank(pbank), "mod"], [hTn])
                tt("dve", hTt[:], hTt[:], mod[:, 0:8, 0:1].to_broadcast([128, KC, 128]), ALU.add,
                   [hTn, "mod"], [hTn])
            else:
                v4 = lambda a: a.rearrange("p k (b s) -> p k b s", s=8)
                tt("dve", v4(hTt[:]), v4(psb(pbank).rearrange("p (k t) -> p k t", t=128)),
                   mod[:, 8:16, 1:17].unsqueeze(3).to_broadcast([128, KC, 16, 8]), ALU.mult,
                   [bank(pbank), "mod"], [hTn])
                tt("pool", v4(hTt[:]), v4(hTt[:]),
                   mod[:, 0:8, 1:17].unsqueeze(3).to_broadcast([128, KC, 16, 8]), ALU.add,
                   [hTn, "mod"], [hTn])

        def tokgroup(pb, hTt, hTn, W, Wn, c0, n):
            for kc in range(KC):
                mm(psf(pb)[:, 0:n], hTt[:, kc, :], W[:, kc, c0:c0 + n], kc == 0, kc == KC - 1,
                   [hTn, Wn], [bank(pb)])

        def featgroup(pb, col, hTt, hTn, W, Wn, c0, m):
            for kc in range(KC):
                mm(psf(pb)[0:m, col:col + 128], W[:, kc, c0:c0 + m], hTt[:, kc, :], kc == 0, kc == KC - 1,
                   [hTn, Wn], [bank(pb)])

        def misc_post(pb, ropet, ropen, ckvn, ckvnn, krr, krrn, tmp):
            sq, ss, rstd, t1, t2 = tmp
            act(sq[:, 0:256], psf(pb)[:, 0:256], AF.Square, [bank(pb)], ["m_sq"])
            red(ss[:, 0:1], sq[:, 0:256], ALU.add, ["m_sq"], ["m_ss"])
            ts("dve", ss[:, 1:2], ss[:, 0:1], 1.0 / 256.0, EPS, ALU.mult, ALU.add, ["m_ss"], ["m_ss2"])
            rsqrt_pool(rstd[:, 0:1], ss[:, 1:2], mhalf[:, 0:1], ["m_ss2", "mhalf"], ["m_rstd"])
            stt(ckvn[:], psf(pb)[:, 0:256], rstd[:, 0:1], gkv_bc[:], ALU.mult, ALU.mult,
                [bank(pb), "m_rstd", "gkv_bc"], [ckvnn])
            tt("dve", t1[:], psf(pb)[:, 256:288], ropet[:, 0:32], ALU.mult, [bank(pb), ropen], ["m_t1"])
            tt("dve", t2[:], psf(pb)[:, 288:320], ropet[:, 32:64], ALU.mult, [bank(pb), ropen], ["m_t2"])
            tt("pool", krr[:], t1[:], t2[:], ALU.add, ["m_t1", "m_t2"], [krrn])

        def misc_tmp(stack, pfx):
            return (T(stack, pfx + "sq", [128, 256], F32), T(stack, pfx + "ss", [128, 2], F32),
                    T(stack, pfx + "rstd", [128, 1], F32), T(stack, pfx + "t1", [128, 32], F32),
                    T(stack, pfx + "t2", [128, 32], F32))

        def mlstm_intra(rn, tp, qT, kT, vaug, wlc, ec, maskT, x2_list, tmp):
            st_, xs, dn, hm = tmp
            for h in range(4):
                mm(psf(3)[:, h * 128:(h + 1) * 128], kT[:, h, :], qT[:, h, :], True, True,
                   [rn["kT"], rn["qT"]], [bank(3)])
            for h in range(4):
                stt(st_[:, h, :], psf(3)[:, h * 128:(h + 1) * 128], wlc[:, h:h + 1], maskT, ALU.mult, ALU.mult,
                    [bank(3), rn["wl"], rn["mask"]], [tp + "st"])
            for h in range(4):
                col = slice((h % 2) * 129, (h % 2) * 129 + 129)
                lst = x2_list(h)
                mm(psf(4 + h // 2)[:, col], st_[:, h, :], vaug[:, h, :], True, False,
                   [tp + "st", rn["vaug"]], [bank(4 + h // 2)])
                for n, (lh, rh, rd) in enumerate(lst):
                    mm(psf(4 + h // 2)[:, col], lh, rh, False, n == len(lst) - 1, rd, [bank(4 + h // 2)])
            for hb in range(2):
                cp("act", xs[:, 2 * hb:2 * hb + 2, :], psf(4 + hb)[:, 0:258].rearrange("p (h v) -> p h v", v=129),
                   [bank(4 + hb)], [tp + "xs"])
            act(dn[:, 0:4], xs[:, :, 128], AF.Abs, [tp + "xs"], [tp + "dn"])
            tt("dve", dn[:, 0:4], dn[:, 0:4], ec, ALU.max, [tp + "dn", rn["e"]], [tp + "dn"])
            op("dve", lambda e: e.reciprocal(dn[:, 4:8], dn[:, 0:4]), [tp + "dn"], [tp + "dn2"])
            tt("dve", hm[:], xs[:, :, 0:128], dn[:, 4:8].unsqueeze(2).to_broadcast([128, 4, 128]), ALU.mult,
               [tp + "xs", tp + "dn2"], [tp + "hm"])
            return hm

        def intra_tmp(stack, pfx):
            return (T(stack, pfx + "st", [128, 4, 128], BF16), T(stack, pfx + "xs", [128, 4, 129], F32),
                    T(stack, pfx + "dn", [128, 8], F32), T(stack, pfx + "hm", [128, 4, 128], F32))

        def ml_post(rn, tp, hm, sig_mo, silu_mz, gn_bc, ydst, yname, tmp):
            sqh, st4 = tmp
            v3 = lambda a: a.rearrange("p (h v) -> p h v", v=128)
            tt("dve", hm[:], hm[:], v3(sig_mo), ALU.mult, [tp + "hm", rn["sig_mo"]], [tp + "hm"])
            red(st4[:, 0:4], hm[:], ALU.add, [tp + "hm"], [tp + "g_s1"])
            act(sqh[:], hm[:], AF.Square, [tp + "hm"], [tp + "sqh"])
            red(st4[:, 4:8], sqh[:], ALU.add, [tp + "sqh"], [tp + "g_s2"])
            ts("dve", st4[:, 8:12], st4[:, 0:4], 1.0 / 128.0, None, ALU.mult, None, [tp + "g_s1"], [tp + "g_mean"])
            tt("dve", st4[:, 12:16], st4[:, 8:12], st4[:, 8:12], ALU.mult, [tp + "g_mean"], [tp + "g_msq"])
            stt(st4[:, 16:20], st4[:, 4:8], 1.0 / 128.0, st4[:, 12:16], ALU.mult, ALU.subtract,
                [tp + "g_s2", tp + "g_msq"], [tp + "g_var"])
            ts("dve", st4[:, 16:20], st4[:, 16:20], EPS, None, ALU.add, None, [tp + "g_var"], [tp + "g_var"])
            rsqrt_pool(st4[:, 20:24], st4[:, 16:20], None, [tp + "g_var"], [tp + "g_rstd"])
            tt("dve", hm[:], hm[:], st4[:, 8:12].unsqueeze(2).to_broadcast([128, 4, 128]), ALU.subtract,
               [tp + "hm", tp + "g_mean"], [tp + "hm"])
            tt("dve", hm[:], hm[:], st4[:, 20:24].unsqueeze(2).to_broadcast([128, 4, 128]), ALU.mult,
               [tp + "hm", tp + "g_rstd"], [tp + "hm"])
            tt("dve", hm[:], hm[:], v3(gn_bc), ALU.mult, [tp + "hm", "gn_bc"], [tp + "hm"])
            tt("dve", v3(ydst), hm[:], v3(silu_mz), ALU.mult, [tp + "hm", rn["silu_mz"]], [yname])

        def ln_steps(ipfx, pfx, pbs, xf, gate, lng, lnb, odst, oname, tmp):
            z, sqz, st1 = tmp
            for half in range(2):
                hs = slice(half * 512, (half + 1) * 512)
                tt("dve", z[:, hs], psf(pbs[half]), gate[:, hs], ALU.mult, [bank(pbs[half]), "gate"], [pfx + "z"])
            yield
            stt(z[:], xf, ALPHA, z[:], ALU.mult, ALU.add, [ipfx + "xf", pfx + "z"], [pfx + "z"])
            red(st1[:, 0:1], z[:], ALU.add, [pfx + "z"], [pfx + "l_s1"])
            act(sqz[:], z[:], AF.Square, [pfx + "z"], [pfx + "sqz"])
            red(st1[:, 1:2], sqz[:], ALU.add, [pfx + "sqz"], [pfx + "l_s2"])
            ts("dve", st1[:, 2:3], st1[:, 0:1], 1.0 / D, None, ALU.mult, None, [pfx + "l_s1"], [pfx + "l_mean"])
            tt("dve", st1[:, 3:4], st1[:, 2:3], st1[:, 2:3], ALU.mult, [pfx + "l_mean"], [pfx + "l_msq"])
            stt(st1[:, 4:5], st1[:, 1:2], 1.0 / D, st1[:, 3:4], ALU.mult, ALU.subtract,
                [pfx + "l_s2", pfx + "l_msq"], [pfx + "l_var"])
            ts("dve", st1[:, 4:5], st1[:, 4:5], EPS, None, ALU.add, None, [pfx + "l_var"], [pfx + "l_var"])
            rsqrt_pool(st1[:, 5:6], st1[:, 4:5], None, [pfx + "l_var"], [pfx + "l_rstd"])
            stt(z[:], z[:], st1[:, 2:3], lng[:], ALU.subtract, ALU.mult, [pfx + "z", pfx + "l_mean", "lng"], [pfx + "z"])
            stt(z[:], z[:], st1[:, 5:6], lnb[:], ALU.mult, ALU.add, [pfx + "z", pfx + "l_rstd", "lnb"], [pfx + "z"])
            dma("sp", pfx + "yout", odst, z[:], reads=[pfx + "z"], writes=[oname])
            yield

        def ln_out(*args):
            for _ in ln_steps(*args):
                pass

        if stage >= 3:
          sst = contextlib.ExitStack()
          ys_ml = T(sst, "ys_ml", [128, 512], BF16)
          silu_azT = T(sst, "silu_azT", [128, 8, 128], BF16)
          qlatT = T(sst, "qlatT", [128, 2, 8, 128], BF16)
          qropeT = T(sst, "qropeT", [128, 8, 128], BF16)
          ckvnb_s = T(sst, "ckvnb_s", [128, 256], BF16)
          ckvnT_s = T(sst, "ckvnT_s", [128, 2, 128], BF16)
          krT_s = T(sst, "krT_s", [128, 128], BF16)
          xf_s = T(sst, "xf_s", [128, D], F32)
          qr4 = T(sst, "qr4", [128, 4, 8, 128], BF16)
          op("pool", lambda e: e.memset(qr4[:], 0.0), writes=["qr4"])
          op("pool", lambda e: e.memset(krT_s[:], 0.0), writes=["krT_s"])
          qT = T(sst, "s_qT", [128, 4, 128], BF16)
          kT = T(sst, "s_kT", [128, 4, 128], BF16)
          ktok = T(sst, "s_ktok", [128, 4, 128], BF16)
          vaug = T(sst, "s_vaug", [128, 4, 129], BF16)
          gs = T(sst, "s_gs", [128, 8], F32)
          sig_mo = T(sst, "s_sig_mo", [128, 512], BF16)
          silu_mz = T(sst, "s_silu_mz", [128, 512], BF16)
          with contextlib.ExitStack() as ph:
            WOWN = DIN("w_own")
            Wa = cast_w(ph, "sWa", DIN("w_all"), KC, 1352)
            Wq5 = cast_w(ph, "sWq5", WOWN[:, 0:1024], KC, 1024)
            Wcq = cast_w(ph, "sWcq", WOWN[:, 1024:1408], KC, 384)
            Wg5 = cast_w(ph, "sWg5", WOWN[:, 1408:2944], KC, 1536)
            gqT = T(ph, "s_gqT", [128, 3], F32)
            dma("sp", "s_gqT", gqT[:], DIN("gqT"), writes=["s_gqT"])
            wq_st = T(ph, "s_wq_st", [128, 3, 768], F32)
            wuq = T(ph, "s_wuq", [128, 3, 768], BF16)
            wuqp = T(ph, "s_wuqp", [128, 3, 768], BF16)
            for nm, dst, dn in (("w_uq", wuq, "s_wuq"), ("w_uqp", wuqp, "s_wuqp")):
                dma("sp", "s_wq_st", wq_st[:], DIN(nm).rearrange("(c p) h n -> p c (h n)", p=128), writes=["s_wq_st"])
                for cc in range(3):
                    ts("dve", dst[:, cc, :], wq_st[:, cc, :], gqT[:, cc:cc + 1], None, ALU.mult, None,
                       ["s_wq_st", "s_gqT"], [dn])
            wukT = T(ph, "s_wukT", [64, 8, 256], BF16)
            dma("pool", "s_wukT", wukT[:], DIN("w_ukT"), writes=["s_wukT"])
            mhq = T(ph, "s_mhq", [128, 128], F32)
            op("pool", lambda e: e.memset(mhq[:], -0.5), writes=["s_mhq"])
            xb = T(ph, "s_xb", [128, D], BF16)
            hT = T(ph, "s_hT", [128, KC, 128], BF16)
            rope = T(ph, "s_rope", [128, 64], F32)
            ropeT = T(ph, "s_ropeT", [128, 2, 128], F32)
            ckvn = T(ph, "s_ckvn", [128, 256], F32)
            krr = T(ph, "s_krr", [128, 32], F32)
            krb = T(ph, "s_krb", [128, 32], BF16)
            sqc = T(ph, "s_sqc", [128, 3, 128], BF16)
            rq = T(ph, "s_rq", [128, 128], F32)
            rq2 = T(ph, "s_rq2", [128, 128], F32)
            cqn = T(ph, "s_cqn", [128, 3, 128], BF16)
            qnT = T(ph, "s_qnT", [64, 8, 128], BF16)
            tq1 = T(ph, "s_tq1", [32, 8, 128], F32)
            tq2 = T(ph, "s_tq2", [32, 8, 128], F32)
            mt = misc_tmp(ph, "sp")
            op("pool", lambda e: e.memset(vaug[:], 1.0), writes=["s_vaug"])
            dma("pool", "s_xb", xb[:], DIN("x_smp"), writes=["s_xb"])
            dma("sp", "s_xf", xf_s[:], DIN("x_smp"), writes=["s_xf"])
            dma("sp", "s_rope", rope[:], DIN("rope_smp"), writes=["s_rope"])
            dma("sp", "s_ropeT", ropeT[:], DIN("ropeT_smp"), writes=["s_ropeT"])
            make_hT(xb, "s_xb", hT, "s_hT", 0, True)
            v3 = lambda a: a.rearrange("p (h t) -> p h t", t=128)
            for h in range(4):
                featgroup(2, h * 128, hT, "s_hT", Wq5, "sWq5", h * 128, 128)
            for h in range(4):
                featgroup(3, h * 128, hT, "s_hT", Wq5, "sWq5", 512 + h * 128, 128)
            cp("act", qT[:].rearrange("p h t -> p (h t)"), psf(2), [bank(2)], ["s_qT"])
            act(kT[:].rearrange("p h t -> p (h t)"), psf(3), AF.Identity, [bank(3)], ["s_kT"], scale=DK ** -0.5)
            tokgroup(4, hT, "s_hT", Wa, "sWa", 0, 512)
            act(ktok[:].rearrange("p h t -> p (h t)"), psf(4), AF.Identity, [bank(4)], ["s_ktok"], scale=DK ** -0.5)
            tokgroup(5, hT, "s_hT", Wa, "sWa", 512, 512)
            cp("dve", vaug[:, :, 0:128], psf(5).rearrange("p (h v) -> p h v", v=128), [bank(5)], ["s_vaug"])
            tokgroup(6, hT, "s_hT", Wa, "sWa", 1024, 328)
            misc_post(6, rope, "s_rope", ckvn, "s_ckvn", krr, "s_krr", mt)
            cp("act", gs[:], psf(6)[:, 320:328], [bank(6)], ["s_gs"])
            dma("sp", "o_ckvs", outs["o_ckvs"], ckvn[:], reads=["s_ckvn"], writes=["out_ckvs"])
            dma("sp", "o_krs", outs["o_krs"], krr[:], reads=["s_krr"], writes=["out_krs"])
            cp("pool", ckvnb_s[:], ckvn[:], ["s_ckvn"], ["ckvnb_s"])
            cp("pool", krb[:], krr[:], ["s_krr"], ["s_krb"])
            for cc in range(2):
                tr(psb(7)[:, cc * 128:(cc + 1) * 128], ckvnb_s[:, cc * 128:(cc + 1) * 128], ident[:],
                   ["ckvnb_s", "ident"], [bank(7)])
            tr(psb(7)[0:32, 256:384], krb[:], ident[:], ["s_krb", "ident"], [bank(7)])
            cp("act", ckvnT_s[:].rearrange("p c t -> p (c t)"), psb(7)[:, 0:256], [bank(7)], ["ckvnT_s"])
            cp("act", krT_s[0:32, :], psb(7)[0:32, 256:384], [bank(7)], ["krT_s"])
            tokgroup(2, hT, "s_hT", Wg5, "sWg5", 0, 512)
            act(sig_mo[:], psf(2), AF.Sigmoid, [bank(2)], ["s_sig_mo"])
            tokgroup(3, hT, "s_hT", Wg5, "sWg5", 512, 512)
            act(silu_mz[:], psf(3), AF.Silu, [bank(3)], ["s_silu_mz"])
            for h in range(8):
                featgroup(4 + h // 4, (h % 4) * 128, hT, "s_hT", Wg5, "sWg5", 1024 + h * 64, 64)
            for hb in range(2):
                act(silu_azT[0:64, 4 * hb:4 * hb + 4, :], v3(psf(4 + hb)[0:64, :]), AF.Silu, [bank(4 + hb)], ["silu_azT"])
            for cc in range(3):
                featgroup(6, cc * 128, hT, "s_hT", Wcq, "sWcq", cc * 128, 128)
            act(sqc[:], psf(6)[:, 0:384].rearrange("p (c t) -> p c t", t=128), AF.Square, [bank(6)], ["s_sqc"])
            for cc in range(3):
                mm(psf(7)[:, 0:128], onesb[:], sqc[:, cc, :], cc == 0, cc == 2, ["onesb", "s_sqc"], [bank(7)])
            ts("dve", rq[:], psf(7)[:, 0:128], 1.0 / 384.0, EPS, ALU.mult, ALU.add, [bank(7)], ["s_rq"])
            rsqrt_pool(rq2[:], rq[:], mhq[:], ["s_rq", "s_mhq"], ["s_rq2"])
            tt("dve", cqn[:], psf(6)[:, 0:384].rearrange("p (c t) -> p c t", t=128),
               rq2[:].unsqueeze(1).to_broadcast([128, 3, 128]), ALU.mult, [bank(6), "s_rq2"], ["s_cqn"])
            for h in range(8):
                col = slice((h % 4) * 128, (h % 4 + 1) * 128)
                for cc in range(3):
                    mm(psf(2 + h // 4)[0:64, col], wuq[:, cc, h * 96:h * 96 + 64], cqn[:, cc, :], cc == 0, cc == 2,
                       ["s_wuq", "s_cqn"], [bank(2 + h // 4)])
                for cc in range(3):
                    mm(psf(4 + h // 4)[0:32, col], wuq[:, cc, h * 96 + 64:h * 96 + 96], cqn[:, cc, :], cc == 0, cc == 2,
                       ["s_wuq", "s_cqn"], [bank(4 + h // 4)])
                for cc in range(3):
                    mm(psf(6 + h // 4)[0:32, col], wuqp[:, cc, h * 96 + 64:h * 96 + 96], cqn[:, cc, :], cc == 0, cc == 2,
                       ["s_wuqp", "s_cqn"], [bank(6 + h // 4)])
            for hb in range(2):
                hs = slice(4 * hb, 4 * hb + 4)
                cp("act", qnT[:, hs, :], v3(psf(2 + hb)[0:64, :]), [bank(2 + hb)], ["s_qnT"])
                tt("dve", tq1[:, hs, :], v3(psf(4 + hb)[0:32, :]), ropeT[0:32, 0:1, :].to_broadcast([32, 4, 128]), ALU.mult,
                   [bank(4 + hb), "s_ropeT"], ["s_tq1"])
                tt("dve", tq2[:, hs, :], v3(psf(6 + hb)[0:32, :]), ropeT[0:32, 1:2, :].to_broadcast([32, 4, 128]), ALU.mult,
                   [bank(6 + hb), "s_ropeT"], ["s_tq2"])
            tt("pool", qropeT[0:32, :, :], tq1[:], tq2[:], ALU.add, ["s_tq1", "s_tq2"], ["qropeT"])
            for i4 in range(4):
                dma("sp", "qr4", qr4[i4 * 32:(i4 + 1) * 32, i4, :, :], qropeT[0:32, :, :], reads=["qropeT", "qr4"], writes=["qr4"])
            for cc in range(2):
                for h in range(8):
                    pb = 2 + cc * 2 + h // 4
                    mm(psf(pb)[:, (h % 4) * 128:(h % 4 + 1) * 128], wukT[:, h, cc * 128:(cc + 1) * 128], qnT[:, h, :],
                       True, True, ["s_wukT", "s_qnT"], [bank(pb)])
                for hb in range(2):
                    cp("act" if hb == 0 else "dve", qlatT[:, cc, 4 * hb:4 * hb + 4, :], v3(psf(2 + cc * 2 + hb)),
                       [bank(2 + cc * 2 + hb)], ["qlatT"])
            S.emit()
          with contextlib.ExitStack() as ph:
            gn_bc = T(ph, "s_gn_bc", [128, 512], F32)
            dma("sp", "s_gn_bc", gn_bc[:], DIN("ml_gn").partition_broadcast(128), writes=["gn_bc"])
            caus_s = T(ph, "s_caus", [128, 128], F32)
            dma("sp", "s_caus", caus_s[:], DIN("caus_s"), writes=["s_caus"])
            bmask = T(ph, "s_bmask", [128, 16, 128], BF16)
            dma("pool", "s_bmask", bmask[:], DIN("bmask"), writes=["s_bmask"])
            rmask = T(ph, "s_rmask", [128, 16], F32)
            dma("sp", "s_rmask", rmask[:], DIN("rmask"), writes=["s_rmask"])
            seg8 = T(ph, "s_seg8", [4, 128], F32)
            dma("sp", "s_seg8", seg8[:], DIN("seg8"), writes=["s_seg8"])
            bcol = T(ph, "s_bcol", [4, 2], F32)
            dma("sp", "s_bcol", bcol[:], DIN("b_if_col"), writes=["s_bcol"])
            i4t = T(ph, "s_i4", [4, 4], F32)
            dma("sp", "s_i4", i4t[:], DIN("i4"), writes=["s_i4"])
            m0T = T(ph, "s_m0T", [4, 16], F32)
            dma("sp", "s_m0T", m0T[:], DIN("state_mT"), writes=["s_m0T"])
            n0T = T(ph, "s_n0T", [128, 16, 4], F32)
            dma("sp", "s_n0T", n0T[:], DIN("state_nT"), writes=["s_n0T"])
            kts = T(ph, "s_kts", [128, 4, 128], BF16)
            it = intra_tmp(ph, "s_")
            gt = (T(ph, "s_sqh", [128, 4, 128], F32), T(ph, "s_st4", [128, 24], F32))
            gf = T(ph, "s_gf", [4, 12, 128], F32)
            g16 = T(ph, "s_g16", [4, 8, 16], F32)
            aldg = T(ph, "s_aldg", [4, 4, 16], F32)
            albc_s = T(ph, "s_albc", [128, 4, 16], F32)
            cols = T(ph, "s_cols", [128, 12], F32)
            tr(psf(0)[0:4, 0:128], gs[:, 0:4], identf[:], ["s_gs", "identf"], [bank(0)])
            tr(psf(0)[0:4, 128:256], gs[:, 4:8], identf[:], ["s_gs", "identf"], [bank(0)])
            ts("dve", gf[:, 0, :], psf(0)[0:4, 0:128], bcol[:, 0:1], None, ALU.add, None, [bank(0), "s_bcol"], ["s_ig"])
            ts("dve", gf[:, 1, :], psf(0)[0:4, 128:256], bcol[:, 1:2], None, ALU.add, None, [bank(0), "s_bcol"], ["s_lf"])
            act(gf[:, 1, :], gf[:, 1, :], AF.Exp, ["s_lf"], ["s_lf"], scale=-1.0)
            act(gf[:, 1, :], gf[:, 1, :], AF.Ln, ["s_lf"], ["s_lf"], bias=1.0)
            op("dve", lambda e: e.tensor_tensor_scan(gf[:, 2, :], seg8[:], gf[:, 1, :], 0.0, ALU.mult, ALU.add),
               ["s_seg8", "s_lf"], ["s_L"])
            tt("dve", gf[:, 3, :], gf[:, 0, :], gf[:, 2, :], ALU.add, ["s_ig", "s_L"], ["s_G"])
            b8 = lambda a: a.rearrange("p (b s) -> p b s", s=8)
            red(g16[:, 0, :], b8(gf[:, 3, :]), ALU.max, ["s_G"], ["s_gmax"])
            tt("dve", g16[:, 1, :], g16[:, 0, :], m0T[:], ALU.max, ["s_gmax", "s_m0T"], ["s_R"])
            bcR = g16[:, 1, :].unsqueeze(2).to_broadcast([4, 16, 8])
            tt("dve", b8(gf[:, 4, :]), b8(gf[:, 3, :]), bcR, ALU.subtract, ["s_G", "s_R"], ["s_wl"])
            act(gf[:, 4, :], gf[:, 4, :], AF.Exp, ["s_wl"], ["s_wl"])
            tt("dve", b8(gf[:, 5, :]), b8(gf[:, 2, :]), bcR, ALU.subtract, ["s_L", "s_R"], ["s_e"])
            act(gf[:, 5, :], gf[:, 5, :], AF.Exp, ["s_e"], ["s_e"])
            tt("dve", g16[:, 2, :], m0T[:], g16[:, 1, :], ALU.subtract, ["s_m0T", "s_R"], ["s_al"])
            act(g16[:, 2, :], g16[:, 2, :], AF.Exp, ["s_al"], ["s_al"])
            tt("dve", g16[:, 3, :], g16[:, 1, :], b8(gf[:, 2, :])[:, :, 7], ALU.subtract, ["s_R", "s_L"], ["s_mnew"])
            dma("sp", "o_ms", outs["o_ms"], g16[:, 3, :], reads=["s_mnew"], writes=["out_ms"])
            cp("dve", b8(gf[:, 6, :]), g16[:, 2, :].unsqueeze(2).to_broadcast([4, 16, 8]), ["s_al"], ["s_alx"])
            for n, (src, rn) in enumerate(((4, "s_wl"), (5, "s_e"), (6, "s_alx"))):
                tr(psf(1)[:, n * 4:(n + 1) * 4], gf[:, src, :], identf[0:4, 0:4], [rn, "identf"], [bank(1)])
            cp("dve", cols[:], psf(1)[:, 0:12], [bank(1)], ["s_cols", "s_wlc", "s_ec", "s_alc"])
            wlc_s, ec_s, alc_s = cols[:, 0:4], cols[:, 4:8], cols[:, 8:12]
            tt("dve", aldg[:], g16[:, 2, :].unsqueeze(1).to_broadcast([4, 4, 16]),
               i4t[:].unsqueeze(2).to_broadcast([4, 4, 16]), ALU.mult, ["s_al", "s_i4"], ["s_aldg"])
            mm(psf(0)[:, 256:320], onesf[0:4, :], aldg[:].rearrange("p h b -> p (h b)"), True, True,
               ["onesf", "s_aldg"], [bank(0)])
            cp("dve", albc_s[:], psf(0)[:, 256:320].rearrange("p (h b) -> p h b", b=16), [bank(0)], ["s_albc"])
            C0all = T(ph, "s_C0all", [128, 16, 4, 129], F32)
            C0bf = T(ph, "s_C0bf", [128, 16, 4, 129], BF16)
            vm = T(ph, "s_vm", [128, 16, 516], BF16)
            qm = [T(ph, f"s_qm{i}", [128, 16, 128], BF16) for i in range(2)]
            Cn = [T(ph, f"s_Cn{i}", [128, 4, 129], F32) for i in range(2)]
            nnew = T(ph, "s_nnew", [128, 16, 4], F32)
            sC = DIN("state_C")
            c0regs = [f"s_C0a{q}" for q in range(4)]
            for q in range(4):
                dma("sp", c0regs[q], C0all[:, 4 * q:4 * q + 4, :, 0:128], sC[4 * q:4 * q + 4].rearrange("b h k v -> k b h v"),
                    writes=[c0regs[q]])
            cp("dve", C0all[:, :, :, 128], n0T[:, :, :], ["s_n0T"] + c0regs, c0regs)
            for b in range(16):
                tt("dve", C0bf[:, b, :, :], C0all[:, b, :, :], albc_s[:, :, b:b + 1].to_broadcast([128, 4, 129]), ALU.mult,
                   [c0regs[b // 4], "s_albc"], ["s_C0bf"])
            tt("pool", kts[:], ktok[:], cols[:, 0:4].unsqueeze(2).to_broadcast([128, 4, 128]), ALU.mult,
               ["s_ktok", "s_cols"], ["s_kts"])
            tt("pool", vm[:], vaug[:].rearrange("p h v -> p (h v)").unsqueeze(1).to_broadcast([128, 16, 516]),
               rmask[:].unsqueeze(2).to_broadcast([128, 16, 516]), ALU.mult, ["s_vaug", "s_rmask"], ["s_vm"])

            def x2_s(h):
                i = h % 2
                tt("pool", qm[i][:], qT[:, h:h + 1, :].to_broadcast([128, 16, 128]), bmask[:], ALU.mult,
                   ["s_qT", "s_bmask"], [f"s_qm{i}"])
                return [(qm[i][:, b, :], C0bf[:, b, h, :], [f"s_qm{i}", "s_C0bf"]) for b in range(16)]

            rn_s = dict(qT="s_qT", kT="s_kT", vaug="s_vaug", wl="s_cols", e="s_cols", mask="s_caus",
                        sig_mo="s_sig_mo", silu_mz="s_silu_mz")
            hm = mlstm_intra(rn_s, "s_", qT, kT, vaug, wlc_s, ec_s, caus_s[:], x2_s, it)
            ml_post(rn_s, "s_", hm, sig_mo[:], silu_mz[:], gn_bc[:], ys_ml[:], "s_y", gt)
            for b in range(16):
                i = b % 2
                for h in range(4):
                    pb = 2 + 2 * i + h // 2
                    col = slice((h % 2) * 129, (h % 2) * 129 + 129)
                    mm(psf(pb)[:, col], kts[:, h, :], vm[:, b, h * 129:(h + 1) * 129], True, True,
                       ["s_kts", "s_vm"], [bank(pb)])
                tt("pool", Cn[i][:], C0all[:, b, :, :], albc_s[:, :, b:b + 1].to_broadcast([128, 4, 129]), ALU.mult,
                   [c0regs[b // 4], "s_albc"], [f"s_Cn{i}"])
                for hh in range(2):
                    pb = 2 + 2 * i + hh
                    tt("dve", Cn[i][:, 2 * hh:2 * hh + 2, :], Cn[i][:, 2 * hh:2 * hh + 2, :],
                       psf(pb)[:, 0:258].rearrange("p (h v) -> p h v", v=129), ALU.add, [f"s_Cn{i}", bank(pb)], [f"s_Cn{i}"])
                dma("sp", f"o_Cs{i}", outs["o_Cs"][b], Cn[i][:, :, 0:128], reads=[f"s_Cn{i}"], writes=["out_Cs"])
                cp("act", nnew[:, b, :], Cn[i][:, :, 128], [f"s_Cn{i}"], ["s_nnew"])
            dma("sp", "o_ns", outs["o_ns"], nnew[:], reads=["s_nnew"], writes=["out_ns"])
            S.emit()
          omT = T(sst, "b_omT", [64, 8, 128], F32)
          with contextlib.ExitStack() as ph:
            wuv_s = cast_w(ph, "b_wuv", DIN("w_uv"), 2, 512)
            nmask = T(ph, "b_nmask", [128, 16, 64], BF16)
            dma("pool", "b_nmask", nmask[:], DIN("nmask"), writes=["b_nmask"])
            ptab = T(ph, "b_ptab", [128, 16], I32)
            dma("sp", "b_ptab", ptab[:], DIN("ptab"), writes=["b_ptab"])
            idx = T(ph, "b_idx", [128, 16], I32)
            idx8 = T(ph, "b_idx8", [128, 16, 8], I32)
            ts("dve", idx[0:64, :], ptab[0:64, :], 2.0, None, ALU.mult, None, ["b_ptab"], ["b_idx"])
            ts("dve", idx[64:128, :], ptab[64:128, :], 2.0, 1.0, ALU.mult, ALU.add, ["b_ptab"], ["b_idx"])
            for tc in range(8):
                ts("dve", idx8[:, :, tc], idx[:], 8.0, float(tc), ALU.mult, ALU.add, ["b_idx"], ["b_idx8"])
            ckv_view = DIN("cache_ckv").rearrange("p (a t) c -> (p a) (t c)", t=8)
            kr_view = DIN("cache_kr").rearrange("p (a t) r -> (p a) (t r)", a=2)
            NG = 3
            cg = [T(ph, f"b_cg{i}", [128, 64 * 256], BF16) for i in range(NG)]
            krg = [T(ph, f"b_krg{i}", [128, 2048], BF16) for i in range(NG)]
            ckT = [T(ph, f"b_ckT{i}", [128, 1024], BF16) for i in range(2)]
            krTt = [T(ph, f"b_krTt{i}", [128, 128], BF16) for i in range(2)]
            PTs = [T(ph, f"b_PTs{i}", [128, 65, 64], BF16) for i in range(2)]
            rs = T(ph, "b_rs", [128, 64], F32)
            onT = T(ph, "b_onT", [128, 2, 64], BF16)

            def gather(b):
                i = b % NG
                for tc in range(8):
                    S.raw("pool", f"b_cg{i}",
                          lambda e, i=i, tc=tc, b=b: e.indirect_dma_start(
                              out=cg[i][:, tc * 2048:(tc + 1) * 2048], out_offset=None, in_=ckv_view,
                              in_offset=bass.IndirectOffsetOnAxis(ap=idx8[:, b, tc:tc + 1], axis=0)),
                          reads=["b_idx8"], writes=[f"b_cg{i}"], nowaw=(tc > 0))
                S.raw("pool", f"b_krg{i}",
                      lambda e, i=i, b=b: e.indirect_dma_start(
                          out=krg[i][:], out_offset=None, in_=kr_view,
                          in_offset=bass.IndirectOffsetOnAxis(ap=idx[:, b:b + 1], axis=0)),
                      reads=["b_idx"], writes=[f"b_krg{i}"])

            def score_steps(b):
                i = b % NG
                P = PTs[b % 2]
                pn = f"b_PTs{b % 2}"
                qcols = slice(b * 8, (b + 1) * 8)

                def sb_T(tg):
                    k2 = tg % 2
                    for j in range(4):
                        t = 4 * tg + j
                        for cc in range(2):
                            tr(psb(k2)[:, (j * 2 + cc) * 128:(j * 2 + cc + 1) * 128],
                               cg[i][:, t * 256 + cc * 128:t * 256 + (cc + 1) * 128], ident[:],
                               [f"b_cg{i}", "ident"], [bank(k2)])
                    tr(psb(2 + k2)[:, 0:128], krg[i][:, tg * 128:(tg + 1) * 128], ident[:],
                       [f"b_krg{i}", "ident"], [bank(2 + k2)])
                    cp("act" if tg % 2 == 0 else "dve", ckT[k2][:], psb(k2), [bank(k2)], [f"b_ckT{k2}"])
                    cp("dve" if tg % 2 == 0 else "act", krTt[k2][:, 0:128], psb(2 + k2)[:, 0:128], [bank(2 + k2)],
                       [f"b_krTt{k2}"])

                def sb_S(tg):
                    k2 = tg % 2
                    sbk = 4 + (tg // 2) % 2
                    for j in range(4):
                        t = 4 * tg + j
                        oc = slice((t % 8) * 64, (t % 8 + 1) * 64)
                        mm(psf(sbk)[:, oc], ckT[k2][:, (j * 2) * 128:(j * 2 + 1) * 128], qlatT[:, 0, :, qcols],
                           True, False, [f"b_ckT{k2}", "qlatT"], [bank(sbk)])
                        mm(psf(sbk)[:, oc], ckT[k2][:, (j * 2 + 1) * 128:(j * 2 + 2) * 128], qlatT[:, 1, :, qcols],
                           False, False, [f"b_ckT{k2}", "qlatT"], [bank(sbk)])
                        mm(psf(sbk)[:, oc], krTt[k2][:, 0:128], qr4[:, j, :, qcols],
                           False, True, [f"b_krTt{k2}", "qr4"], [bank(sbk)])
                    if tg % 2 == 1:
                        t0 = 4 * (tg - 1)
                        act(P[:, t0:t0 + 8, :].rearrange("p t q -> p (t q)"), psf(sbk), AF.Exp, [bank(sbk)], [pn],
                            scale=MLA_SCALE)

                sb_T(0)
                yield
                for tg in range(16):
                    if tg + 1 < 16:
                        sb_T(tg + 1)
                    sb_S(tg)
                    yield
                mm(psf(4)[:, 0:64], ckvnT_s[:, 0, :], qlatT[:, 0, :, qcols], True, False, ["ckvnT_s", "qlatT"], [bank(4)])
                mm(psf(4)[:, 0:64], ckvnT_s[:, 1, :], qlatT[:, 1, :, qcols], False, False, ["ckvnT_s", "qlatT"], [bank(4)])
                mm(psf(4)[:, 0:64], krT_s[:, :], qr4[:, 0, :, qcols], False, True, ["krT_s", "qr4"], [bank(4)])
                act(P[:, 64, :], psf(4)[:, 0:64], AF.Exp, [bank(4)], [pn], scale=MLA_SCALE)
                tt("dve", P[:, 64, :], P[:, 64, :], nmask[:, b, :], ALU.mult, [pn, "b_nmask"], [pn])
                yield

            def pv_steps(b):
                i = b % NG
                P = PTs[b % 2]
                pn = f"b_PTs{b % 2}"
                qcols = slice(b * 8, (b + 1) * 8)
                n = 0
                for cc in range(3):
                    for t in range(65):
                        if cc < 2:
                            lh = cg[i][:, t * 256 + cc * 128:t * 256 + (cc + 1) * 128] if t < 64 else ckvnb_s[:, cc * 128:(cc + 1) * 128]
                            mm(psf(6)[:, cc * 64:(cc + 1) * 64], lh, P[:, t, :], t == 0, t == 64,
                               [f"b_cg{i}", "ckvnb_s", pn], [bank(6)])
                        else:
                            mm(psf(6)[:, 128:192], onesb[:], P[:, t, :], t == 0, t == 64, ["onesb", pn], [bank(6)])
                        n += 1
                        if n % 12 == 0:
                            yield
                op("dve", lambda e: e.reciprocal(rs[:], psf(6)[:, 128:192]), [bank(6)], ["b_rs"])
                tt("dve", onT[:], psf(6)[:, 0:128].rearrange("p (c q) -> p c q", q=64),
                   rs[:].unsqueeze(1).to_broadcast([128, 2, 64]), ALU.mult, [bank(6), "b_rs"], ["b_onT"])
                yield
                for h in range(8):
                    for cc in range(2):
                        mm(psf(7)[0:64, h * 8:(h + 1) * 8], wuv_s[:, cc, h * 64:(h + 1) * 64], onT[:, cc, h * 8:(h + 1) * 8],
                           cc == 0, cc == 1, ["b_wuv", "b_onT"], [bank(7)])
                cp("act", omT[:, :, qcols], psf(7)[0:64, 0:64].rearrange("p (h s) -> p h s", s=8), [bank(7)], ["b_omT"])
                yield

            def drain(*gens):
                live = list(gens)
                while live:
                    for gobj in list(live):
                        try:
                            next(gobj)
                        except StopIteration:
                            live.remove(gobj)

            gather(0)
            gather(1)
            drain(score_steps(0))
            for b in range(16):
                if b + 2 < 16:
                    gather(b + 2)
                if b + 1 < 16:
                    drain(score_steps(b + 1), pv_steps(b))
                else:
                    drain(pv_steps(b))
            S.emit()
          with contextlib.ExitStack() as ph:
            wout_ml = cast_w(ph, "b_wout_ml", DIN("w_out")[0:512, :], 4, 1024)
            wout_mla = T(ph, "b_wout_mla", [64, 8, 1024], BF16)
            dma("pool", "b_wout_mla", wout_mla[:], DIN("w_out")[512:1024, :].rearrange("(h v) n -> v h n", v=64),
                writes=["b_wout_mla"])
            lng = T(ph, "b_lng", [128, D], F32)
            lnb = T(ph, "b_lnb", [128, D], F32)
            dma("sp", "b_lng", lng[:], DIN("ln_g").partition_broadcast(128), writes=["lng"])
            dma("sp", "b_lnb", lnb[:], DIN("ln_b").partition_broadcast(128), writes=["lnb"])
            ymlaT = T(ph, "b_ymlaT", [64, 8, 128], BF16)
            yT_ml = T(ph, "b_yT_ml", [128, 4, 128], BF16)
            lt = (T(ph, "b_z", [128, D], F32), T(ph, "b_sqz", [128, D], F32), T(ph, "b_st1", [128, 8], F32))
            tt("dve", ymlaT[:], omT[:], silu_azT[0:64, :, :], ALU.mult, ["b_omT", "silu_azT"], ["b_ymlaT"])
            for fc in range(4):
                tr(psb(0)[:, fc * 128:(fc + 1) * 128], ys_ml[:, fc * 128:(fc + 1) * 128], ident[:], ["s_y", "ident"], [bank(0)])
            cp("act", yT_ml[:].rearrange("p k t -> p (k t)"), psb(0)[:, 0:512], [bank(0)], ["b_yT_ml"])
            for half in range(2):
                hs = slice(half * 512, (half + 1) * 512)
                for fc in range(4):
                    mm(psf(2 + half), yT_ml[:, fc, :], wout_ml[:, fc, hs], fc == 0, False, ["b_yT_ml", "b_wout_ml"], [bank(2 + half)])
                for h in range(8):
                    mm(psf(2 + half), ymlaT[:, h, :], wout_mla[:, h, hs], False, h == 7, ["b_ymlaT", "b_wout_mla"], [bank(2 + half)])
            ln_out("b_", "b_", (2, 3), xf_s[:], gate_s, lng, lnb, outs["o_ys"], "out_ys", lt)
            S.emit()
          sst.close()
        wl_own = T(st, "wl_own", [128, 4, NOWN], F32)
        e_own = T(st, "e_own", [128, 4, NOWN], F32)
        al_own = T(st, "al_own", [128, 4, NOWN], F32)
        wlT = T(st, "wlT", [128, 4, NBLK], F32)
        eT = T(st, "eT", [128, 4, NBLK], F32)
        albc = T(st, "albc", [128, 4, NBLK], F32)
        attn = T(st, "attn", [128, NOWN, 512], BF16)
        Cown = T(st, "Cown", [128, NOWN, 4, 129], BF16)
        st14 = contextlib.ExitStack()
        ckvT_all = T(st14, "ckvT_all", [128, 2, SEQ], BF16)
        KRp = T(st14, "KRp", [128, NBLK, 96], BF16)
        GT = T(st14, "GT", [128, NBLK, 8], F32)
        op("pool", lambda e: e.memset(KRp[:], 0.0), writes=["KRp"])
        with contextlib.ExitStack() as ph:
            Wa = cast_w(ph, "Wa", DIN("w_all"), KC, 1352)
            xb = [T(ph, f"xb{i}", [128, D], BF16) for i in range(4)]
            hT = [T(ph, f"hT{i}", [128, KC, 128], BF16) for i in range(2)]
            rope = [T(ph, f"rope{i}", [128, 64], F32) for i in range(4)]
            ktok = [T(ph, f"ktok{i}", [128, 512], BF16) for i in range(2)]
            vaug = [T(ph, f"vaug{i}", [128, 4, 129], BF16) for i in range(2)]
            ckvn = [T(ph, f"ckvn{i}", [128, 256], F32) for i in range(2)]
            ckvb = [T(ph, f"ckvb{i}", [128, 256], BF16) for i in range(2)]
            krr = [T(ph, f"krr{i}", [128, 32], F32) for i in range(2)]
            mt = misc_tmp(ph, "p1")
            for i in range(2):
                op("pool", lambda e, i=i: e.memset(vaug[i][:], 1.0), writes=[f"vaug{i}"])
            xall = DIN("x_all")
            ropeall = DIN("rope_all")
            def p1_L(blk):
                b4 = blk % 4
                rows = slice(blk * 128, (blk + 1) * 128)
                dma("pool", f"xb{b4}", xb[b4][:], xall[rows, :], writes=[f"xb{b4}"])
                dma("sp", f"rope{b4}", rope[b4][:], ropeall[rows, :], writes=[f"rope{b4}"])

            def p1_T(blk):
                b = blk % 2
                b4 = blk % 4
                make_hT(xb[b4], f"xb{b4}", hT[b], f"hT{b}", b, False)

            def p1_M(blk):
                b = blk % 2
                tokgroup(2, hT[b], f"hT{b}", Wa, "Wa", 0, 512)
                act(ktok[b][:], psf(2), AF.Identity, [bank(2)], [f"ktok{b}"], scale=DK ** -0.5)
                tokgroup(3, hT[b], f"hT{b}", Wa, "Wa", 512, 512)
                cp("dve", vaug[b][:, :, 0:128], psf(3).rearrange("p (h v) -> p h v", v=128), [bank(3)], [f"vaug{b}"])
                tokgroup(4 + b, hT[b], f"hT{b}", Wa, "Wa", 1024, 328)
                dma("sp", f"spk{b}", kscr[blk], ktok[b][:], reads=[f"ktok{b}"], writes=[f"kscr{blk}"])
                dma("sp", f"spv{b}", vscr[blk], vaug[b][:], reads=[f"vaug{b}"], writes=[f"vscr{blk}"])
                misc_post(4 + b, rope[blk % 4], f"rope{blk % 4}", ckvn[b], f"ckvn{b}", krr[b], f"krr{b}", mt)
                cp("act", GT[:, blk, :], psf(4 + b)[:, 320:328], [bank(4 + b)], ["GT"])
                cp("pool", ckvb[b][:], ckvn[b][:], [f"ckvn{b}"], [f"ckvb{b}"])
                cp("pool", KRp[:, blk, 64:96], krr[b][:], [f"krr{b}"], ["KRp"])

            def p1_CT(blk):
                b = blk % 2
                rows = slice(blk * 128, (blk + 1) * 128)
                for cc in range(2):
                    tr(psb(6 + b)[:, cc * 128:(cc + 1) * 128], ckvb[b][:, cc * 128:(cc + 1) * 128], ident[:],
                       [f"ckvb{b}", "ident"], [bank(6 + b)])
                cp("act", ckvT_all[:, :, rows], psb(6 + b)[:, 0:256].rearrange("p (c t) -> p c t", t=128),
                   [bank(6 + b)], ["ckvT_all"])

            for blk in range(3):
                p1_L(blk)
            p1_T(0)
            for blk in range(NBLK):
                if blk + 3 < NBLK:
                    p1_L(blk + 3)
                if blk + 1 < NBLK:
                    p1_T(blk + 1)
                p1_M(blk)
                if blk >= 1:
                    p1_CT(blk - 1)
            p1_CT(NBLK - 1)
            S.emit()

        def p2_steps(ph):
                tstrict = T(ph, "tstrict", [64, 64], F32)
                segm = T(ph, "segm", [64, 512], F32)
                bif = T(ph, "bif", [64, 8], F32)
                dma("sp", "tstrict", tstrict[:], DIN("tstrict"), writes=["tstrict"])
                dma("sp", "segm", segm[:], DIN("segmask")[0:64, :], writes=["segm"])
                dma("sp", "bif", bif[:], DIN("b_if_bc"), writes=["bif"])
                ig = T(ph, "ig", [64, 4, 128], F32)
                lf = T(ph, "lf", [64, 4, 128], F32)
                Ll = T(ph, "Ll", [64, 4, 128], F32)
                Lg = T(ph, "Lg", [64, 4, 128], F32)
                G = T(ph, "G", [64, 4, 128], F32)
                wl = T(ph, "wl", [64, 4, 128], F32)
                ee = T(ph, "ee", [64, 4, 128], F32)
                sm = T(ph, "sm", [64, 8, 4], F32)
                smT = T(ph, "smT", [4, 4, 64], F32)
                aldiag = T(ph, "aldiag", [64, 4, 64], F32)
                for j in range(8):
                    pb = 0 if j < 4 else 1
                    tr(psf(pb)[0:64, (j % 4) * 128:(j % 4 + 1) * 128], GT[:, :, j], identf[:], ["GT", "identf"], [bank(pb)])
                v3 = lambda a: a.rearrange("p (h s) -> p h s", s=128)
                tt("dve", ig[:], v3(psf(0)[0:64, :]), bif[:, 0:4].unsqueeze(2).to_broadcast([64, 4, 128]), ALU.add,
                   [bank(0), "bif"], ["ig"])
                tt("dve", lf[:], v3(psf(1)[0:64, :]), bif[:, 4:8].unsqueeze(2).to_broadcast([64, 4, 128]), ALU.add,
                   [bank(1), "bif"], ["lf"])
                yield
                act(lf[:], lf[:], AF.Exp, ["lf"], ["lf"], scale=-1.0)
                act(lf[:], lf[:], AF.Ln, ["lf"], ["lf"], bias=1.0)
                op("dve", lambda e: e.tensor_tensor_scan(Ll[:].rearrange("p h s -> p (h s)"), segm[:],
                                                           lf[:].rearrange("p h s -> p (h s)"), 0.0, ALU.mult, ALU.add),
                   ["segm", "lf"], ["Ll"])
                yield
                cp("dve", sm[:, 0, :], Ll[:, :, 127], ["Ll"], ["sm0"])
                mm(psf(2)[0:64, 0:4], tstrict[:], sm[:, 0, :], True, True, ["tstrict", "sm0"], [bank(2)])
                cp("dve", sm[:, 1, :], psf(2)[0:64, 0:4], [bank(2)], ["sm1"])
                tt("dve", Lg[:], Ll[:], sm[:, 1, :].unsqueeze(2).to_broadcast([64, 4, 128]), ALU.add, ["Ll", "sm1"], ["Lg"])
                tt("dve", G[:], ig[:], Lg[:], ALU.add, ["ig", "Lg"], ["G"])
                yield
                red(sm[:, 2, :], G[:], ALU.max, ["G"], ["sm2"])
                tr(psf(3)[0:4, 0:64], sm[:, 2, :], identf[0:64, 0:64], ["sm2", "identf"], [bank(3)])
                cp("dve", smT[:, 0, :], psf(3)[0:4, 0:64], [bank(3)], ["smT0"])
                op("dve", lambda e: e.tensor_tensor_scan(smT[:, 1, :], smT[:, 0, :], smT[:, 0, :], NEG, ALU.max, ALU.max),
                   ["smT0"], ["smT1"])
                op("dve", lambda e: e.memset(smT[:, 2, 0:1], NEG), [], ["smT2a"])
                cp("dve", smT[:, 2, 1:64], smT[:, 1, 0:63], ["smT1", "smT2a"], ["smT2"])
                tr(psf(4)[0:64, 0:4], smT[:, 1, :], identf[0:4, 0:4], ["smT1", "identf"], [bank(4)])
                tr(psf(4)[0:64, 4:8], smT[:, 2, :], identf[0:4, 0:4], ["smT2", "identf"], [bank(4)])
                cp("dve", sm[:, 3:5, :], psf(4)[0:64, 0:8].rearrange("p (a h) -> p a h", h=4), [bank(4)], ["sm34"])
                bcR = sm[:, 3, :].unsqueeze(2).to_broadcast([64, 4, 128])
                tt("dve", wl[:], G[:], bcR, ALU.subtract, ["G", "sm34"], ["wl"])
                act(wl[:], wl[:], AF.Exp, ["wl"], ["wl"])
                tt("dve", ee[:], Lg[:], bcR, ALU.subtract, ["Lg", "sm34"], ["ee"])
                act(ee[:], ee[:], AF.Exp, ["ee"], ["ee"])
                yield
                tt("dve", sm[:, 5, :], sm[:, 4, :], sm[:, 3, :], ALU.subtract, ["sm34"], ["sm5"])
                act(sm[:, 5, :], sm[:, 5, :], AF.Exp, ["sm5"], ["sm5"])
                tt("dve", sm[:, 6, :], sm[:, 3, :], Lg[:, :, 127], ALU.subtract, ["sm34", "Lg"], ["sm6"])
                dma("sp", "o_m", outs["o_m"], sm[63:64, 6, :], reads=["sm6"], writes=["o_m"])
                for h in range(4):
                    tr(psf(5)[:, h * 64:(h + 1) * 64], wl[:, h, :], identf[0:64, 0:64], ["wl", "identf"], [bank(5)])
                    tr(psf(6)[:, h * 64:(h + 1) * 64], ee[:, h, :], identf[0:64, 0:64], ["ee", "identf"], [bank(6)])
                cp("dve", wlT[:], psf(5)[:, 0:256].rearrange("p (h c) -> p h c", c=64), [bank(5)], ["wlT"])
                cp("act", eT[:], psf(6)[:, 0:256].rearrange("p (h c) -> p h c", c=64), [bank(6)], ["eT"])
                yield
                tt("dve", aldiag[:], sm[:, 5, :].unsqueeze(2).to_broadcast([64, 4, 64]),
                   identf[0:64, 0:64].unsqueeze(1).to_broadcast([64, 4, 64]), ALU.mult, ["sm5", "identf"], ["aldiag"])
                mm(psf(7)[:, 0:256], onesf[0:64, :], aldiag[:].rearrange("p h c -> p (h c)"), True, True,
                   ["onesf", "aldiag"], [bank(7)])
                cp("dve", albc[:], psf(7)[:, 0:256].rearrange("p (h c) -> p h c", c=64), [bank(7)], ["albc"])
                yield
                for src, dst, dn in ((wlT, wl_own, "wl_own"), (eT, e_own, "e_own"), (albc, al_own, "al_own")):
                    sv = src[:].rearrange("p h (g i) -> p h g i", i=4)
                    ts("dve", dst[:], sv[:, :, :, 0], sel[:, 0:1], None, ALU.mult, None, ["wlT", "eT", "albc", "sel"], [dn])
                    for i in range(1, 4):
                        stt(dst[:], sv[:, :, :, i], sel[:, i:i + 1], dst[:], ALU.mult, ALU.add, ["wlT", "eT", "albc", "sel", dn], [dn])
                yield
        if stage >= 2:
          with contextlib.ExitStack() as ph:
            WOWN = DIN("w_own")
            qaT = T(ph, "qaT", [128, 8, NOWN * 128], BF16)
            wukp = cast_w(ph, "wukp", DIN("w_ukp").rearrange("c h n -> c (h n)"), 2, 768)
            wuv = cast_w(ph, "wuv", DIN("w_uv"), 2, 512)
            maskA = T(ph, "maskA", [128, 512], BF16)
            dma("pool", "maskA", maskA[:], DIN("maskA").rearrange("p i q -> p (i q)"), writes=["maskA"])
            ph_outer = ph
            ph = contextlib.ExitStack()
            Wcq = cast_w(ph, "Wcq", WOWN[:, 1024:1408], KC, 384)
            gqT = T(ph, "gqT", [128, 3], F32)
            dma("sp", "gqT", gqT[:], DIN("gqT"), writes=["gqT"])
            wq_st = T(ph, "wq_st", [128, 3, 768], F32)
            wuq = T(ph, "wuq", [128, 3, 768], BF16)
            wuqp = T(ph, "wuqp", [128, 3, 768], BF16)
            for nm, dst, dn in (("w_uq", wuq, "wuq"), ("w_uqp", wuqp, "wuqp")):
                dma("sp", "wq_st", wq_st[:], DIN(nm).rearrange("(c p) h n -> p c (h n)", p=128), writes=["wq_st"])
                for cc in range(3):
                    ts("dve", dst[:, cc, :], wq_st[:, cc, :], gqT[:, cc:cc + 1], None, ALU.mult, None,
                       ["wq_st", "gqT"], [dn])
            xb = [T(ph, f"q_xb{i}", [128, D], BF16) for i in range(4)]
            hT = [T(ph, f"q_hT{i}", [128, KC, 128], BF16) for i in range(3)]
            ropeT = [T(ph, f"q_ropeT{i}", [128, 2, 128], F32) for i in range(4)]
            sqc = T(ph, "sqc", [128, 3, 128], BF16)
            rq = T(ph, "rq", [128, 128], F32)
            rq2 = T(ph, "rq2", [128, 128], F32)
            cqn = [T(ph, f"cqn{i}", [128, 3, 128], BF16) for i in range(2)]
            tq1 = T(ph, "tq1", [128, 4, 128], F32)
            tq2 = T(ph, "tq2", [128, 4, 128], F32)
            xown = DIN("x_own")
            ropeTo = DIN("ropeT_own")
            v3 = lambda a: a.rearrange("p (h t) -> p h t", t=128)

            def q_L(g):
                b4 = g % 4
                rows = slice(g * 128, (g + 1) * 128)
                dma("pool", f"q_xb{b4}", xb[b4][:], xown[rows, :], writes=[f"q_xb{b4}"])
                dma("sp", f"q_ropeT{b4}", ropeT[b4][:], ropeTo[:, :, rows], writes=[f"q_ropeT{b4}"])

            def q_T(g):
                make_hT(xb[g % 4], f"q_xb{g % 4}", hT[g % 3], f"q_hT{g % 3}", g % 2, False)

            def q_A(g):
                b = g % 2
                for cc in range(3):
                    featgroup(2, cc * 128, hT[g % 3], f"q_hT{g % 3}", Wcq, "Wcq", cc * 128, 128)
                act(sqc[:], psf(2)[:, 0:384].rearrange("p (c t) -> p c t", t=128), AF.Square, [bank(2)], ["sqc"])
                for cc in range(3):
                    mm(psf(3)[:, 0:128], onesb[:], sqc[:, cc, :], cc == 0, cc == 2, ["onesb", "sqc"], [bank(3)])
                ts("dve", rq[:], psf(3)[:, 0:128], 1.0 / 384.0, EPS, ALU.mult, ALU.add, [bank(3)], ["rq"])
                rsqrt_pool(rq2[:], rq[:], None, ["rq"], ["rq2"])
                tt("dve", cqn[b][:], psf(2)[:, 0:384].rearrange("p (c t) -> p c t", t=128),
                   rq2[:].unsqueeze(1).to_broadcast([128, 3, 128]), ALU.mult, [bank(2), "rq2"], [f"cqn{b}"])

            def q_B(g):
                b = g % 2
                b4 = g % 4
                rows = slice(g * 128, (g + 1) * 128)
                for h in range(8):
                    col = slice((h % 4) * 128, (h % 4 + 1) * 128)
                    for cc in range(3):
                        mm(psf(4 + h // 4)[0:96, col], wuq[:, cc, h * 96:(h + 1) * 96], cqn[b][:, cc, :], cc == 0, cc == 2,
                           ["wuq", f"cqn{b}"], [bank(4 + h // 4)])
                    for cc in range(3):
                        mm(psf(6 + h // 4)[0:96, col], wuqp[:, cc, h * 96:(h + 1) * 96], cqn[b][:, cc, :], cc == 0, cc == 2,
                           ["wuqp", f"cqn{b}"], [bank(6 + h // 4)])
                for hb in range(2):
                    cp("act", qaT[0:64, 4 * hb:4 * hb + 4, rows], v3(psf(4 + hb)[0:64, :]), [bank(4 + hb)], ["qaT"])
                    tt("dve", tq1[64:96], v3(psf(4 + hb)[64:96, :]),
                       ropeT[b4][64:96, 0:1, :].to_broadcast([32, 4, 128]), ALU.mult,
                       [bank(4 + hb), f"q_ropeT{b4}"], ["tq1"])
                    tt("dve", tq2[64:96], v3(psf(6 + hb)[64:96, :]),
                       ropeT[b4][64:96, 1:2, :].to_broadcast([32, 4, 128]), ALU.mult,
                       [bank(6 + hb), f"q_ropeT{b4}"], ["tq2"])
                    tt("pool", qaT[64:96, 4 * hb:4 * hb + 4, rows], tq1[64:96], tq2[64:96], ALU.add,
                       ["tq1", "tq2"], ["qaT"])

            p2 = p2_steps(ph)
            for g in range(3):
                q_L(g)
            q_T(0)
            q_T(1)
            q_A(0)
            next(p2, None)
            for g in range(NOWN):
                if g + 3 < NOWN:
                    q_L(g + 3)
                if g + 2 < NOWN:
                    q_T(g + 2)
                if g + 1 < NOWN:
                    q_A(g + 1)
                q_B(g)
                next(p2, None)
            for _ in p2:
                pass
            S.emit()
            ph.close()
            ph = ph_outer
            KT = T(ph, "KT", [128, 2, SEQ], BF16)
            Vt = T(ph, "Vt", [128, NBLK, 2, 65], BF16)
            Cst = T(ph, "Cst", [128, 4, 129], F32)
            Cown32 = T(ph, "Cown32", [128, 4, 129], F32)
            kin = [T(ph, f"kin{i}", [128, 4, 128], BF16) for i in range(2)]
            vin = [T(ph, f"vin{i}", [128, 4, 129], BF16) for i in range(2)]
            kt = [T(ph, f"kt{i}", [128, 4, 128], BF16) for i in range(2)]
            op("pool", lambda e: e.memset(Cst[:], 0.0), writes=["Cst"])

            def rec_steps():
                for c in range(NBLK):
                    b = c % 2
                    g, i = c // 4, c % 4
                    dma("sp", f"kin{b}", kin[b][:].rearrange("p h d -> p (h d)"), kscr[c], reads=[f"kscr{c}"], writes=[f"kin{b}"])
                    dma("sp", f"vin{b}", vin[b][:], vscr[c], reads=[f"vscr{c}"], writes=[f"vin{b}"])
                    tt("pool", kt[b][:], kin[b][:], wlT[:, :, c:c + 1].to_broadcast([128, 4, 128]), ALU.mult,
                       [f"kin{b}", "wlT"], [f"kt{b}"])
                    yield
                    for h in range(4):
                        pb = h // 2
                        col = slice((h % 2) * 129, (h % 2) * 129 + 129)
                        mm(psf(pb)[:, col], kt[b][:, h, :], vin[b][:, h, :], True, True, [f"kt{b}", f"vin{b}"], [bank(pb)])
                    if i == 0:
                        ts("dve", Cown32[:], Cst[:], sel[:, 0:1], None, ALU.mult, None, ["Cst", "sel"], ["Cown32"])
                    else:
                        stt(Cown32[:], Cst[:], sel[:, i:i + 1], Cown32[:], ALU.mult, ALU.add, ["Cst", "sel", "Cown32"], ["Cown32"])
                    if i == 3:
                        tt("pool", Cown[:, g, :, :], Cown32[:], al_own[:, :, g:g + 1].to_broadcast([128, 4, 129]), ALU.mult,
                           ["Cown32", "al_own"], ["Cown"])
                    for h in range(4):
                        pb = h // 2
                        col = slice((h % 2) * 129, (h % 2) * 129 + 129)
                        stt(Cst[:, h, :], Cst[:, h, :], albc[:, h, c:c + 1], psf(pb)[:, col], ALU.mult, ALU.add,
                            ["Cst", "albc", bank(pb)], ["Cst"])
                    yield
                dma("sp", "o_C", outs["o_C"], Cst[:], reads=["Cst"], writes=["o_C"])
                yield

            rec = rec_steps()

            def rec_tick():
                try:
                    next(rec)
                except StopIteration:
                    pass

            PT = [T(ph, f"PT{i}", [128, 512], BF16) for i in range(3)]
            rinv = T(ph, "rinv", [128, 2], F32)
            op("pool", lambda e: e.memset(Vt[:], 1.0), writes=["Vt"])
            cnt = 0
            for p in range(4):
                for kb4 in range(16):
                    for hh in range(2):
                        h = 2 * p + hh
                        pb = 2 + (kb4 % 2) * 2 + hh
                        keys = slice(kb4 * 512, (kb4 + 1) * 512)
                        for cc in range(2):
                            mm(psf(pb)[0:96, :], wukp[:, cc, h * 96:(h + 1) * 96], ckvT_all[:, cc, keys], cc == 0, False,
                               ["wukp", "ckvT_all"], [bank(pb)])
                        for i in range(4):
                            mm(psf(pb)[0:96, i * 128:(i + 1) * 128], KRp[:, kb4 * 4 + i, :], ident[:], False, i == 3,
                               ["KRp", "ident"], [bank(pb)])
                        cp("act" if hh == 0 else "dve", KT[0:96, hh, keys], psf(pb)[0:96, :], [bank(pb)], [f"KT{hh}"])
                for kb in range(NBLK):
                    pb = 6 + (kb // 4) % 2
                    for cc in range(2):
                        mm(psf(pb)[:, (kb % 4) * 128:(kb % 4 + 1) * 128], ckvT_all[:, cc, kb * 128:(kb + 1) * 128],
                           wuv[:, cc, p * 128:(p + 1) * 128], cc == 0, cc == 1, ["ckvT_all", "wuv"], [bank(pb)])
                    if kb % 4 == 3:
                        cp("dve", Vt[:, kb - 3:kb + 1, :, 0:64],
                           psf(pb)[:, 0:512].rearrange("p (k h v) -> p k h v", h=2, v=64), [bank(pb)], ["Vt"])
                items = [(g, hh, kg) for g in range(NOWN) for hh in range(2) for kg in range(g + 1)]

                def emit_S(n):
                    g, hh, kg = items[n]
                    h = 2 * p + hh
                    sbk = 2 + n % 4
                    for i in range(4):
                        kb = 4 * kg + i
                        mm(psf(sbk)[:, i * 128:(i + 1) * 128], KT[0:96, hh, kb * 128:(kb + 1) * 128],
                           qaT[0:96, h, g * 128:(g + 1) * 128], True, True, [f"KT{hh}", "qaT"], [bank(sbk)])

                emit_S(0)
                for n, (g, hh, kg) in enumerate(items):
                    if n % 8 == 4:
                        rec_tick()
                    ob = 6 + g % 2
                    sbk = 2 + n % 4
                    pt = PT[n % 3]
                    ptn = f"PT{n % 3}"
                    act(pt[:], psf(sbk), AF.Exp, [bank(sbk)], [ptn], scale=MLA_SCALE)
                    if kg == g:
                        tt("dve", pt[:], pt[:], maskA[:], ALU.mult, [ptn, "maskA"], [ptn])
                    if n + 1 < len(items):
                        emit_S(n + 1)
                    for i in range(4):
                        kb = 4 * kg + i
                        mm(psf(ob)[:, hh * 65:(hh + 1) * 65], pt[:, i * 128:(i + 1) * 128], Vt[:, kb, hh, :],
                           kg == 0 and i == 0, kg == g and i == 3, [ptn, "Vt"], [bank(ob)])
                    if hh == 1 and kg == g:
                        ov = psf(ob)[:, 0:130].rearrange("p (h v) -> p h v", v=65)
                        op("dve", lambda e, ov=ov: e.reciprocal(rinv[:], ov[:, :, 64]), [bank(ob)], ["rinv"])
                        tt("dve", attn[:, g, p * 128:(p + 1) * 128].rearrange("p (h v) -> p h v", v=64), ov[:, :, 0:64],
                           rinv[:].unsqueeze(2).to_broadcast([128, 2, 64]), ALU.mult, [bank(ob), "rinv"], ["attn"])
            for _ in range(200):
                rec_tick()
            S.emit()
        st14.close()
        if stage >= 2:
          with contextlib.ExitStack() as ph:
            WOWN = DIN("w_own")
            WALL = DIN("w_all")
            Wq5 = cast_w(ph, "Wq5", WOWN[:, 0:1024], KC, 1024)
            Wg5 = cast_w(ph, "Wg5", WOWN[:, 1408:2944], KC, 1536)
            Wv5 = cast_w(ph, "Wv5", WALL[:, 512:1352], KC, 840)
            wout = cast_w(ph, "wout", DIN("w_out"), KC, 1024)
            gn_bc = T(ph, "gn_bc", [128, 512], F32)
            lng = T(ph, "lng", [128, D], F32)
            lnb = T(ph, "lnb", [128, D], F32)
            caus = T(ph, "caus", [128, 128], F32)
            dma("sp", "gn_bc", gn_bc[:], DIN("ml_gn").partition_broadcast(128), writes=["gn_bc"])
            dma("sp", "lng", lng[:], DIN("ln_g").partition_broadcast(128), writes=["lng"])
            dma("sp", "lnb", lnb[:], DIN("ln_b").partition_broadcast(128), writes=["lnb"])
            dma("sp", "caus", caus[:], DIN("caus"), writes=["caus"])
            xb = [T(ph, f"o_xb{i}", [128, D], BF16) for i in range(4)]
            xf = [T(ph, f"o_xf{i}", [128, D], F32) for i in range(2)]
            hT = [T(ph, f"o_hT{i}", [128, KC, 128], BF16) for i in range(3)]
            rope = [T(ph, f"o_rope{i}", [128, 64], F32) for i in range(4)]
            qT = [T(ph, f"o_qT{i}", [128, 4, 128], BF16) for i in range(2)]
            kT = [T(ph, f"o_kT{i}", [128, 4, 128], BF16) for i in range(2)]
            vaug = [T(ph, f"o_vaug{i}", [128, 4, 129], BF16) for i in range(2)]
            ckvn = [T(ph, f"o_ckvn{i}", [128, 256], F32) for i in range(2)]
            krr = [T(ph, f"o_krr{i}", [128, 32], F32) for i in range(2)]
            sig_mo = [T(ph, f"o_sig_mo{i}", [128, 512], BF16) for i in range(2)]
            silu_mz = [T(ph, f"o_silu_mz{i}", [128, 512], BF16) for i in range(2)]
            silu_az = [T(ph, f"o_silu_az{i}", [128, 512], BF16) for i in range(2)]
            y = T(ph, "o_y", [128, D], BF16)
            yT = T(ph, "o_yT", [128, KC, 128], BF16)
            mt = misc_tmp(ph, "p5")
            it = intra_tmp(ph, "o_")
            gt = (T(ph, "o_sqh", [128, 4, 128], F32), T(ph, "o_st4", [128, 24], F32))
            lt = (T(ph, "o_z", [128, D], F32), T(ph, "o_sqz", [128, D], F32), T(ph, "o_st1", [128, 8], F32))
            for i in range(2):
                op("pool", lambda e, i=i: e.memset(vaug[i][:], 1.0), writes=[f"o_vaug{i}"])
            xown = DIN("x_own")
            ropeo = DIN("rope_own")

            def p5_L(g):
                b4 = g % 4
                rows = slice(g * 128, (g + 1) * 128)
                dma("pool", f"o_xb{b4}", xb[b4][:], xown[rows, :], writes=[f"o_xb{b4}"])
                dma("sp", f"o_rope{b4}", rope[b4][:], ropeo[rows, :], writes=[f"o_rope{b4}"])

            def p5_T(g):
                make_hT(xb[g % 4], f"o_xb{g % 4}", hT[g % 3], f"o_hT{g % 3}", g % 2, False)

            def p5_M(g):
                b = g % 2
                rows = slice(g * 128, (g + 1) * 128)
                hn = f"o_hT{g % 3}"
                dma("sp", f"o_xf{b}", xf[b][:], xown[rows, :], writes=[f"o{b}_xf"])
                for h in range(4):
                    featgroup(2, h * 128, hT[g % 3], hn, Wq5, "Wq5", h * 128, 128)
                cp("act", qT[b][:].rearrange("p h t -> p (h t)"), psf(2), [bank(2)], [f"o_qT{b}"])
                for h in range(4):
                    featgroup(3, h * 128, hT[g % 3], hn, Wq5, "Wq5", 512 + h * 128, 128)
                act(kT[b][:].rearrange("p h t -> p (h t)"), psf(3), AF.Identity, [bank(3)], [f"o_kT{b}"], scale=DK ** -0.5)
                tokgroup(4, hT[g % 3], hn, Wv5, "Wv5", 0, 512)
                cp("act", vaug[b][:, :, 0:128], psf(4).rearrange("p (h v) -> p h v", v=128), [bank(4)], [f"o_vaug{b}"])
                tokgroup(5, hT[g % 3], hn, Wv5, "Wv5", 512, 328)
                misc_post(5, rope[g % 4], f"o_rope{g % 4}", ckvn[b], f"o_ckvn{b}", krr[b], f"o_krr{b}", mt)
                dma("sp", f"o_ckv{b}", outs["o_ckv"][rows, :], ckvn[b][:], reads=[f"o_ckvn{b}"], writes=["out_ckv"])
                dma("sp", f"o_kr{b}", outs["o_kr"][rows, :], krr[b][:], reads=[f"o_krr{b}"], writes=["out_kr"])
                tokgroup(6, hT[g % 3], hn, Wg5, "Wg5", 0, 512)
                act(sig_mo[b][:], psf(6), AF.Sigmoid, [bank(6)], [f"o_sig_mo{b}"])
                tokgroup(7, hT[g % 3], hn, Wg5, "Wg5", 512, 512)
                act(silu_mz[b][:], psf(7), AF.Silu, [bank(7)], [f"o_silu_mz{b}"])
                tokgroup(2, hT[g % 3], hn, Wg5, "Wg5", 1024, 512)
                act(silu_az[b][:], psf(2), AF.Silu, [bank(2)], [f"o_silu_az{b}"])

            def p5_mid(g):
                b = g % 2
                rn = dict(qT=f"o_qT{b}", kT=f"o_kT{b}", vaug=f"o_vaug{b}", wl="wl_own", e="e_own", mask="caus",
                          sig_mo=f"o_sig_mo{b}", silu_mz=f"o_silu_mz{b}")
                hm = mlstm_intra(rn, "o_", qT[b], kT[b], vaug[b], wl_own[:, :, g], e_own[:, :, g], caus[:],
                                 lambda h: [(qT[b][:, h, :], Cown[:, g, h, :], [f"o_qT{b}", "Cown"])], it)
                ml_post(rn, "o_", hm, sig_mo[b][:], silu_mz[b][:], gn_bc[:], y[:, 0:512], "o_y", gt)
                tt("pool", y[:, 512:1024], attn[:, g, :], silu_az[b][:], ALU.mult, ["attn", f"o_silu_az{b}"], ["o_y"])

            def p5_tail(g):
                b = g % 2
                rows = slice(g * 128, (g + 1) * 128)
                for fc in range(KC):
                    tr(psb(2)[:, fc * 128:(fc + 1) * 128], y[:, fc * 128:(fc + 1) * 128], ident[:],
                       ["o_y", "ident"], [bank(2)])
                cp("act", yT[:].rearrange("p k t -> p (k t)"), psb(2), [bank(2)], ["o_yT"])
                for half in range(2):
                    for fc in range(KC):
                        mm(psf(3 + half), yT[:, fc, :], wout[:, fc, half * 512:(half + 1) * 512], fc == 0, fc == KC - 1,
                           ["o_yT", "wout"], [bank(3 + half)])
                gen = ln_steps(f"o{b}_", "o_", (3, 4), xf[b][:], gate_p, lng, lnb, outs["o_y"][rows, :], "out_y", lt)
                next(gen)
                return gen

            def finish(gen):
                if gen is not None:
                    for _ in gen:
                        pass

            for g in range(3):
                p5_L(g)
            p5_T(0)
            p5_T(1)
            p5_M(0)
            pending = None
            for g in range(NOWN):
                if g + 3 < NOWN:
                    p5_L(g + 3)
                if g + 2 < NOWN:
                    p5_T(g + 2)
                p5_mid(g)
                finish(pending)
                if g + 1 < NOWN:
                    p5_M(g + 1)
                pending = p5_tail(g)
            finish(pending)
            S.emit()
        S.emit()
    return nc, sorted(used_in)


def _rope_tables(pos):
    half = 16
    inv = (10000.0 ** (-np.arange(half, dtype=np.float32) / half)).astype(np.float32)
    ang = pos.astype(np.float32)[:, None] * inv[None, :]
    cos = np.cos(ang).astype(np.float32)
    sin = np.sin(ang).astype(np.float32)
    tok = np.concatenate([cos, cos, -sin, sin], axis=1).astype(np.float32)
    return tok


def _host_inputs(inp):
    f32 = np.float32
    w_in = np.asarray(inp["w_in"][0], f32)
    offs = np.cumsum([0, 512, 512, 512, 4, 4, 512, 512, 384, 256, 32, 512])
    q_, k_, v_, i_, f_, mo_, mz_, cq_, ckv_, kr_, az_ = [slice(int(offs[n]), int(offs[n + 1])) for n in range(11)]
    krw = w_in[:, kr_]
    krperm = np.concatenate([krw[:, 16:32], krw[:, 0:16]], axis=1)
    w_all = np.ascontiguousarray(np.concatenate([w_in[:, k_], w_in[:, v_], w_in[:, ckv_], krw, krperm,
                                                 w_in[:, i_], w_in[:, f_]], axis=1))
    w_own = np.ascontiguousarray(np.concatenate([w_in[:, q_], w_in[:, k_], w_in[:, cq_], w_in[:, mo_],
                                                 w_in[:, mz_], w_in[:, az_]], axis=1))
    b_ada = np.asarray(inp["b_ada"][0], f32)
    b_adaT = np.ascontiguousarray(b_ada.reshape(24, 128).T)
    b_gate = np.ascontiguousarray(b_ada[2048:3072])
    w_uq = np.asarray(inp["mla_w_uq"][0], f32).reshape(384, 8, 96)
    w_uqp = np.zeros_like(w_uq)
    w_uqp[:, :, 64:80] = w_uq[:, :, 80:96]
    w_uqp[:, :, 80:96] = w_uq[:, :, 64:80]
    w_uk = np.asarray(inp["mla_w_uk"][0], f32).reshape(256, 8, 64)
    w_ukp = np.zeros((256, 8, 96), f32)
    w_ukp[:, :, 0:64] = w_uk
    w_ukT = np.ascontiguousarray(w_uk.transpose(2, 1, 0))
    b_if = np.concatenate([np.asarray(inp["ml_b_i"][0], f32), np.asarray(inp["ml_b_f"][0], f32)])
    rope_all = _rope_tables(np.arange(SEQ))
    rope_smp = np.ascontiguousarray(np.tile(_rope_tables(SEQ + np.arange(8)), (16, 1)))
    tri = np.tril(np.ones((128, 128), f32))
    caus = np.ascontiguousarray(tri.T)
    tok = np.arange(128)
    same_b = (tok[:, None] // 8) == (tok[None, :] // 8)
    caus_s = (same_b & (tok[:, None] <= tok[None, :])).astype(f32)
    bmask = np.zeros((128, 16, 128), f32)
    rmask = np.zeros((128, 16), f32)
    nmask = np.zeros((128, 16, 8, 8), f32)
    for b in range(16):
        bmask[:, b, b * 8:(b + 1) * 8] = 1.0
        rmask[b * 8:(b + 1) * 8, b] = 1.0
        for sk in range(8):
            nmask[b * 8 + sk, b, :, sk:] = 1.0
    segmask = np.ones((128, 512), f32)
    segmask[:, 0::128] = 0.0
    seg8 = np.ones((4, 128), f32)
    seg8[:, 0::8] = 0.0
    common = {
        "w_ada": np.asarray(inp["w_ada"][0], f32), "b_adaT": b_adaT, "b_gate": b_gate,
        "w_all": w_all, "w_own": w_own, "rope_all": rope_all, "rope_smp": rope_smp,
        "ident": np.eye(128, dtype=f32), "tstrict": np.triu(np.ones((64, 64), f32), 1), "segmask": segmask,
        "caus": caus, "caus_s": caus_s, "bmask": bmask, "rmask": rmask,
        "nmask": np.ascontiguousarray(nmask.reshape(128, 16, 64)), "i4": np.eye(4, dtype=f32), "seg8": seg8,
        "b_if_bc": np.ascontiguousarray(np.tile(b_if[None, :], (64, 1))),
        "b_if_col": np.ascontiguousarray(b_if.reshape(2, 4).T),
        "gkv": np.asarray(inp["mla_kv_norm"][0], f32),
        "gqT": np.ascontiguousarray(np.asarray(inp["mla_q_norm"][0], f32).reshape(3, 128).T),
        "ml_gn": np.asarray(inp["ml_gn"][0], f32), "ln_g": np.asarray(inp["ln_g"][0], f32),
        "ln_b": np.asarray(inp["ln_b"][0], f32),
        "w_uq": np.ascontiguousarray(w_uq), "w_uqp": w_uqp, "w_ukp": w_ukp, "w_ukT": w_ukT,
        "w_uv": np.asarray(inp["mla_w_uv"][0], f32), "w_out": np.asarray(inp["w_out"][0], f32),
        "cache_ckv": np.asarray(inp["cache_ckv"][0], f32), "cache_kr": np.asarray(inp["cache_krope"][0], f32),
    }
    ropeT_smp = np.zeros((128, 2, 128), f32)
    ropeT_smp[0:32, 0, :] = rope_smp[:, 0:32].T
    ropeT_smp[0:32, 1, :] = rope_smp[:, 32:64].T
    ropeT_smp[64:96] = ropeT_smp[0:32]
    common["ropeT_smp"] = ropeT_smp
    per = []
    xp = np.asarray(inp["x_prompt"], f32)
    xs = np.asarray(inp["x_sample"], f32)
    for j in range(8):
        b, jj = j // 4, j % 4
        own = [4 * g + jj for g in range(NOWN)]
        xb = xp[b].reshape(NBLK, 128, D)
        ropeo = rope_all.reshape(NBLK, 128, 64)[own].reshape(NOWN * 128, 64)
        ropeT_own = np.zeros((128, 2, NOWN * 128), f32)
        ropeT_own[64:96, 0, :] = ropeo[:, 0:32].T
        ropeT_own[64:96, 1, :] = ropeo[:, 32:64].T
        maskA = np.zeros((128, 4, 128), f32)
        for i in range(4):
            if i < jj:
                maskA[:, i, :] = 1.0
            elif i == jj:
                maskA[:, i, :] = caus
        selv = np.zeros((128, 4), f32)
        selv[:, jj] = 1.0
        sb = slice(16 * j, 16 * j + 16)
        cT = np.concatenate([np.asarray(inp["c_prompt"], f32)[b][:, None], np.asarray(inp["c_sample"], f32)[sb].T], axis=1)
        pt = np.asarray(inp["page_table"])[sb].astype(np.int32)
        ptab = np.ascontiguousarray(np.concatenate([pt.T, pt.T], axis=0))
        d = dict(common)
        d.update({
            "x_all": np.ascontiguousarray(xp[b]), "x_own": np.ascontiguousarray(xb[own].reshape(NOWN * 128, D)),
            "x_smp": np.ascontiguousarray(xs[sb].reshape(128, D)), "cT": np.ascontiguousarray(cT),
            "rope_own": np.ascontiguousarray(ropeo), "ropeT_own": ropeT_own, "sel": selv, "maskA": maskA,
            "state_C": np.ascontiguousarray(np.asarray(inp["state_C"][0], f32)[sb]),
            "state_nT": np.ascontiguousarray(np.asarray(inp["state_n"][0], f32)[sb].transpose(2, 0, 1)),
            "state_mT": np.ascontiguousarray(np.asarray(inp["state_m"][0], f32)[sb].T),
            "ptab": ptab,
        })
        per.append(d)
    return per


_CACHE = {}


def kernel(**inp):
    if "nc" not in _CACHE:
        _CACHE["nc"] = build()
    nc, used = _CACHE["nc"]
    per = _host_inputs(inp)
    in_maps = [{k: d[k] for k in used} for d in per]
    res = run_bass_kernel_spmd(nc, in_maps, core_ids=list(range(8)))
    R = res.results
    f32 = np.float32
    y_p = np.zeros((2, SEQ, D), f32)
    ckv_p = np.zeros((1, 2, SEQ, 256), f32)
    kr_p = np.zeros((1, 2, SEQ, 32), f32)
    C_p = np.zeros((1, 2, 4, 128, 128), f32)
    n_p = np.zeros((1, 2, 4, 128), f32)
    m_p = np.zeros((1, 2, 4), f32)
    y_s = np.zeros((128, 8, D), f32)
    ckv_s = np.zeros((1, 128, 8, 256), f32)
    kr_s = np.zeros((1, 128, 8, 32), f32)
    C_s = np.zeros((1, 128, 4, 128, 128), f32)
    n_s = np.zeros((1, 128, 4, 128), f32)
    m_s = np.zeros((1, 128, 4), f32)
    for j in range(8):
        b, jj = j // 4, j % 4
        r = R[j]
        own = [4 * g + jj for g in range(NOWN)]
        y_p[b].reshape(NBLK, 128, D)[own] = r["o_y"].reshape(NOWN, 128, D)
        ckv_p[0, b].reshape(NBLK, 128, 256)[own] = r["o_ckv"].reshape(NOWN, 128, 256)
        kr_p[0, b].reshape(NBLK, 128, 32)[own] = r["o_kr"].reshape(NOWN, 128, 32)
        if jj == 0:
            oc = r["o_C"]
            C_p[0, b] = oc[:, :, 0:128].transpose(1, 0, 2)
            n_p[0, b] = oc[:, :, 128].T
            m_p[0, b] = r["o_m"][0]
        sb = slice(16 * j, 16 * j + 16)
        y_s[sb] = r["o_ys"].reshape(16, 8, D)
        ckv_s[0, sb] = r["o_ckvs"].reshape(16, 8, 256)
        kr_s[0, sb] = r["o_krs"].reshape(16, 8, 32)
        C_s[0, sb] = r["o_Cs"].transpose(0, 2, 1, 3)
        n_s[0, sb] = r["o_ns"].transpose(1, 2, 0)
        m_s[0, sb] = r["o_ms"].T
    return (y_p, y_s, ckv_p, kr_p, C_p, n_p, m_p, ckv_s, kr_s, C_s, n_s, m_s)
```

```python
import contextlib
import numpy as np
import concourse.bass as bass
import concourse.mybir as mybir
from concourse.bass_utils import run_bass_kernel_spmd

F32 = mybir.dt.float32
BF16 = mybir.dt.bfloat16
I32 = mybir.dt.int32
AF = mybir.ActivationFunctionType
ALU = mybir.AluOpType
AX = mybir.AxisListType

ENGS = ("pe", "act", "dve", "pool", "sp")

D = 1024
KC = 8
SEQ = 8192
NBLK = 64
NOWN = 16
DK = 128
EPS = 1e-6
ALPHA = 2.0 ** 0.25
MLA_SCALE = 96.0 ** -0.5
NPOOL = 10240
NEG = -1.0e30


class Sched:
    def __init__(self, nc, stack):
        self.nc = nc
        self.stack = stack
        self.sems = {}
        self.count = {}
        self.ops = {e: [] for e in ENGS}
        self.waited = {e: {} for e in ENGS}
        self.writer = {}
        self.readers = {}
        self.tagmap = {}
        self.maxtags = 0

    def _sem(self, key):
        if key not in self.sems:
            self.sems[key] = self.stack.enter_context(self.nc.semaphore("s_" + key.replace(":", "_")))
            self.count[key] = 0
        return self.sems[key]

    def _deps(self, reads, writes):
        deps = {}

        def add(tok):
            if tok is None:
                return
            k, v = tok
            if deps.get(k, 0) < v:
                deps[k] = v

        for r in reads:
            add(self.writer.get(r))
        for w in writes:
            add(self.writer.get(w))
            for t in self.readers.get(w, ()):
                add(t)
        return deps

    def _commit(self, tok, reads, writes):
        for r in reads:
            self.readers.setdefault(r, []).append(tok)
        for w in writes:
            self.writer[w] = tok
            self.readers[w] = []

    def _waits(self, q, deps, n=None):
        waits = []
        for k, v in deps.items():
            if n is not None and k == q:
                if q == "pe" or v < n - 2:
                    continue
            if self.waited[q].get(k, 0) >= v:
                continue
            self.waited[q][k] = v
            waits.append((k, v))
        return waits

    def op(self, eng, fn, reads=(), writes=()):
        self._sem(eng)
        deps = self._deps(reads, writes)
        n = self.count[eng] + 1
        waits = self._waits(eng, deps, n)
        self.count[eng] = n
        self.ops[eng].append((waits, fn, (eng, 1)))
        self._commit((eng, n), reads, writes)

    def raw(self, q, tag, fn, reads=(), writes=(), nowaw=False):
        if tag not in self.tagmap:
            self.tagmap[tag] = "t%d" % len(self.tagmap)
            self.maxtags = max(self.maxtags, len(self.tagmap))
        key = "dma:" + self.tagmap[tag]
        self._sem(key)
        deps = self._deps(reads, writes)
        if nowaw:
            deps.pop(key, None)
        waits = self._waits(q, deps)
        self.count[key] += 16
        tok = (key, self.count[key])
        self.ops[q].append((waits, fn, (key, 16)))
        self._commit(tok, reads, writes)

    def dma(self, q, tag, out, in_, reads=(), writes=(), nowaw=False, **kw):
        self.raw(q, tag, lambda e, out=out, in_=in_, kw=kw: e.dma_start(out=out, in_=in_, **kw), reads, writes, nowaw)

    def barrier(self):
        for e in ENGS:
            waits = []
            for k, v in self.count.items():
                if v == 0 or k == e:
                    continue
                if self.waited[e].get(k, 0) >= v:
                    continue
                self.waited[e][k] = v
                waits.append((k, v))
            if waits:
                self.ops[e].append((waits, None, None))

    def emit(self):
        self.barrier()
        nc = self.nc
        ops = self.ops
        sems = self.sems
        self.ops = {e: [] for e in ENGS}
        self.tagmap = {}

        def run(e, lst):
            for waits, fn, inc in lst:
                for k, v in waits:
                    e.wait_ge(sems[k], v)
                if fn is not None:
                    fn(e).then_inc(sems[inc[0]], inc[1])

        with nc.Block() as block:
            @block.tensor
            def _(e):
                run(e, ops["pe"])

            @block.scalar
            def _(e):
                run(e, ops["act"])

            @block.vector
            def _(e):
                run(e, ops["dve"])

            @block.gpsimd
            def _(e):
                run(e, ops["pool"])

            @block.sync
            def _(e):
                run(e, ops["sp"])


IN_SPECS = {
    "x_all": ([SEQ, D], F32), "x_own": ([NOWN * 128, D], F32), "x_smp": ([128, D], F32),
    "cT": ([D, 17], F32), "w_ada": ([D, 3 * D], F32), "b_adaT": ([128, 24], F32), "b_gate": ([D], F32),
    "w_all": ([D, 1352], F32), "w_own": ([D, 2944], F32),
    "rope_all": ([SEQ, 64], F32), "rope_own": ([NOWN * 128, 64], F32), "rope_smp": ([128, 64], F32),
    "ropeT_own": ([128, 2, NOWN * 128], F32), "ropeT_smp": ([128, 2, 128], F32),
    "ident": ([128, 128], F32), "tstrict": ([64, 64], F32), "segmask": ([128, 512], F32),
    "sel": ([128, 4], F32), "maskA": ([128, 4, 128], F32), "caus": ([128, 128], F32),
    "caus_s": ([128, 128], F32), "bmask": ([128, 16, 128], F32), "rmask": ([128, 16], F32),
    "nmask": ([128, 16, 64], F32), "i4": ([4, 4], F32), "seg8": ([4, 128], F32),
    "b_if_bc": ([64, 8], F32), "b_if_col": ([4, 2], F32), "gkv": ([256], F32), "gqT": ([128, 3], F32),
    "ml_gn": ([512], F32), "ln_g": ([D], F32), "ln_b": ([D], F32),
    "w_uq": ([384, 8, 96], F32), "w_uqp": ([384, 8, 96], F32), "w_ukp": ([256, 8, 96], F32),
    "w_ukT": ([64, 8, 256], F32), "w_uv": ([256, 512], F32), "w_out": ([D, D], F32),
    "state_C": ([16, 4, 128, 128], F32), "state_nT": ([128, 16, 4], F32), "state_mT": ([4, 16], F32),
    "ptab": ([128, 16], I32), "cache_ckv": ([NPOOL, 128, 256], F32), "cache_kr": ([NPOOL, 128, 32], F32),
}
OUT_SPECS = {
    "o_y": ([NOWN * 128, D], F32), "o_ckv": ([NOWN * 128, 256], F32), "o_kr": ([NOWN * 128, 32], F32),
    "o_C": ([128, 4, 129], F32), "o_m": ([1, 4], F32),
    "o_ys": ([128, D], F32), "o_ckvs": ([128, 256], F32), "o_krs": ([128, 32], F32),
    "o_Cs": ([16, 128, 4, 128], F32), "o_ns": ([128, 16, 4], F32), "o_ms": ([4, 16], F32),
}

STAGE = 99


def build(stage=STAGE):
    nc = bass.Bass("TRN2", target_bir_lowering=False)
    dr = {}
    used_in = set()
    for k, (shp, dt) in IN_SPECS.items():
        dr[k] = (k, shp, dt)
    dram_cache = {}

    def DIN(name):
        if name not in dram_cache:
            _, shp, dt = dr[name]
            dram_cache[name] = nc.dram_tensor(name, shp, dt, kind="ExternalInput").ap()
            used_in.add(name)
        return dram_cache[name]

    outs = {k: nc.dram_tensor(k, shp, dt, kind="ExternalOutput").ap() for k, (shp, dt) in OUT_SPECS.items()}
    kscr = nc.dram_tensor("kscr", [NBLK, 128, 512], BF16, kind="Internal").ap()
    vscr = nc.dram_tensor("vscr", [NBLK, 128, 4, 129], BF16, kind="Internal").ap()

    with contextlib.ExitStack() as st:
        S = Sched(nc, st)
        op, dma = S.op, S.dma

        def T(stack, name, shape, dt):
            return stack.enter_context(nc.sbuf_tensor("sb_" + name, shape, dt))

        PS = [st.enter_context(nc.psum_tensor(f"ps{i}", [128, 512], F32)) for i in range(8)]

        def psf(i):
            return PS[i][:]

        def psb(i):
            return PS[i][:].bitcast(BF16)

        def bank(i):
            return f"ps{i}"

        def mm(out, lhsT, rhs, start, stop, reads, writes):
            op("pe", lambda e: e.matmul(out, lhsT, rhs, start=start, stop=stop, skip_group_check=True), reads, writes)

        def tr(out, in_, idn, reads, writes):
            op("pe", lambda e: e.transpose(out, in_, idn), reads, writes)

        def act(out, in_, func, reads, writes, scale=1.0, bias=0.0):
            op("act", lambda e: e.activation(out, in_, func, bias=bias, scale=scale), reads, writes)

        def tt(eng, out, in0, in1, alu, reads, writes):
            op(eng, lambda e: e.tensor_tensor(out, in0, in1, alu), reads, writes)

        def ts(eng, out, in0, s1, s2, op0, op1, reads, writes):
            if s2 is None:
                op(eng, lambda e: e.tensor_scalar(out, in0, s1, None, op0), reads, writes)
            else:
                op(eng, lambda e: e.tensor_scalar(out, in0, s1, s2, op0, op1), reads, writes)

        def stt(out, in0, sc, in1, op0, op1, reads, writes):
            op("dve", lambda e: e.scalar_tensor_tensor(out, in0, sc, in1, op0, op1), reads, writes)

        def cp(eng, out, in_, reads, writes):
            if eng == "act":
                op("act", lambda e: e.activation(out, in_, AF.Identity), reads, writes)
            else:
                op(eng, lambda e: e.tensor_copy(out, in_), reads, writes)

        def red(out, in_, alu, reads, writes, axis=AX.X):
            op("dve", lambda e: e.tensor_reduce(out, in_, axis, alu), reads, writes)

        def rsqrt_pool(out, in_, mhalf, reads, writes):
            op("act", lambda e: e.activation(out, in_, AF.Ln, bias=0.0, scale=1.0), reads, writes)
            op("act", lambda e: e.activation(out, out, AF.Exp, bias=0.0, scale=-0.5), list(writes), writes)

        def cast_w(stack, name, src, kc, n, tagq="pool", defer=None):
            t = T(stack, name, [128, kc, n], BF16)
            v = src.rearrange("(k p) n -> p k n", p=128)

            def load():
                c0 = 0
                while c0 < n:
                    c1 = min(n, c0 + 2048)
                    dma("pool", name, t[:, :, c0:c1], v[:, :, c0:c1], writes=[name], nowaw=True)
                    c0 = c1

            if defer is None:
                load()
            else:
                defer.append(load)
            return t

        ident = T(st, "ident", [128, 128], BF16)
        identf = T(st, "identf", [128, 128], F32)
        onesb = T(st, "onesb", [128, 128], BF16)
        onesf = T(st, "onesf", [128, 128], F32)
        mhalf = T(st, "mhalf", [128, 8], F32)
        mod = T(st, "mod", [128, 16, 17], F32)
        gate_p = T(st, "gate_p", [128, D], F32)
        gate_s = T(st, "gate_s", [128, D], F32)
        gkv_bc = T(st, "gkv_bc", [128, 256], F32)
        sel = T(st, "sel", [128, 4], F32)

        dma("pool", "c_ident", ident[:], DIN("ident"), writes=["ident"])
        dma("sp", "c_identf", identf[:], DIN("ident"), writes=["identf"])
        op("pool", lambda e: e.memset(onesb[:], 1.0), writes=["onesb"])
        op("pool", lambda e: e.memset(onesf[:], 1.0), writes=["onesf"])
        op("pool", lambda e: e.memset(mhalf[:], -0.5), writes=["mhalf"])
        dma("sp", "c_gkv", gkv_bc[:], DIN("gkv").partition_broadcast(128), writes=["gkv_bc"])
        dma("sp", "c_sel", sel[:], DIN("sel"), writes=["sel"])

        def phase0(extra_loads):
          with contextlib.ExitStack() as ph:
              cT_bf = T(ph, "cT_bf", [128, KC, 17], BF16)
              dma("pool", "cT", cT_bf[:], DIN("cT").rearrange("(k p) n -> p k n", p=128), writes=["cT_bf"])
              badaT = T(ph, "badaT", [128, 24], F32)
              dma("sp", "badaT", badaT[:], DIN("b_adaT"), writes=["badaT"])
              bgate = T(ph, "bgate", [128, D], F32)
              dma("sp", "bgate", bgate[:], DIN("b_gate").partition_broadcast(128), writes=["bgate"])
              crep_p = T(ph, "crep_p", [128, KC, 128], BF16)
              crep_s = T(ph, "crep_s", [128, KC, 128], BF16)
              cp("dve", crep_p[:], cT_bf[:, :, 0:1].to_broadcast([128, KC, 128]), ["cT_bf"], ["crep_p"])
              cp("dve", crep_s[:].rearrange("p k (b s) -> p k b s", s=8),
                 cT_bf[:, :, 1:17].unsqueeze(3).to_broadcast([128, KC, 16, 8]), ["cT_bf"], ["crep_s"])
              wa = [T(ph, f"wa{i}", [128, KC, 1024], BF16) for i in range(3)]
              wav = DIN("w_ada").rearrange("(k p) n -> p k n", p=128)
              for piece in range(3):
                  dma("pool", f"wa{piece}", wa[piece][:], wav[:, :, piece * 1024:(piece + 1) * 1024], writes=[f"wa{piece}"])
              for f in extra_loads:
                  f()
              for piece in range(3):
                  w = wa[piece]
                  wn = f"wa{piece}"
                  if piece < 2:
                      for nch in range(8):
                          col = (piece * 8 + nch) * 17
                          for kc in range(KC):
                              mm(psf(0)[:, col:col + 17], w[:, kc, nch * 128:(nch + 1) * 128], cT_bf[:, kc, :],
                                 kc == 0, kc == KC - 1, [wn, "cT_bf"], [bank(0)])
                  else:
                      for gi, (crep, gdst, gname) in enumerate(((crep_p, gate_p, "gate_p"), (crep_s, gate_s, "gate_s"))):
                          for half in range(2):
                              b = 1 + gi * 2 + half
                              for kc in range(KC):
                                  mm(psf(b), crep[:, kc, :], w[:, kc, half * 512:(half + 1) * 512],
                                     kc == 0, kc == KC - 1, [wn, "crep_p", "crep_s"], [bank(b)])
                              tt("dve", gdst[:, half * 512:(half + 1) * 512], psf(b), bgate[:, half * 512:(half + 1) * 512],
                                 ALU.add, [bank(b), "bgate"], [gname])
              tt("dve", mod[:], psf(0)[:, 0:272].rearrange("p (c n) -> p c n", n=17),
                 badaT[:, 0:16].unsqueeze(2).to_broadcast([128, 16, 17]), ALU.add, [bank(0), "badaT"], ["mod"])
              ts("dve", mod[:, 8:16, :], mod[:, 8:16, :], 1.0, None, ALU.add, None, ["mod"], ["mod"])
              S.emit()

        def make_hT(xbt, xbn, hTt, hTn, pbank, sample):
            for kc in range(KC):
                tr(psb(pbank)[:, kc * 128:(kc + 1) * 128], xbt[:, kc * 128:(kc + 1) * 128], ident[:],
                   [xbn, "ident"], [bank(pbank)])
            if not sample:
                tt("dve", hTt[:], psb(pbank).rearrange("p (k t) -> p k t", t=128),
                   mod[:, 8:16, 0:1].to_broadcast([128, KC, 128]), ALU.mult, [bank(pbank), "mod"], [hTn])
                tt("dve", hTt[:], hTt[:], mod[:, 0:8, 0:1].to_broadcast([128, KC, 128]), ALU.add,
                   [hTn, "mod"], [hTn])
            else:
                v4 = lambda a: a.rearrange("p k (b s) -> p k b s", s=8)
                tt("dve", v4(hTt[:]), v4(psb(pbank).rearrange("p (k t) -> p k t", t=128)),
                   mod[:, 8:16, 1:17].unsqueeze(3).to_broadcast([128, KC, 16, 8]), ALU.mult,
                   [bank(pbank), "mod"], [hTn])
                tt("pool", v4(hTt[:]), v4(hTt[:]),
                   mod[:, 0:8, 1:17].unsqueeze(3).to_broadcast([128, KC, 16, 8]), ALU.add,
                   [hTn, "mod"], [hTn])

        def tokgroup(pb, hTt, hTn, W, Wn, c0, n):
            for kc in range(KC):
                mm(psf(pb)[:, 0:n], hTt[:, kc, :], W[:, kc, c0:c0 + n], kc == 0, kc == KC - 1,
                   [hTn, Wn], [bank(pb)])

        def featgroup(pb, col, hTt, hTn, W, Wn, c0, m):
            for kc in range(KC):
                mm(psf(pb)[0:m, col:col + 128], W[:, kc, c0:c0 + m], hTt[:, kc, :], kc == 0, kc == KC - 1,
                   [hTn, Wn], [bank(pb)])

        def misc_post(pb, ropet, ropen, ckvn, ckvnn, krr, krrn, tmp):
            sq, ss, rstd, t1, t2 = tmp
            act(sq[:, 0:256], psf(pb)[:, 0:256], AF.Square, [bank(pb)], ["m_sq"])
            red(ss[:, 0:1], sq[:, 0:256], ALU.add, ["m_sq"], ["m_ss"])
            ts("dve", ss[:, 1:2], ss[:, 0:1], 1.0 / 256.0, EPS, ALU.mult, ALU.add, ["m_ss"], ["m_ss2"])
            rsqrt_pool(rstd[:, 0:1], ss[:, 1:2], mhalf[:, 0:1], ["m_ss2", "mhalf"], ["m_rstd"])
            stt(ckvn[:], psf(pb)[:, 0:256], rstd[:, 0:1], gkv_bc[:], ALU.mult, ALU.mult,
                [bank(pb), "m_rstd", "gkv_bc"], [ckvnn])
            tt("dve", t1[:], psf(pb)[:, 256:288], ropet[:, 0:32], ALU.mult, [bank(pb), ropen], ["m_t1"])
            tt("dve", t2[:], psf(pb)[:, 288:320], ropet[:, 32:64], ALU.mult, [bank(pb), ropen], ["m_t2"])
            tt("pool", krr[:], t1[:], t2[:], ALU.add, ["m_t1", "m_t2"], [krrn])

        def misc_tmp(stack, pfx):
            return (T(stack, pfx + "sq", [128, 256], F32), T(stack, pfx + "ss", [128, 2], F32),
                    T(stack, pfx + "rstd", [128, 1], F32), T(stack, pfx + "t1", [128, 32], F32),
                    T(stack, pfx + "t2", [128, 32], F32))

        def mlstm_intra(rn, tp, qT, kT, vaug, wlc, ec, maskT, x2_list, tmp):
            st_, xs, dn, hm = tmp
            for h in range(4):
                mm(psf(3)[:, h * 128:(h + 1) * 128], kT[:, h, :], qT[:, h, :], True, True,
                   [rn["kT"], rn["qT"]], [bank(3)])
            for h in range(4):
                stt(st_[:, h, :], psf(3)[:, h * 128:(h + 1) * 128], wlc[:, h:h + 1], maskT, ALU.mult, ALU.mult,
                    [bank(3), rn["wl"], rn["mask"]], [tp + "st"])
            for h in range(4):
                col = slice((h % 2) * 129, (h % 2) * 129 + 129)
                lst = x2_list(h)
                mm(psf(4 + h // 2)[:, col], st_[:, h, :], vaug[:, h, :], True, False,
                   [tp + "st", rn["vaug"]], [bank(4 + h // 2)])
                for n, (lh, rh, rd) in enumerate(lst):
                    mm(psf(4 + h // 2)[:, col], lh, rh, False, n == len(lst) - 1, rd, [bank(4 + h // 2)])
            for hb in range(2):
                cp("act", xs[:, 2 * hb:2 * hb + 2, :], psf(4 + hb)[:, 0:258].rearrange("p (h v) -> p h v", v=129),
                   [bank(4 + hb)], [tp + "xs"])
            act(dn[:, 0:4], xs[:, :, 128], AF.Abs, [tp + "xs"], [tp + "dn"])
            tt("dve", dn[:, 0:4], dn[:, 0:4], ec, ALU.max, [tp + "dn", rn["e"]], [tp + "dn"])
            op("dve", lambda e: e.reciprocal(dn[:, 4:8], dn[:, 0:4]), [tp + "dn"], [tp + "dn2"])
            tt("dve", hm[:], xs[:, :, 0:128], dn[:, 4:8].unsqueeze(2).to_broadcast([128, 4, 128]), ALU.mult,
               [tp + "xs", tp + "dn2"], [tp + "hm"])
            return hm

        def intra_tmp(stack, pfx):
            return (T(stack, pfx + "st", [128, 4, 128], BF16), T(stack, pfx + "xs", [128, 4, 129], F32),
                    T(stack, pfx + "dn", [128, 8], F32), T(stack, pfx + "hm", [128, 4, 128], F32))

        def ml_post(rn, tp, hm, sig_mo, silu_mz, gn_bc, ydst, yname, tmp):
            sqh, st4 = tmp
            v3 = lambda a: a.rearrange("p (h v) -> p h v", v=128)
            tt("dve", hm[:], hm[:], v3(sig_mo), ALU.mult, [tp + "hm", rn["sig_mo"]], [tp + "hm"])
            red(st4[:, 0:4], hm[:], ALU.add, [tp + "hm"], [tp + "g_s1"])
            act(sqh[:], hm[:], AF.Square, [tp + "hm"], [tp + "sqh"])
            red(st4[:, 4:8], sqh[:], ALU.add, [tp + "sqh"], [tp + "g_s2"])
            ts("dve", st4[:, 8:12], st4[:, 0:4], 1.0 / 128.0, None, ALU.mult, None, [tp + "g_s1"], [tp + "g_mean"])
            tt("dve", st4[:, 12:16], st4[:, 8:12], st4[:, 8:12], ALU.mult, [tp + "g_mean"], [tp + "g_msq"])
            stt(st4[:, 16:20], st4[:, 4:8], 1.0 / 128.0, st4[:, 12:16], ALU.mult, ALU.subtract,
                [tp + "g_s2", tp + "g_msq"], [tp + "g_var"])
            ts("dve", st4[:, 16:20], st4[:, 16:20], EPS, None, ALU.add, None, [tp + "g_var"], [tp + "g_var"])
            rsqrt_pool(st4[:, 20:24], st4[:, 16:20], None, [tp + "g_var"], [tp + "g_rstd"])
            tt("dve", hm[:], hm[:], st4[:, 8:12].unsqueeze(2).to_broadcast([128, 4, 128]), ALU.subtract,
               [tp + "hm", tp + "g_mean"], [tp + "hm"])
            tt("dve", hm[:], hm[:], st4[:, 20:24].unsqueeze(2).to_broadcast([128, 4, 128]), ALU.mult,
               [tp + "hm", tp + "g_rstd"], [tp + "hm"])
            tt("dve", hm[:], hm[:], v3(gn_bc), ALU.mult, [tp + "hm", "gn_bc"], [tp + "hm"])
            tt("dve", v3(ydst), hm[:], v3(silu_mz), ALU.mult, [tp + "hm", rn["silu_mz"]], [yname])

        def ln_steps(ipfx, pfx, pbs, xf, gate, lng, lnb, odst, oname, tmp):
            z, sqz, st1 = tmp
            for half in range(2):
                hs = slice(half * 512, (half + 1) * 512)
                tt("dve", z[:, hs], psf(pbs[half]), gate[:, hs], ALU.mult, [bank(pbs[half]), "gate"], [pfx + "z"])
            yield
            stt(z[:], xf, ALPHA, z[:], ALU.mult, ALU.add, [ipfx + "xf", pfx + "z"], [pfx + "z"])
            red(st1[:, 0:1], z[:], ALU.add, [pfx + "z"], [pfx + "l_s1"])
            act(sqz[:], z[:], AF.Square, [pfx + "z"], [pfx + "sqz"])
            red(st1[:, 1:2], sqz[:], ALU.add, [pfx + "sqz"], [pfx + "l_s2"])
            ts("dve", st1[:, 2:3], st1[:, 0:1], 1.0 / D, None, ALU.mult, None, [pfx + "l_s1"], [pfx + "l_mean"])
            tt("dve", st1[:, 3:4], st1[:, 2:3], st1[:, 2:3], ALU.mult, [pfx + "l_mean"], [pfx + "l_msq"])
            stt(st1[:, 4:5], st1[:, 1:2], 1.0 / D, st1[:, 3:4], ALU.mult, ALU.subtract,
                [pfx + "l_s2", pfx + "l_msq"], [pfx + "l_var"])
            ts("dve", st1[:, 4:5], st1[:, 4:5], EPS, None, ALU.add, None, [pfx + "l_var"], [pfx + "l_var"])
            rsqrt_pool(st1[:, 5:6], st1[:, 4:5], None, [pfx + "l_var"], [pfx + "l_rstd"])
            stt(z[:], z[:], st1[:, 2:3], lng[:], ALU.subtract, ALU.mult, [pfx + "z", pfx + "l_mean", "lng"], [pfx + "z"])
            stt(z[:], z[:], st1[:, 5:6], lnb[:], ALU.mult, ALU.add, [pfx + "z", pfx + "l_rstd", "lnb"], [pfx + "z"])
            dma("sp", pfx + "yout", odst, z[:], reads=[pfx + "z"], writes=[oname])
            yield

        def ln_out(*args):
            for _ in ln_steps(*args):
                pass

        if stage >= 3:
          sst = contextlib.ExitStack()
          ys_ml = T(sst, "ys_ml", [128, 512], BF16)
          silu_azT = T(sst, "silu_azT", [128, 8, 128], BF16)
          qlatT = T(sst, "qlatT", [128, 2, 8, 128], BF16)
          qropeT = T(sst, "qropeT", [128, 8, 128], BF16)
          ckvnb_s = T(sst, "ckvnb_s", [128, 256], BF16)
          ckvnT_s = T(sst, "ckvnT_s", [128, 2, 128], BF16)
          krT_s = T(sst, "krT_s", [128, 128], BF16)
          xf_s = T(sst, "xf_s", [128, D], F32)
          qr4 = T(sst, "qr4", [128, 4, 8, 128], BF16)
          op("pool", lambda e: e.memset(qr4[:], 0.0), writes=["qr4"])
          op("pool", lambda e: e.memset(krT_s[:], 0.0), writes=["krT_s"])
          qT = T(sst, "s_qT", [128, 4, 128], BF16)
          kT = T(sst, "s_kT", [128, 4, 128], BF16)
          ktok = T(sst, "s_ktok", [128, 4, 128], BF16)
          vaug = T(sst, "s_vaug", [128, 4, 129], BF16)
          gs = T(sst, "s_gs", [128, 8], F32)
          sig_mo = T(sst, "s_sig_mo", [128, 512], BF16)
          silu_mz = T(sst, "s_silu_mz", [128, 512], BF16)
          with contextlib.ExitStack() as ph:
            WOWN = DIN("w_own")
            sa1_loads = []
            Wq5 = cast_w(ph, "sWq5", WOWN[:, 0:1024], KC, 1024, defer=sa1_loads)
            Wa = cast_w(ph, "sWa", DIN("w_all"), KC, 1352, defer=sa1_loads)
            Wg5 = cast_w(ph, "sWg5", WOWN[:, 1408:2944], KC, 1536, defer=sa1_loads)
            Wcq = cast_w(ph, "sWcq", WOWN[:, 1024:1408], KC, 384, defer=sa1_loads)
            phase0(sa1_loads)
            gqT = T(ph, "s_gqT", [128, 3], F32)
            dma("sp", "s_gqT", gqT[:], DIN("gqT"), writes=["s_gqT"])
            wq_st = T(ph, "s_wq_st", [128, 3, 768], F32)
            wuq = T(ph, "s_wuq", [128, 3, 768], BF16)
            wuqp = T(ph, "s_wuqp", [128, 3, 768], BF16)
            for nm, dst, dn in (("w_uq", wuq, "s_wuq"), ("w_uqp", wuqp, "s_wuqp")):
                dma("sp", "s_wq_st", wq_st[:], DIN(nm).rearrange("(c p) h n -> p c (h n)", p=128), writes=["s_wq_st"])
                for cc in range(3):
                    ts("dve", dst[:, cc, :], wq_st[:, cc, :], gqT[:, cc:cc + 1], None, ALU.mult, None,
                       ["s_wq_st", "s_gqT"], [dn])
            wukT = T(ph, "s_wukT", [64, 8, 256], BF16)
            dma("pool", "s_wukT", wukT[:], DIN("w_ukT"), writes=["s_wukT"])
            mhq = T(ph, "s_mhq", [128, 128], F32)
            op("pool", lambda e: e.memset(mhq[:], -0.5), writes=["s_mhq"])
            xb = T(ph, "s_xb", [128, D], BF16)
            hT = T(ph, "s_hT", [128, KC, 128], BF16)
            rope = T(ph, "s_rope", [128, 64], F32)
            ropeT = T(ph, "s_ropeT", [128, 2, 128], F32)
            ckvn = T(ph, "s_ckvn", [128, 256], F32)
            krr = T(ph, "s_krr", [128, 32], F32)
            krb = T(ph, "s_krb", [128, 32], BF16)
            sqc = T(ph, "s_sqc", [128, 3, 128], BF16)
            rq = T(ph, "s_rq", [128, 128], F32)
            rq2 = T(ph, "s_rq2", [128, 128], F32)
            cqn = T(ph, "s_cqn", [128, 3, 128], BF16)
            qnT = T(ph, "s_qnT", [64, 8, 128], BF16)
            tq1 = T(ph, "s_tq1", [32, 8, 128], F32)
            tq2 = T(ph, "s_tq2", [32, 8, 128], F32)
            mt = misc_tmp(ph, "sp")
            op("pool", lambda e: e.memset(vaug[:], 1.0), writes=["s_vaug"])
            dma("pool", "s_xb", xb[:], DIN("x_smp"), writes=["s_xb"])
            dma("sp", "s_xf", xf_s[:], DIN("x_smp"), writes=["s_xf"])
            dma("sp", "s_rope", rope[:], DIN("rope_smp"), writes=["s_rope"])
            dma("sp", "s_ropeT", ropeT[:], DIN("ropeT_smp"), writes=["s_ropeT"])
            make_hT(xb, "s_xb", hT, "s_hT", 0, True)
            v3 = lambda a: a.rearrange("p (h t) -> p h t", t=128)
            for h in range(4):
                featgroup(2, h * 128, hT, "s_hT", Wq5, "sWq5", h * 128, 128)
            for h in range(4):
                featgroup(3, h * 128, hT, "s_hT", Wq5, "sWq5", 512 + h * 128, 128)
            cp("act", qT[:].rearrange("p h t -> p (h t)"), psf(2), [bank(2)], ["s_qT"])
            act(kT[:].rearrange("p h t -> p (h t)"), psf(3), AF.Identity, [bank(3)], ["s_kT"], scale=DK ** -0.5)
            tokgroup(4, hT, "s_hT", Wa, "sWa", 0, 512)
            act(ktok[:].rearrange("p h t -> p (h t)"), psf(4), AF.Identity, [bank(4)], ["s_ktok"], scale=DK ** -0.5)
            tokgroup(5, hT, "s_hT", Wa, "sWa", 512, 512)
            cp("dve", vaug[:, :, 0:128], psf(5).rearrange("p (h v) -> p h v", v=128), [bank(5)], ["s_vaug"])
            tokgroup(6, hT, "s_hT", Wa, "sWa", 1024, 328)
            misc_post(6, rope, "s_rope", ckvn, "s_ckvn", krr, "s_krr", mt)
            cp("act", gs[:], psf(6)[:, 320:328], [bank(6)], ["s_gs"])
            dma("sp", "o_ckvs", outs["o_ckvs"], ckvn[:], reads=["s_ckvn"], writes=["out_ckvs"])
            dma("sp", "o_krs", outs["o_krs"], krr[:], reads=["s_krr"], writes=["out_krs"])
            cp("pool", ckvnb_s[:], ckvn[:], ["s_ckvn"], ["ckvnb_s"])
            cp("pool", krb[:], krr[:], ["s_krr"], ["s_krb"])
            for cc in range(2):
                tr(psb(7)[:, cc * 128:(cc + 1) * 128], ckvnb_s[:, cc * 128:(cc + 1) * 128], ident[:],
                   ["ckvnb_s", "ident"], [bank(7)])
            tr(psb(7)[0:32, 256:384], krb[:], ident[:], ["s_krb", "ident"], [bank(7)])
            cp("act", ckvnT_s[:].rearrange("p c t -> p (c t)"), psb(7)[:, 0:256], [bank(7)], ["ckvnT_s"])
            cp("act", krT_s[0:32, :], psb(7)[0:32, 256:384], [bank(7)], ["krT_s"])
            tokgroup(2, hT, "s_hT", Wg5, "sWg5", 0, 512)
            act(sig_mo[:], psf(2), AF.Sigmoid, [bank(2)], ["s_sig_mo"])
            tokgroup(3, hT, "s_hT", Wg5, "sWg5", 512, 512)
            act(silu_mz[:], psf(3), AF.Silu, [bank(3)], ["s_silu_mz"])
            for h in range(8):
                featgroup(4 + h // 4, (h % 4) * 128, hT, "s_hT", Wg5, "sWg5", 1024 + h * 64, 64)
            for hb in range(2):
                act(silu_azT[0:64, 4 * hb:4 * hb + 4, :], v3(psf(4 + hb)[0:64, :]), AF.Silu, [bank(4 + hb)], ["silu_azT"])
            for cc in range(3):
                featgroup(6, cc * 128, hT, "s_hT", Wcq, "sWcq", cc * 128, 128)
            act(sqc[:], psf(6)[:, 0:384].rearrange("p (c t) -> p c t", t=128), AF.Square, [bank(6)], ["s_sqc"])
            for cc in range(3):
                mm(psf(7)[:, 0:128], onesb[:], sqc[:, cc, :], cc == 0, cc == 2, ["onesb", "s_sqc"], [bank(7)])
            ts("dve", rq[:], psf(7)[:, 0:128], 1.0 / 384.0, EPS, ALU.mult, ALU.add, [bank(7)], ["s_rq"])
            rsqrt_pool(rq2[:], rq[:], mhq[:], ["s_rq", "s_mhq"], ["s_rq2"])
            tt("dve", cqn[:], psf(6)[:, 0:384].rearrange("p (c t) -> p c t", t=128),
               rq2[:].unsqueeze(1).to_broadcast([128, 3, 128]), ALU.mult, [bank(6), "s_rq2"], ["s_cqn"])
            for h in range(8):
                col = slice((h % 4) * 128, (h % 4 + 1) * 128)
                for cc in range(3):
                    mm(psf(2 + h // 4)[0:64, col], wuq[:, cc, h * 96:h * 96 + 64], cqn[:, cc, :], cc == 0, cc == 2,
                       ["s_wuq", "s_cqn"], [bank(2 + h // 4)])
                for cc in range(3):
                    mm(psf(4 + h // 4)[0:32, col], wuq[:, cc, h * 96 + 64:h * 96 + 96], cqn[:, cc, :], cc == 0, cc == 2,
                       ["s_wuq", "s_cqn"], [bank(4 + h // 4)])
                for cc in range(3):
                    mm(psf(6 + h // 4)[0:32, col], wuqp[:, cc, h * 96 + 64:h * 96 + 96], cqn[:, cc, :], cc == 0, cc == 2,
                       ["s_wuqp", "s_cqn"], [bank(6 + h // 4)])
            for hb in range(2):
                hs = slice(4 * hb, 4 * hb + 4)
                cp("act", qnT[:, hs, :], v3(psf(2 + hb)[0:64, :]), [bank(2 + hb)], ["s_qnT"])
                tt("dve", tq1[:, hs, :], v3(psf(4 + hb)[0:32, :]), ropeT[0:32, 0:1, :].to_broadcast([32, 4, 128]), ALU.mult,
                   [bank(4 + hb), "s_ropeT"], ["s_tq1"])
                tt("dve", tq2[:, hs, :], v3(psf(6 + hb)[0:32, :]), ropeT[0:32, 1:2, :].to_broadcast([32, 4, 128]), ALU.mult,
                   [bank(6 + hb), "s_ropeT"], ["s_tq2"])
            tt("pool", qropeT[0:32, :, :], tq1[:], tq2[:], ALU.add, ["s_tq1", "s_tq2"], ["qropeT"])
            for i4 in range(4):
                dma("sp", "qr4", qr4[i4 * 32:(i4 + 1) * 32, i4, :, :], qropeT[0:32, :, :], reads=["qropeT", "qr4"], writes=["qr4"])
            for cc in range(2):
                for h in range(8):
                    pb = 2 + cc * 2 + h // 4
                    mm(psf(pb)[:, (h % 4) * 128:(h % 4 + 1) * 128], wukT[:, h, cc * 128:(cc + 1) * 128], qnT[:, h, :],
                       True, True, ["s_wukT", "s_qnT"], [bank(pb)])
                for hb in range(2):
                    cp("act" if hb == 0 else "dve", qlatT[:, cc, 4 * hb:4 * hb + 4, :], v3(psf(2 + cc * 2 + hb)),
                       [bank(2 + cc * 2 + hb)], ["qlatT"])
            S.emit()
          with contextlib.ExitStack() as ph:
            gn_bc = T(ph, "s_gn_bc", [128, 512], F32)
            dma("sp", "s_gn_bc", gn_bc[:], DIN("ml_gn").partition_broadcast(128), writes=["gn_bc"])
            caus_s = T(ph, "s_caus", [128, 128], F32)
            dma("sp", "s_caus", caus_s[:], DIN("caus_s"), writes=["s_caus"])
            bmask = T(ph, "s_bmask", [128, 16, 128], BF16)
            dma("pool", "s_bmask", bmask[:], DIN("bmask"), writes=["s_bmask"])
            rmask = T(ph, "s_rmask", [128, 16], F32)
            dma("sp", "s_rmask", rmask[:], DIN("rmask"), writes=["s_rmask"])
            seg8 = T(ph, "s_seg8", [4, 128], F32)
            dma("sp", "s_seg8", seg8[:], DIN("seg8"), writes=["s_seg8"])
            bcol = T(ph, "s_bcol", [4, 2], F32)
            dma("sp", "s_bcol", bcol[:], DIN("b_if_col"), writes=["s_bcol"])
            i4t = T(ph, "s_i4", [4, 4], F32)
            dma("sp", "s_i4", i4t[:], DIN("i4"), writes=["s_i4"])
            m0T = T(ph, "s_m0T", [4, 16], F32)
            dma("sp", "s_m0T", m0T[:], DIN("state_mT"), writes=["s_m0T"])
            n0T = T(ph, "s_n0T", [128, 16, 4], F32)
            dma("sp", "s_n0T", n0T[:], DIN("state_nT"), writes=["s_n0T"])
            kts = T(ph, "s_kts", [128, 4, 128], BF16)
            it = intra_tmp(ph, "s_")
            gt = (T(ph, "s_sqh", [128, 4, 128], F32), T(ph, "s_st4", [128, 24], F32))
            gf = T(ph, "s_gf", [4, 12, 128], F32)
            g16 = T(ph, "s_g16", [4, 8, 16], F32)
            aldg = T(ph, "s_aldg", [4, 4, 16], F32)
            albc_s = T(ph, "s_albc", [128, 4, 16], F32)
            cols = T(ph, "s_cols", [128, 12], F32)
            tr(psf(0)[0:4, 0:128], gs[:, 0:4], identf[:], ["s_gs", "identf"], [bank(0)])
            tr(psf(0)[0:4, 128:256], gs[:, 4:8], identf[:], ["s_gs", "identf"], [bank(0)])
            ts("dve", gf[:, 0, :], psf(0)[0:4, 0:128], bcol[:, 0:1], None, ALU.add, None, [bank(0), "s_bcol"], ["s_ig"])
            ts("dve", gf[:, 1, :], psf(0)[0:4, 128:256], bcol[:, 1:2], None, ALU.add, None, [bank(0), "s_bcol"], ["s_lf"])
            act(gf[:, 1, :], gf[:, 1, :], AF.Exp, ["s_lf"], ["s_lf"], scale=-1.0)
            act(gf[:, 1, :], gf[:, 1, :], AF.Ln, ["s_lf"], ["s_lf"], bias=1.0)
            op("dve", lambda e: e.tensor_tensor_scan(gf[:, 2, :], seg8[:], gf[:, 1, :], 0.0, ALU.mult, ALU.add),
               ["s_seg8", "s_lf"], ["s_L"])
            tt("dve", gf[:, 3, :], gf[:, 0, :], gf[:, 2, :], ALU.add, ["s_ig", "s_L"], ["s_G"])
            b8 = lambda a: a.rearrange("p (b s) -> p b s", s=8)
            red(g16[:, 0, :], b8(gf[:, 3, :]), ALU.max, ["s_G"], ["s_gmax"])
            tt("dve", g16[:, 1, :], g16[:, 0, :], m0T[:], ALU.max, ["s_gmax", "s_m0T"], ["s_R"])
            bcR = g16[:, 1, :].unsqueeze(2).to_broadcast([4, 16, 8])
            tt("dve", b8(gf[:, 4, :]), b8(gf[:, 3, :]), bcR, ALU.subtract, ["s_G", "s_R"], ["s_wl"])
            act(gf[:, 4, :], gf[:, 4, :], AF.Exp, ["s_wl"], ["s_wl"])
            tt("dve", b8(gf[:, 5, :]), b8(gf[:, 2, :]), bcR, ALU.subtract, ["s_L", "s_R"], ["s_e"])
            act(gf[:, 5, :], gf[:, 5, :], AF.Exp, ["s_e"], ["s_e"])
            tt("dve", g16[:, 2, :], m0T[:], g16[:, 1, :], ALU.subtract, ["s_m0T", "s_R"], ["s_al"])
            act(g16[:, 2, :], g16[:, 2, :], AF.Exp, ["s_al"], ["s_al"])
            tt("dve", g16[:, 3, :], g16[:, 1, :], b8(gf[:, 2, :])[:, :, 7], ALU.subtract, ["s_R", "s_L"], ["s_mnew"])
            dma("sp", "o_ms", outs["o_ms"], g16[:, 3, :], reads=["s_mnew"], writes=["out_ms"])
            cp("dve", b8(gf[:, 6, :]), g16[:, 2, :].unsqueeze(2).to_broadcast([4, 16, 8]), ["s_al"], ["s_alx"])
            for n, (src, rn) in enumerate(((4, "s_wl"), (5, "s_e"), (6, "s_alx"))):
                tr(psf(1)[:, n * 4:(n + 1) * 4], gf[:, src, :], identf[0:4, 0:4], [rn, "identf"], [bank(1)])
            cp("dve", cols[:], psf(1)[:, 0:12], [bank(1)], ["s_cols", "s_wlc", "s_ec", "s_alc"])
            wlc_s, ec_s, alc_s = cols[:, 0:4], cols[:, 4:8], cols[:, 8:12]
            tt("dve", aldg[:], g16[:, 2, :].unsqueeze(1).to_broadcast([4, 4, 16]),
               i4t[:].unsqueeze(2).to_broadcast([4, 4, 16]), ALU.mult, ["s_al", "s_i4"], ["s_aldg"])
            mm(psf(0)[:, 256:320], onesf[0:4, :], aldg[:].rearrange("p h b -> p (h b)"), True, True,
               ["onesf", "s_aldg"], [bank(0)])
            cp("dve", albc_s[:], psf(0)[:, 256:320].rearrange("p (h b) -> p h b", b=16), [bank(0)], ["s_albc"])
            C0all = T(ph, "s_C0all", [128, 16, 4, 129], F32)
            C0bf = T(ph, "s_C0bf", [128, 16, 4, 129], BF16)
            vm = T(ph, "s_vm", [128, 16, 516], BF16)
            qm = [T(ph, f"s_qm{i}", [128, 16, 128], BF16) for i in range(2)]
            Cn = [T(ph, f"s_Cn{i}", [128, 4, 129], F32) for i in range(2)]
            nnew = T(ph, "s_nnew", [128, 16, 4], F32)
            sC = DIN("state_C")
            c0regs = [f"s_C0a{q}" for q in range(4)]
            for q in range(4):
                dma("sp", c0regs[q], C0all[:, 4 * q:4 * q + 4, :, 0:128], sC[4 * q:4 * q + 4].rearrange("b h k v -> k b h v"),
                    writes=[c0regs[q]])
            cp("dve", C0all[:, :, :, 128], n0T[:, :, :], ["s_n0T"] + c0regs, c0regs)
            for b in range(16):
                tt("dve", C0bf[:, b, :, :], C0all[:, b, :, :], albc_s[:, :, b:b + 1].to_broadcast([128, 4, 129]), ALU.mult,
                   [c0regs[b // 4], "s_albc"], ["s_C0bf"])
            tt("pool", kts[:], ktok[:], cols[:, 0:4].unsqueeze(2).to_broadcast([128, 4, 128]), ALU.mult,
               ["s_ktok", "s_cols"], ["s_kts"])
            tt("pool", vm[:], vaug[:].rearrange("p h v -> p (h v)").unsqueeze(1).to_broadcast([128, 16, 516]),
               rmask[:].unsqueeze(2).to_broadcast([128, 16, 516]), ALU.mult, ["s_vaug", "s_rmask"], ["s_vm"])

            def x2_s(h):
                i = h % 2
                tt("pool", qm[i][:], qT[:, h:h + 1, :].to_broadcast([128, 16, 128]), bmask[:], ALU.mult,
                   ["s_qT", "s_bmask"], [f"s_qm{i}"])
                return [(qm[i][:, b, :], C0bf[:, b, h, :], [f"s_qm{i}", "s_C0bf"]) for b in range(16)]

            rn_s = dict(qT="s_qT", kT="s_kT", vaug="s_vaug", wl="s_cols", e="s_cols", mask="s_caus",
                        sig_mo="s_sig_mo", silu_mz="s_silu_mz")
            hm = mlstm_intra(rn_s, "s_", qT, kT, vaug, wlc_s, ec_s, caus_s[:], x2_s, it)
            ml_post(rn_s, "s_", hm, sig_mo[:], silu_mz[:], gn_bc[:], ys_ml[:], "s_y", gt)
            for b in range(16):
                i = b % 2
                for h in range(4):
                    pb = 2 + 2 * i + h // 2
                    col = slice((h % 2) * 129, (h % 2) * 129 + 129)
                    mm(psf(pb)[:, col], kts[:, h, :], vm[:, b, h * 129:(h + 1) * 129], True, True,
                       ["s_kts", "s_vm"], [bank(pb)])
                for h in range(4):
                    pb = 2 + 2 * i + h // 2
                    col = slice((h % 2) * 129, (h % 2) * 129 + 129)
                    stt(Cn[i][:, h, :], C0all[:, b, h, :], albc_s[:, h, b:b + 1], psf(pb)[:, col], ALU.mult, ALU.add,
                        [c0regs[b // 4], "s_albc", bank(pb)], [f"s_Cn{i}"])
                dma("sp", f"o_Cs{i}", outs["o_Cs"][b], Cn[i][:, :, 0:128], reads=[f"s_Cn{i}"], writes=["out_Cs"])
                cp("act", nnew[:, b, :], Cn[i][:, :, 128], [f"s_Cn{i}"], ["s_nnew"])
            dma("sp", "o_ns", outs["o_ns"], nnew[:], reads=["s_nnew"], writes=["out_ns"])
            S.emit()
          omT = T(sst, "b_omT", [64, 8, 128], F32)
          with contextlib.ExitStack() as ph:
            wuv_s = cast_w(ph, "b_wuv", DIN("w_uv"), 2, 512)
            nmask = T(ph, "b_nmask", [128, 16, 64], BF16)
            dma("pool", "b_nmask", nmask[:], DIN("nmask"), writes=["b_nmask"])
            ptab = T(ph, "b_ptab", [128, 16], I32)
            dma("sp", "b_ptab", ptab[:], DIN("ptab"), writes=["b_ptab"])
            idx = T(ph, "b_idx", [128, 16], I32)
            idx8 = T(ph, "b_idx8", [128, 16, 8], I32)
            ts("dve", idx[0:64, :], ptab[0:64, :], 2.0, None, ALU.mult, None, ["b_ptab"], ["b_idx"])
            ts("dve", idx[64:128, :], ptab[64:128, :], 2.0, 1.0, ALU.mult, ALU.add, ["b_ptab"], ["b_idx"])
            for tc in range(8):
                ts("dve", idx8[:, :, tc], idx[:], 8.0, float(tc), ALU.mult, ALU.add, ["b_idx"], ["b_idx8"])
            ckv_view = DIN("cache_ckv").rearrange("p (a t) c -> (p a) (t c)", t=8)
            kr_view = DIN("cache_kr").rearrange("p (a t) r -> (p a) (t r)", a=2)
            NG = 3
            cg = [T(ph, f"b_cg{i}", [128, 64 * 256], BF16) for i in range(NG)]
            krg = [T(ph, f"b_krg{i}", [128, 2048], BF16) for i in range(NG)]
            ckT = [T(ph, f"b_ckT{i}", [128, 1024], BF16) for i in range(2)]
            krTt = [T(ph, f"b_krTt{i}", [128, 128], BF16) for i in range(2)]
            PTs = [T(ph, f"b_PTs{i}", [128, 65, 64], BF16) for i in range(2)]
            rs = T(ph, "b_rs", [128, 64], F32)
            onT = T(ph, "b_onT", [128, 2, 64], BF16)

            def gather(b):
                i = b % NG
                for tc in range(8):
                    S.raw("pool", f"b_cg{i}",
                          lambda e, i=i, tc=tc, b=b: e.indirect_dma_start(
                              out=cg[i][:, tc * 2048:(tc + 1) * 2048], out_offset=None, in_=ckv_view,
                              in_offset=bass.IndirectOffsetOnAxis(ap=idx8[:, b, tc:tc + 1], axis=0)),
                          reads=["b_idx8"], writes=[f"b_cg{i}"], nowaw=(tc > 0))
                S.raw("pool", f"b_krg{i}",
                      lambda e, i=i, b=b: e.indirect_dma_start(
                          out=krg[i][:], out_offset=None, in_=kr_view,
                          in_offset=bass.IndirectOffsetOnAxis(ap=idx[:, b:b + 1], axis=0)),
                      reads=["b_idx"], writes=[f"b_krg{i}"])

            def score_steps(b):
                i = b % NG
                P = PTs[b % 2]
                pn = f"b_PTs{b % 2}"
                qcols = slice(b * 8, (b + 1) * 8)

                def sb_T(tg):
                    k2 = tg % 2
                    for j in range(4):
                        t = 4 * tg + j
                        for cc in range(2):
                            tr(psb(k2)[:, (j * 2 + cc) * 128:(j * 2 + cc + 1) * 128],
                               cg[i][:, t * 256 + cc * 128:t * 256 + (cc + 1) * 128], ident[:],
                               [f"b_cg{i}", "ident"], [bank(k2)])
                    tr(psb(2 + k2)[:, 0:128], krg[i][:, tg * 128:(tg + 1) * 128], ident[:],
                       [f"b_krg{i}", "ident"], [bank(2 + k2)])
                    cp("act" if tg % 2 == 0 else "dve", ckT[k2][:], psb(k2), [bank(k2)], [f"b_ckT{k2}"])
                    cp("dve" if tg % 2 == 0 else "act", krTt[k2][:, 0:128], psb(2 + k2)[:, 0:128], [bank(2 + k2)],
                       [f"b_krTt{k2}"])

                def sb_S(tg):
                    k2 = tg % 2
                    sbk = 4 + (tg // 2) % 2
                    for j in range(4):
                        t = 4 * tg + j
                        oc = slice((t % 8) * 64, (t % 8 + 1) * 64)
                        mm(psf(sbk)[:, oc], ckT[k2][:, (j * 2) * 128:(j * 2 + 1) * 128], qlatT[:, 0, :, qcols],
                           True, False, [f"b_ckT{k2}", "qlatT"], [bank(sbk)])
                        mm(psf(sbk)[:, oc], ckT[k2][:, (j * 2 + 1) * 128:(j * 2 + 2) * 128], qlatT[:, 1, :, qcols],
                           False, False, [f"b_ckT{k2}", "qlatT"], [bank(sbk)])
                        mm(psf(sbk)[:, oc], krTt[k2][:, 0:128], qr4[:, j, :, qcols],
                           False, True, [f"b_krTt{k2}", "qr4"], [bank(sbk)])
                    if tg % 2 == 1:
                        t0 = 4 * (tg - 1)
                        act(P[:, t0:t0 + 8, :].rearrange("p t q -> p (t q)"), psf(sbk), AF.Exp, [bank(sbk)], [pn],
                            scale=MLA_SCALE)

                sb_T(0)
                yield
                for tg in range(16):
                    if tg + 1 < 16:
                        sb_T(tg + 1)
                    sb_S(tg)
                    yield
                mm(psf(4)[:, 0:64], ckvnT_s[:, 0, :], qlatT[:, 0, :, qcols], True, False, ["ckvnT_s", "qlatT"], [bank(4)])
                mm(psf(4)[:, 0:64], ckvnT_s[:, 1, :], qlatT[:, 1, :, qcols], False, False, ["ckvnT_s", "qlatT"], [bank(4)])
                mm(psf(4)[:, 0:64], krT_s[:, :], qr4[:, 0, :, qcols], False, True, ["krT_s", "qr4"], [bank(4)])
                act(P[:, 64, :], psf(4)[:, 0:64], AF.Exp, [bank(4)], [pn], scale=MLA_SCALE)
                tt("dve", P[:, 64, :], P[:, 64, :], nmask[:, b, :], ALU.mult, [pn, "b_nmask"], [pn])
                yield

            def pv_steps(b):
                i = b % NG
                P = PTs[b % 2]
                pn = f"b_PTs{b % 2}"
                qcols = slice(b * 8, (b + 1) * 8)
                n = 0
                for cc in range(3):
                    for t in range(65):
                        if cc < 2:
                            lh = cg[i][:, t * 256 + cc * 128:t * 256 + (cc + 1) * 128] if t < 64 else ckvnb_s[:, cc * 128:(cc + 1) * 128]
                            mm(psf(6)[:, cc * 64:(cc + 1) * 64], lh, P[:, t, :], t == 0, t == 64,
                               [f"b_cg{i}", "ckvnb_s", pn], [bank(6)])
                        else:
                            mm(psf(6)[:, 128:192], onesb[:], P[:, t, :], t == 0, t == 64, ["onesb", pn], [bank(6)])
                        n += 1
                        if n % 12 == 0:
                            yield
                op("dve", lambda e: e.reciprocal(rs[:], psf(6)[:, 128:192]), [bank(6)], ["b_rs"])
                tt("dve", onT[:], psf(6)[:, 0:128].rearrange("p (c q) -> p c q", q=64),
                   rs[:].unsqueeze(1).to_broadcast([128, 2, 64]), ALU.mult, [bank(6), "b_rs"], ["b_onT"])
                yield
                for h in range(8):
                    for cc in range(2):
                        mm(psf(7)[0:64, h * 8:(h + 1) * 8], wuv_s[:, cc, h * 64:(h + 1) * 64], onT[:, cc, h * 8:(h + 1) * 8],
                           cc == 0, cc == 1, ["b_wuv", "b_onT"], [bank(7)])
                cp("act", omT[:, :, qcols], psf(7)[0:64, 0:64].rearrange("p (h s) -> p h s", s=8), [bank(7)], ["b_omT"])
                yield

            def drain(*gens):
                live = list(gens)
                while live:
                    for gobj in list(live):
                        try:
                            next(gobj)
                        except StopIteration:
                            live.remove(gobj)

            gather(0)
            gather(1)
            drain(score_steps(0))
            for b in range(16):
                if b + 2 < 16:
                    gather(b + 2)
                if b + 1 < 16:
                    drain(score_steps(b + 1), pv_steps(b))
                else:
                    drain(pv_steps(b))
            S.emit()
          with contextlib.ExitStack() as ph:
            wout_ml = cast_w(ph, "b_wout_ml", DIN("w_out")[0:512, :], 4, 1024)
            wout_mla = T(ph, "b_wout_mla", [64, 8, 1024], BF16)
            dma("pool", "b_wout_mla", wout_mla[:], DIN("w_out")[512:1024, :].rearrange("(h v) n -> v h n", v=64),
                writes=["b_wout_mla"])
            lng = T(ph, "b_lng", [128, D], F32)
            lnb = T(ph, "b_lnb", [128, D], F32)
            dma("sp", "b_lng", lng[:], DIN("ln_g").partition_broadcast(128), writes=["lng"])
            dma("sp", "b_lnb", lnb[:], DIN("ln_b").partition_broadcast(128), writes=["lnb"])
            ymlaT = T(ph, "b_ymlaT", [64, 8, 128], BF16)
            yT_ml = T(ph, "b_yT_ml", [128, 4, 128], BF16)
            lt = (T(ph, "b_z", [128, D], F32), T(ph, "b_sqz", [128, D], F32), T(ph, "b_st1", [128, 8], F32))
            tt("dve", ymlaT[:], omT[:], silu_azT[0:64, :, :], ALU.mult, ["b_omT", "silu_azT"], ["b_ymlaT"])
            for fc in range(4):
                tr(psb(0)[:, fc * 128:(fc + 1) * 128], ys_ml[:, fc * 128:(fc + 1) * 128], ident[:], ["s_y", "ident"], [bank(0)])
            cp("act", yT_ml[:].rearrange("p k t -> p (k t)"), psb(0)[:, 0:512], [bank(0)], ["b_yT_ml"])
            for half in range(2):
                hs = slice(half * 512, (half + 1) * 512)
                for fc in range(4):
                    mm(psf(2 + half), yT_ml[:, fc, :], wout_ml[:, fc, hs], fc == 0, False, ["b_yT_ml", "b_wout_ml"], [bank(2 + half)])
                for h in range(8):
                    mm(psf(2 + half), ymlaT[:, h, :], wout_mla[:, h, hs], False, h == 7, ["b_ymlaT", "b_wout_mla"], [bank(2 + half)])
            ln_out("b_", "b_", (2, 3), xf_s[:], gate_s, lng, lnb, outs["o_ys"], "out_ys", lt)
            S.emit()
          sst.close()
        wl_own = T(st, "wl_own", [128, 4, NOWN], F32)
        e_own = T(st, "e_own", [128, 4, NOWN], F32)
        al_own = T(st, "al_own", [128, 4, NOWN], F32)
        wlT = T(st, "wlT", [128, 4, NBLK], F32)
        eT = T(st, "eT", [128, 4, NBLK], F32)
        albc = T(st, "albc", [128, 4, NBLK], F32)
        attn = T(st, "attn", [128, NOWN, 512], BF16)
        Cown = T(st, "Cown", [128, NOWN, 4, 129], BF16)
        st14 = contextlib.ExitStack()
        ckvT_all = T(st14, "ckvT_all", [128, 2, SEQ], BF16)
        KRp = T(st14, "KRp", [128, NBLK, 96], BF16)
        GT = T(st14, "GT", [128, NBLK, 8], F32)
        op("pool", lambda e: e.memset(KRp[:], 0.0), writes=["KRp"])
        with contextlib.ExitStack() as ph:
            Wa = cast_w(ph, "Wa", DIN("w_all"), KC, 1352)
            xb = [T(ph, f"xb{i}", [128, D], BF16) for i in range(4)]
            hT = [T(ph, f"hT{i}", [128, KC, 128], BF16) for i in range(2)]
            rope = [T(ph, f"rope{i}", [128, 64], F32) for i in range(4)]
            ktok = [T(ph, f"ktok{i}", [128, 512], BF16) for i in range(2)]
            vaug = [T(ph, f"vaug{i}", [128, 4, 129], BF16) for i in range(2)]
            ckvn = [T(ph, f"ckvn{i}", [128, 256], F32) for i in range(2)]
            ckvb = [T(ph, f"ckvb{i}", [128, 256], BF16) for i in range(2)]
            krr = [T(ph, f"krr{i}", [128, 32], F32) for i in range(2)]
            mt = misc_tmp(ph, "p1")
            for i in range(2):
                op("pool", lambda e, i=i: e.memset(vaug[i][:], 1.0), writes=[f"vaug{i}"])
            xall = DIN("x_all")
            ropeall = DIN("rope_all")
            def p1_L(blk):
                b4 = blk % 4
                rows = slice(blk * 128, (blk + 1) * 128)
                dma("pool", f"xb{b4}", xb[b4][:], xall[rows, :], writes=[f"xb{b4}"])
                dma("sp", f"rope{b4}", rope[b4][:], ropeall[rows, :], writes=[f"rope{b4}"])

            def p1_T(blk):
                b = blk % 2
                b4 = blk % 4
                make_hT(xb[b4], f"xb{b4}", hT[b], f"hT{b}", b, False)

            def p1_M(blk):
                b = blk % 2
                tokgroup(2, hT[b], f"hT{b}", Wa, "Wa", 0, 512)
                act(ktok[b][:], psf(2), AF.Identity, [bank(2)], [f"ktok{b}"], scale=DK ** -0.5)
                tokgroup(3, hT[b], f"hT{b}", Wa, "Wa", 512, 512)
                cp("dve", vaug[b][:, :, 0:128], psf(3).rearrange("p (h v) -> p h v", v=128), [bank(3)], [f"vaug{b}"])
                tokgroup(4 + b, hT[b], f"hT{b}", Wa, "Wa", 1024, 328)
                dma("sp", f"spk{b}", kscr[blk], ktok[b][:], reads=[f"ktok{b}"], writes=[f"kscr{blk}"])
                dma("sp", f"spv{b}", vscr[blk], vaug[b][:], reads=[f"vaug{b}"], writes=[f"vscr{blk}"])
                misc_post(4 + b, rope[blk % 4], f"rope{blk % 4}", ckvn[b], f"ckvn{b}", krr[b], f"krr{b}", mt)
                cp("act", GT[:, blk, :], psf(4 + b)[:, 320:328], [bank(4 + b)], ["GT"])
                cp("pool", ckvb[b][:], ckvn[b][:], [f"ckvn{b}"], [f"ckvb{b}"])
                cp("pool", KRp[:, blk, 64:96], krr[b][:], [f"krr{b}"], ["KRp"])

            def p1_CT(blk):
                b = blk % 2
                rows = slice(blk * 128, (blk + 1) * 128)
                for cc in range(2):
                    tr(psb(6 + b)[:, cc * 128:(cc + 1) * 128], ckvb[b][:, cc * 128:(cc + 1) * 128], ident[:],
                       [f"ckvb{b}", "ident"], [bank(6 + b)])
                cp("act", ckvT_all[:, :, rows], psb(6 + b)[:, 0:256].rearrange("p (c t) -> p c t", t=128),
                   [bank(6 + b)], ["ckvT_all"])

            for blk in range(3):
                p1_L(blk)
            p1_T(0)
            for blk in range(NBLK):
                if blk + 3 < NBLK:
                    p1_L(blk + 3)
                if blk + 1 < NBLK:
                    p1_T(blk + 1)
                p1_M(blk)
                if blk >= 1:
                    p1_CT(blk - 1)
            p1_CT(NBLK - 1)
            S.emit()

        def p2_steps(ph):
                tstrict = T(ph, "tstrict", [64, 64], F32)
                segm = T(ph, "segm", [64, 512], F32)
                bif = T(ph, "bif", [64, 8], F32)
                dma("sp", "tstrict", tstrict[:], DIN("tstrict"), writes=["tstrict"])
                dma("sp", "segm", segm[:], DIN("segmask")[0:64, :], writes=["segm"])
                dma("sp", "bif", bif[:], DIN("b_if_bc"), writes=["bif"])
                ig = T(ph, "ig", [64, 4, 128], F32)
                lf = T(ph, "lf", [64, 4, 128], F32)
                Ll = T(ph, "Ll", [64, 4, 128], F32)
                Lg = T(ph, "Lg", [64, 4, 128], F32)
                G = T(ph, "G", [64, 4, 128], F32)
                wl = T(ph, "wl", [64, 4, 128], F32)
                ee = T(ph, "ee", [64, 4, 128], F32)
                sm = T(ph, "sm", [64, 8, 4], F32)
                smT = T(ph, "smT", [4, 4, 64], F32)
                aldiag = T(ph, "aldiag", [64, 4, 64], F32)
                for j in range(8):
                    pb = 0 if j < 4 else 1
                    tr(psf(pb)[0:64, (j % 4) * 128:(j % 4 + 1) * 128], GT[:, :, j], identf[:], ["GT", "identf"], [bank(pb)])
                v3 = lambda a: a.rearrange("p (h s) -> p h s", s=128)
                tt("dve", ig[:], v3(psf(0)[0:64, :]), bif[:, 0:4].unsqueeze(2).to_broadcast([64, 4, 128]), ALU.add,
                   [bank(0), "bif"], ["ig"])
                tt("dve", lf[:], v3(psf(1)[0:64, :]), bif[:, 4:8].unsqueeze(2).to_broadcast([64, 4, 128]), ALU.add,
                   [bank(1), "bif"], ["lf"])
                yield
                act(lf[:], lf[:], AF.Exp, ["lf"], ["lf"], scale=-1.0)
                act(lf[:], lf[:], AF.Ln, ["lf"], ["lf"], bias=1.0)
                op("dve", lambda e: e.tensor_tensor_scan(Ll[:].rearrange("p h s -> p (h s)"), segm[:],
                                                           lf[:].rearrange("p h s -> p (h s)"), 0.0, ALU.mult, ALU.add),
                   ["segm", "lf"], ["Ll"])
                yield
                cp("dve", sm[:, 0, :], Ll[:, :, 127], ["Ll"], ["sm0"])
                mm(psf(2)[0:64, 0:4], tstrict[:], sm[:, 0, :], True, True, ["tstrict", "sm0"], [bank(2)])
                cp("dve", sm[:, 1, :], psf(2)[0:64, 0:4], [bank(2)], ["sm1"])
                tt("dve", Lg[:], Ll[:], sm[:, 1, :].unsqueeze(2).to_broadcast([64, 4, 128]), ALU.add, ["Ll", "sm1"], ["Lg"])
                tt("dve", G[:], ig[:], Lg[:], ALU.add, ["ig", "Lg"], ["G"])
                yield
                red(sm[:, 2, :], G[:], ALU.max, ["G"], ["sm2"])
                tr(psf(3)[0:4, 0:64], sm[:, 2, :], identf[0:64, 0:64], ["sm2", "identf"], [bank(3)])
                cp("dve", smT[:, 0, :], psf(3)[0:4, 0:64], [bank(3)], ["smT0"])
                op("dve", lambda e: e.tensor_tensor_scan(smT[:, 1, :], smT[:, 0, :], smT[:, 0, :], NEG, ALU.max, ALU.max),
                   ["smT0"], ["smT1"])
                op("dve", lambda e: e.memset(smT[:, 2, 0:1], NEG), [], ["smT2a"])
                cp("dve", smT[:, 2, 1:64], smT[:, 1, 0:63], ["smT1", "smT2a"], ["smT2"])
                tr(psf(4)[0:64, 0:4], smT[:, 1, :], identf[0:4, 0:4], ["smT1", "identf"], [bank(4)])
                tr(psf(4)[0:64, 4:8], smT[:, 2, :], identf[0:4, 0:4], ["smT2", "identf"], [bank(4)])
                cp("dve", sm[:, 3:5, :], psf(4)[0:64, 0:8].rearrange("p (a h) -> p a h", h=4), [bank(4)], ["sm34"])
                bcR = sm[:, 3, :].unsqueeze(2).to_broadcast([64, 4, 128])
                tt("dve", wl[:], G[:], bcR, ALU.subtract, ["G", "sm34"], ["wl"])
                act(wl[:], wl[:], AF.Exp, ["wl"], ["wl"])
                tt("dve", ee[:], Lg[:], bcR, ALU.subtract, ["Lg", "sm34"], ["ee"])
                act(ee[:], ee[:], AF.Exp, ["ee"], ["ee"])
                yield
                tt("dve", sm[:, 5, :], sm[:, 4, :], sm[:, 3, :], ALU.subtract, ["sm34"], ["sm5"])
                act(sm[:, 5, :], sm[:, 5, :], AF.Exp, ["sm5"], ["sm5"])
                tt("dve", sm[:, 6, :], sm[:, 3, :], Lg[:, :, 127], ALU.subtract, ["sm34", "Lg"], ["sm6"])
                dma("sp", "o_m", outs["o_m"], sm[63:64, 6, :], reads=["sm6"], writes=["o_m"])
                for h in range(4):
                    tr(psf(5)[:, h * 64:(h + 1) * 64], wl[:, h, :], identf[0:64, 0:64], ["wl", "identf"], [bank(5)])
                    tr(psf(6)[:, h * 64:(h + 1) * 64], ee[:, h, :], identf[0:64, 0:64], ["ee", "identf"], [bank(6)])
                cp("dve", wlT[:], psf(5)[:, 0:256].rearrange("p (h c) -> p h c", c=64), [bank(5)], ["wlT"])
                cp("act", eT[:], psf(6)[:, 0:256].rearrange("p (h c) -> p h c", c=64), [bank(6)], ["eT"])
                yield
                tt("dve", aldiag[:], sm[:, 5, :].unsqueeze(2).to_broadcast([64, 4, 64]),
                   identf[0:64, 0:64].unsqueeze(1).to_broadcast([64, 4, 64]), ALU.mult, ["sm5", "identf"], ["aldiag"])
                mm(psf(7)[:, 0:256], onesf[0:64, :], aldiag[:].rearrange("p h c -> p (h c)"), True, True,
                   ["onesf", "aldiag"], [bank(7)])
                cp("dve", albc[:], psf(7)[:, 0:256].rearrange("p (h c) -> p h c", c=64), [bank(7)], ["albc"])
                yield
                for src, dst, dn in ((wlT, wl_own, "wl_own"), (eT, e_own, "e_own"), (albc, al_own, "al_own")):
                    sv = src[:].rearrange("p h (g i) -> p h g i", i=4)
                    ts("dve", dst[:], sv[:, :, :, 0], sel[:, 0:1], None, ALU.mult, None, ["wlT", "eT", "albc", "sel"], [dn])
                    for i in range(1, 4):
                        stt(dst[:], sv[:, :, :, i], sel[:, i:i + 1], dst[:], ALU.mult, ALU.add, ["wlT", "eT", "albc", "sel", dn], [dn])
                yield
        if stage >= 2:
          with contextlib.ExitStack() as ph:
            WOWN = DIN("w_own")
            qaT = T(ph, "qaT", [128, 8, NOWN * 128], BF16)
            wukp = cast_w(ph, "wukp", DIN("w_ukp").rearrange("c h n -> c (h n)"), 2, 768)
            wuv = cast_w(ph, "wuv", DIN("w_uv"), 2, 512)
            maskA = T(ph, "maskA", [128, 512], BF16)
            dma("pool", "maskA", maskA[:], DIN("maskA").rearrange("p i q -> p (i q)"), writes=["maskA"])
            ph_outer = ph
            ph = contextlib.ExitStack()
            Wcq = cast_w(ph, "Wcq", WOWN[:, 1024:1408], KC, 384)
            gqT = T(ph, "gqT", [128, 3], F32)
            dma("sp", "gqT", gqT[:], DIN("gqT"), writes=["gqT"])
            wq_st = T(ph, "wq_st", [128, 3, 768], F32)
            wuq = T(ph, "wuq", [128, 3, 768], BF16)
            wuqp = T(ph, "wuqp", [128, 3, 768], BF16)
            for nm, dst, dn in (("w_uq", wuq, "wuq"), ("w_uqp", wuqp, "wuqp")):
                dma("sp", "wq_st", wq_st[:], DIN(nm).rearrange("(c p) h n -> p c (h n)", p=128), writes=["wq_st"])
                for cc in range(3):
                    ts("dve", dst[:, cc, :], wq_st[:, cc, :], gqT[:, cc:cc + 1], None, ALU.mult, None,
                       ["wq_st", "gqT"], [dn])
            xb = [T(ph, f"q_xb{i}", [128, D], BF16) for i in range(4)]
            hT = [T(ph, f"q_hT{i}", [128, KC, 128], BF16) for i in range(3)]
            ropeT = [T(ph, f"q_ropeT{i}", [128, 2, 128], F32) for i in range(4)]
            sqc = T(ph, "sqc", [128, 3, 128], BF16)
            rq = T(ph, "rq", [128, 128], F32)
            rq2 = T(ph, "rq2", [128, 128], F32)
            cqn = [T(ph, f"cqn{i}", [128, 3, 128], BF16) for i in range(2)]
            tq1 = T(ph, "tq1", [128, 4, 128], F32)
            tq2 = T(ph, "tq2", [128, 4, 128], F32)
            xown = DIN("x_own")
            ropeTo = DIN("ropeT_own")
            v3 = lambda a: a.rearrange("p (h t) -> p h t", t=128)

            def q_L(g):
                b4 = g % 4
                rows = slice(g * 128, (g + 1) * 128)
                dma("pool", f"q_xb{b4}", xb[b4][:], xown[rows, :], writes=[f"q_xb{b4}"])
                dma("sp", f"q_ropeT{b4}", ropeT[b4][:], ropeTo[:, :, rows], writes=[f"q_ropeT{b4}"])

            def q_T(g):
                make_hT(xb[g % 4], f"q_xb{g % 4}", hT[g % 3], f"q_hT{g % 3}", g % 2, False)

            def q_A(g):
                b = g % 2
                for cc in range(3):
                    featgroup(2, cc * 128, hT[g % 3], f"q_hT{g % 3}", Wcq, "Wcq", cc * 128, 128)
                act(sqc[:], psf(2)[:, 0:384].rearrange("p (c t) -> p c t", t=128), AF.Square, [bank(2)], ["sqc"])
                for cc in range(3):
                    mm(psf(3)[:, 0:128], onesb[:], sqc[:, cc, :], cc == 0, cc == 2, ["onesb", "sqc"], [bank(3)])
                ts("dve", rq[:], psf(3)[:, 0:128], 1.0 / 384.0, EPS, ALU.mult, ALU.add, [bank(3)], ["rq"])
                rsqrt_pool(rq2[:], rq[:], None, ["rq"], ["rq2"])
                tt("dve", cqn[b][:], psf(2)[:, 0:384].rearrange("p (c t) -> p c t", t=128),
                   rq2[:].unsqueeze(1).to_broadcast([128, 3, 128]), ALU.mult, [bank(2), "rq2"], [f"cqn{b}"])

            def q_B(g):
                b = g % 2
                b4 = g % 4
                rows = slice(g * 128, (g + 1) * 128)
                for h in range(8):
                    col = slice((h % 4) * 128, (h % 4 + 1) * 128)
                    for cc in range(3):
                        mm(psf(4 + h // 4)[0:96, col], wuq[:, cc, h * 96:(h + 1) * 96], cqn[b][:, cc, :], cc == 0, cc == 2,
                           ["wuq", f"cqn{b}"], [bank(4 + h // 4)])
                    for cc in range(3):
                        mm(psf(6 + h // 4)[0:96, col], wuqp[:, cc, h * 96:(h + 1) * 96], cqn[b][:, cc, :], cc == 0, cc == 2,
                           ["wuqp", f"cqn{b}"], [bank(6 + h // 4)])
                for hb in range(2):
                    cp("act", qaT[0:64, 4 * hb:4 * hb + 4, rows], v3(psf(4 + hb)[0:64, :]), [bank(4 + hb)], ["qaT"])
                    tt("dve", tq1[64:96], v3(psf(4 + hb)[64:96, :]),
                       ropeT[b4][64:96, 0:1, :].to_broadcast([32, 4, 128]), ALU.mult,
                       [bank(4 + hb), f"q_ropeT{b4}"], ["tq1"])
                    tt("dve", tq2[64:96], v3(psf(6 + hb)[64:96, :]),
                       ropeT[b4][64:96, 1:2, :].to_broadcast([32, 4, 128]), ALU.mult,
                       [bank(6 + hb), f"q_ropeT{b4}"], ["tq2"])
                    tt("pool", qaT[64:96, 4 * hb:4 * hb + 4, rows], tq1[64:96], tq2[64:96], ALU.add,
                       ["tq1", "tq2"], ["qaT"])

            p2 = p2_steps(ph)
            for g in range(3):
                q_L(g)
            q_T(0)
            q_T(1)
            q_A(0)
            next(p2, None)
            for g in range(NOWN):
                if g + 3 < NOWN:
                    q_L(g + 3)
                if g + 2 < NOWN:
                    q_T(g + 2)
                if g + 1 < NOWN:
                    q_A(g + 1)
                q_B(g)
                next(p2, None)
            for _ in p2:
                pass
            S.emit()
            ph.close()
            ph = ph_outer
            KT = T(ph, "KT", [128, 2, SEQ], BF16)
            Vt = T(ph, "Vt", [128, NBLK, 2, 65], BF16)
            Cst = T(ph, "Cst", [128, 4, 129], F32)
            Cown32 = T(ph, "Cown32", [128, 4, 129], F32)
            kin = [T(ph, f"kin{i}", [128, 4, 128], BF16) for i in range(2)]
            vin = [T(ph, f"vin{i}", [128, 4, 129], BF16) for i in range(2)]
            kt = [T(ph, f"kt{i}", [128, 4, 128], BF16) for i in range(2)]
            op("pool", lambda e: e.memset(Cst[:], 0.0), writes=["Cst"])

            def rec_steps():
                for c in range(NBLK):
                    b = c % 2
                    g, i = c // 4, c % 4
                    dma("sp", f"kin{b}", kin[b][:].rearrange("p h d -> p (h d)"), kscr[c], reads=[f"kscr{c}"], writes=[f"kin{b}"])
                    dma("sp", f"vin{b}", vin[b][:], vscr[c], reads=[f"vscr{c}"], writes=[f"vin{b}"])
                    tt("pool", kt[b][:], kin[b][:], wlT[:, :, c:c + 1].to_broadcast([128, 4, 128]), ALU.mult,
                       [f"kin{b}", "wlT"], [f"kt{b}"])
                    yield
                    for h in range(4):
                        pb = h // 2
                        col = slice((h % 2) * 129, (h % 2) * 129 + 129)
                        mm(psf(pb)[:, col], kt[b][:, h, :], vin[b][:, h, :], True, True, [f"kt{b}", f"vin{b}"], [bank(pb)])
                    if i == 0:
                        ts("dve", Cown32[:], Cst[:], sel[:, 0:1], None, ALU.mult, None, ["Cst", "sel"], ["Cown32"])
                    else:
                        stt(Cown32[:], Cst[:], sel[:, i:i + 1], Cown32[:], ALU.mult, ALU.add, ["Cst", "sel", "Cown32"], ["Cown32"])
                    if i == 3:
                        tt("pool", Cown[:, g, :, :], Cown32[:], al_own[:, :, g:g + 1].to_broadcast([128, 4, 129]), ALU.mult,
                           ["Cown32", "al_own"], ["Cown"])
                    for h in range(4):
                        pb = h // 2
                        col = slice((h % 2) * 129, (h % 2) * 129 + 129)
                        stt(Cst[:, h, :], Cst[:, h, :], albc[:, h, c:c + 1], psf(pb)[:, col], ALU.mult, ALU.add,
                            ["Cst", "albc", bank(pb)], ["Cst"])
                    yield
                dma("sp", "o_C", outs["o_C"], Cst[:], reads=["Cst"], writes=["o_C"])
                yield

            rec = rec_steps()

            def rec_tick():
                try:
                    next(rec)
                except StopIteration:
                    pass

            PT = [T(ph, f"PT{i}", [128, 512], BF16) for i in range(3)]
            rinv = T(ph, "rinv", [128, 2], F32)
            op("pool", lambda e: e.memset(Vt[:], 1.0), writes=["Vt"])
            cnt = 0
            for p in range(4):
                for kb4 in range(16):
                    for hh in range(2):
                        h = 2 * p + hh
                        pb = 2 + (kb4 % 2) * 2 + hh
                        keys = slice(kb4 * 512, (kb4 + 1) * 512)
                        for cc in range(2):
                            mm(psf(pb)[0:96, :], wukp[:, cc, h * 96:(h + 1) * 96], ckvT_all[:, cc, keys], cc == 0, False,
                               ["wukp", "ckvT_all"], [bank(pb)])
                        for i in range(4):
                            mm(psf(pb)[0:96, i * 128:(i + 1) * 128], KRp[:, kb4 * 4 + i, :], ident[:], False, i == 3,
                               ["KRp", "ident"], [bank(pb)])
                        cp("act" if hh == 0 else "dve", KT[0:96, hh, keys], psf(pb)[0:96, :], [bank(pb)], [f"KT{hh}"])
                for kb in range(NBLK):
                    pb = 6 + (kb // 4) % 2
                    for cc in range(2):
                        mm(psf(pb)[:, (kb % 4) * 128:(kb % 4 + 1) * 128], ckvT_all[:, cc, kb * 128:(kb + 1) * 128],
                           wuv[:, cc, p * 128:(p + 1) * 128], cc == 0, cc == 1, ["ckvT_all", "wuv"], [bank(pb)])
                    if kb % 4 == 3:
                        cp("dve", Vt[:, kb - 3:kb + 1, :, 0:64],
                           psf(pb)[:, 0:512].rearrange("p (k h v) -> p k h v", h=2, v=64), [bank(pb)], ["Vt"])
                items = [(g, hh, kg) for g in range(NOWN) for hh in range(2) for kg in range(g + 1)]

                def emit_S(n):
                    g, hh, kg = items[n]
                    h = 2 * p + hh
                    sbk = 2 + n % 4
                    for i in range(4):
                        kb = 4 * kg + i
                        mm(psf(sbk)[:, i * 128:(i + 1) * 128], KT[0:96, hh, kb * 128:(kb + 1) * 128],
                           qaT[0:96, h, g * 128:(g + 1) * 128], True, True, [f"KT{hh}", "qaT"], [bank(sbk)])

                emit_S(0)
                for n, (g, hh, kg) in enumerate(items):
                    if n % 8 == 4:
                        rec_tick()
                    ob = 6 + g % 2
                    sbk = 2 + n % 4
                    pt = PT[n % 3]
                    ptn = f"PT{n % 3}"
                    act(pt[:], psf(sbk), AF.Exp, [bank(sbk)], [ptn], scale=MLA_SCALE)
                    if kg == g:
                        tt("dve", pt[:], pt[:], maskA[:], ALU.mult, [ptn, "maskA"], [ptn])
                    if n + 1 < len(items):
                        emit_S(n + 1)
                    for i in range(4):
                        kb = 4 * kg + i
                        mm(psf(ob)[:, hh * 65:(hh + 1) * 65], pt[:, i * 128:(i + 1) * 128], Vt[:, kb, hh, :],
                           kg == 0 and i == 0, kg == g and i == 3, [ptn, "Vt"], [bank(ob)])
                    if hh == 1 and kg == g:
                        ov = psf(ob)[:, 0:130].rearrange("p (h v) -> p h v", v=65)
                        op("dve", lambda e, ov=ov: e.reciprocal(rinv[:], ov[:, :, 64]), [bank(ob)], ["rinv"])
                        tt("dve", attn[:, g, p * 128:(p + 1) * 128].rearrange("p (h v) -> p h v", v=64), ov[:, :, 0:64],
                           rinv[:].unsqueeze(2).to_broadcast([128, 2, 64]), ALU.mult, [bank(ob), "rinv"], ["attn"])
            for _ in range(200):
                rec_tick()
            S.emit()
        st14.close()
        if stage >= 2:
          with contextlib.ExitStack() as ph:
            WOWN = DIN("w_own")
            WALL = DIN("w_all")
            Wq5 = cast_w(ph, "Wq5", WOWN[:, 0:1024], KC, 1024)
            Wg5 = cast_w(ph, "Wg5", WOWN[:, 1408:2944], KC, 1536)
            Wv5 = cast_w(ph, "Wv5", WALL[:, 512:1352], KC, 840)
            wout = cast_w(ph, "wout", DIN("w_out"), KC, 1024)
            gn_bc = T(ph, "gn_bc", [128, 512], F32)
            lng = T(ph, "lng", [128, D], F32)
            lnb = T(ph, "lnb", [128, D], F32)
            caus = T(ph, "caus", [128, 128], F32)
            dma("sp", "gn_bc", gn_bc[:], DIN("ml_gn").partition_broadcast(128), writes=["gn_bc"])
            dma("sp", "lng", lng[:], DIN("ln_g").partition_broadcast(128), writes=["lng"])
            dma("sp", "lnb", lnb[:], DIN("ln_b").partition_broadcast(128), writes=["lnb"])
            dma("sp", "caus", caus[:], DIN("caus"), writes=["caus"])
            xb = [T(ph, f"o_xb{i}", [128, D], BF16) for i in range(4)]
            xf = [T(ph, f"o_xf{i}", [128, D], F32) for i in range(2)]
            hT = [T(ph, f"o_hT{i}", [128, KC, 128], BF16) for i in range(3)]
            rope = [T(ph, f"o_rope{i}", [128, 64], F32) for i in range(4)]
            qT = [T(ph, f"o_qT{i}", [128, 4, 128], BF16) for i in range(2)]
            kT = [T(ph, f"o_kT{i}", [128, 4, 128], BF16) for i in range(2)]
            vaug = [T(ph, f"o_vaug{i}", [128, 4, 129], BF16) for i in range(2)]
            ckvn = [T(ph, f"o_ckvn{i}", [128, 256], F32) for i in range(2)]
            krr = [T(ph, f"o_krr{i}", [128, 32], F32) for i in range(2)]
            sig_mo = [T(ph, f"o_sig_mo{i}", [128, 512], BF16) for i in range(2)]
            silu_mz = [T(ph, f"o_silu_mz{i}", [128, 512], BF16) for i in range(2)]
            silu_az = [T(ph, f"o_silu_az{i}", [128, 512], BF16) for i in range(2)]
            y = T(ph, "o_y", [128, D], BF16)
            yT = T(ph, "o_yT", [128, KC, 128], BF16)
            mt = misc_tmp(ph, "p5")
            it = intra_tmp(ph, "o_")
            gt = (T(ph, "o_sqh", [128, 4, 128], F32), T(ph, "o_st4", [128, 24], F32))
            lt = (T(ph, "o_z", [128, D], F32), T(ph, "o_sqz", [128, D], F32), T(ph, "o_st1", [128, 8], F32))
            for i in range(2):
                op("pool", lambda e, i=i: e.memset(vaug[i][:], 1.0), writes=[f"o_vaug{i}"])
            xown = DIN("x_own")
            ropeo = DIN("rope_own")

            def p5_L(g):
                b4 = g % 4
                rows = slice(g * 128, (g + 1) * 128)
                dma("pool", f"o_xb{b4}", xb[b4][:], xown[rows, :], writes=[f"o_xb{b4}"])
                dma("sp", f"o_rope{b4}", rope[b4][:], ropeo[rows, :], writes=[f"o_rope{b4}"])

            def p5_T(g):
                make_hT(xb[g % 4], f"o_xb{g % 4}", hT[g % 3], f"o_hT{g % 3}", g % 2, False)

            def p5_M(g):
                b = g % 2
                rows = slice(g * 128, (g + 1) * 128)
                hn = f"o_hT{g % 3}"
                dma("sp", f"o_xf{b}", xf[b][:], xown[rows, :], writes=[f"o{b}_xf"])
                for h in range(4):
                    featgroup(2, h * 128, hT[g % 3], hn, Wq5, "Wq5", h * 128, 128)
                cp("act", qT[b][:].rearrange("p h t -> p (h t)"), psf(2), [bank(2)], [f"o_qT{b}"])
                for h in range(4):
                    featgroup(3, h * 128, hT[g % 3], hn, Wq5, "Wq5", 512 + h * 128, 128)
                act(kT[b][:].rearrange("p h t -> p (h t)"), psf(3), AF.Identity, [bank(3)], [f"o_kT{b}"], scale=DK ** -0.5)
                tokgroup(4, hT[g % 3], hn, Wv5, "Wv5", 0, 512)
                cp("act", vaug[b][:, :, 0:128], psf(4).rearrange("p (h v) -> p h v", v=128), [bank(4)], [f"o_vaug{b}"])
                tokgroup(5, hT[g % 3], hn, Wv5, "Wv5", 512, 328)
                misc_post(5, rope[g % 4], f"o_rope{g % 4}", ckvn[b], f"o_ckvn{b}", krr[b], f"o_krr{b}", mt)
                dma("sp", f"o_ckv{b}", outs["o_ckv"][rows, :], ckvn[b][:], reads=[f"o_ckvn{b}"], writes=["out_ckv"])
                dma("sp", f"o_kr{b}", outs["o_kr"][rows, :], krr[b][:], reads=[f"o_krr{b}"], writes=["out_kr"])
                tokgroup(6, hT[g % 3], hn, Wg5, "Wg5", 0, 512)
                act(sig_mo[b][:], psf(6), AF.Sigmoid, [bank(6)], [f"o_sig_mo{b}"])
                tokgroup(7, hT[g % 3], hn, Wg5, "Wg5", 512, 512)
                act(silu_mz[b][:], psf(7), AF.Silu, [bank(7)], [f"o_silu_mz{b}"])
                tokgroup(2, hT[g % 3], hn, Wg5, "Wg5", 1024, 512)
                act(silu_az[b][:], psf(2), AF.Silu, [bank(2)], [f"o_silu_az{b}"])

            def p5_mid(g):
                b = g % 2
                rn = dict(qT=f"o_qT{b}", kT=f"o_kT{b}", vaug=f"o_vaug{b}", wl="wl_own", e="e_own", mask="caus",
                          sig_mo=f"o_sig_mo{b}", silu_mz=f"o_silu_mz{b}")
                hm = mlstm_intra(rn, "o_", qT[b], kT[b], vaug[b], wl_own[:, :, g], e_own[:, :, g], caus[:],
                                 lambda h: [(qT[b][:, h, :], Cown[:, g, h, :], [f"o_qT{b}", "Cown"])], it)
                ml_post(rn, "o_", hm, sig_mo[b][:], silu_mz[b][:], gn_bc[:], y[:, 0:512], "o_y", gt)
                tt("pool", y[:, 512:1024], attn[:, g, :], silu_az[b][:], ALU.mult, ["attn", f"o_silu_az{b}"], ["o_y"])

            def p5_tail(g):
                b = g % 2
                rows = slice(g * 128, (g + 1) * 128)
                for fc in range(KC):
                    tr(psb(2)[:, fc * 128:(fc + 1) * 128], y[:, fc * 128:(fc + 1) * 128], ident[:],
                       ["o_y", "ident"], [bank(2)])
                cp("act", yT[:].rearrange("p k t -> p (k t)"), psb(2), [bank(2)], ["o_yT"])
                for half in range(2):
                    for fc in range(KC):
                        mm(psf(3 + half), yT[:, fc, :], wout[:, fc, half * 512:(half + 1) * 512], fc == 0, fc == KC - 1,
                           ["o_yT", "wout"], [bank(3 + half)])
                gen = ln_steps(f"o{b}_", "o_", (3, 4), xf[b][:], gate_p, lng, lnb, outs["o_y"][rows, :], "out_y", lt)
                next(gen)
                return gen

            def finish(gen):
                if gen is not None:
                    for _ in gen:
                        pass

            for g in range(3):
                p5_L(g)
            p5_T(0)
            p5_T(1)
            p5_M(0)
            pending = None
            for g in range(NOWN):
                if g + 3 < NOWN:
                    p5_L(g + 3)
                if g + 2 < NOWN:
                    p5_T(g + 2)
                p5_mid(g)
                finish(pending)
                if g + 1 < NOWN:
                    p5_M(g + 1)
                pending = p5_tail(g)
            finish(pending)
            S.emit()
        S.emit()
    return nc, sorted(used_in)


def _rope_tables(pos):
    half = 16
    inv = (10000.0 ** (-np.arange(half, dtype=np.float32) / half)).astype(np.float32)
    ang = pos.astype(np.float32)[:, None] * inv[None, :]
    cos = np.cos(ang).astype(np.float32)
    sin = np.sin(ang).astype(np.float32)
    tok = np.concatenate([cos, cos, -sin, sin], axis=1).astype(np.float32)
    return tok


def _host_inputs(inp):
    f32 = np.float32
    w_in = np.asarray(inp["w_in"][0], f32)
    offs = np.cumsum([0, 512, 512, 512, 4, 4, 512, 512, 384, 256, 32, 512])
    q_, k_, v_, i_, f_, mo_, mz_, cq_, ckv_, kr_, az_ = [slice(int(offs[n]), int(offs[n + 1])) for n in range(11)]
    krw = w_in[:, kr_]
    krperm = np.concatenate([krw[:, 16:32], krw[:, 0:16]], axis=1)
    w_all = np.ascontiguousarray(np.concatenate([w_in[:, k_], w_in[:, v_], w_in[:, ckv_], krw, krperm,
                                                 w_in[:, i_], w_in[:, f_]], axis=1))
    w_own = np.ascontiguousarray(np.concatenate([w_in[:, q_], w_in[:, k_], w_in[:, cq_], w_in[:, mo_],
                                                 w_in[:, mz_], w_in[:, az_]], axis=1))
    b_ada = np.asarray(inp["b_ada"][0], f32)
    b_adaT = np.ascontiguousarray(b_ada.reshape(24, 128).T)
    b_gate = np.ascontiguousarray(b_ada[2048:3072])
    w_uq = np.asarray(inp["mla_w_uq"][0], f32).reshape(384, 8, 96)
    w_uqp = np.zeros_like(w_uq)
    w_uqp[:, :, 64:80] = w_uq[:, :, 80:96]
    w_uqp[:, :, 80:96] = w_uq[:, :, 64:80]
    w_uk = np.asarray(inp["mla_w_uk"][0], f32).reshape(256, 8, 64)
    w_ukp = np.zeros((256, 8, 96), f32)
    w_ukp[:, :, 0:64] = w_uk
    w_ukT = np.ascontiguousarray(w_uk.transpose(2, 1, 0))
    b_if = np.concatenate([np.asarray(inp["ml_b_i"][0], f32), np.asarray(inp["ml_b_f"][0], f32)])
    rope_all = _rope_tables(np.arange(SEQ))
    rope_smp = np.ascontiguousarray(np.tile(_rope_tables(SEQ + np.arange(8)), (16, 1)))
    tri = np.tril(np.ones((128, 128), f32))
    caus = np.ascontiguousarray(tri.T)
    tok = np.arange(128)
    same_b = (tok[:, None] // 8) == (tok[None, :] // 8)
    caus_s = (same_b & (tok[:, None] <= tok[None, :])).astype(f32)
    bmask = np.zeros((128, 16, 128), f32)
    rmask = np.zeros((128, 16), f32)
    nmask = np.zeros((128, 16, 8, 8), f32)
    for b in range(16):
        bmask[:, b, b * 8:(b + 1) * 8] = 1.0
        rmask[b * 8:(b + 1) * 8, b] = 1.0
        for sk in range(8):
            nmask[b * 8 + sk, b, :, sk:] = 1.0
    segmask = np.ones((128, 512), f32)
    segmask[:, 0::128] = 0.0
    seg8 = np.ones((4, 128), f32)
    seg8[:, 0::8] = 0.0
    common = {
        "w_ada": np.asarray(inp["w_ada"][0], f32), "b_adaT": b_adaT, "b_gate": b_gate,
        "w_all": w_all, "w_own": w_own, "rope_all": rope_all, "rope_smp": rope_smp,
        "ident": np.eye(128, dtype=f32), "tstrict": np.triu(np.ones((64, 64), f32), 1), "segmask": segmask,
        "caus": caus, "caus_s": caus_s, "bmask": bmask, "rmask": rmask,
        "nmask": np.ascontiguousarray(nmask.reshape(128, 16, 64)), "i4": np.eye(4, dtype=f32), "seg8": seg8,
        "b_if_bc": np.ascontiguousarray(np.tile(b_if[None, :], (64, 1))),
        "b_if_col": np.ascontiguousarray(b_if.reshape(2, 4).T),
        "gkv": np.asarray(inp["mla_kv_norm"][0], f32),
        "gqT": np.ascontiguousarray(np.asarray(inp["mla_q_norm"][0], f32).reshape(3, 128).T),
        "ml_gn": np.asarray(inp["ml_gn"][0], f32), "ln_g": np.asarray(inp["ln_g"][0], f32),
        "ln_b": np.asarray(inp["ln_b"][0], f32),
        "w_uq": np.ascontiguousarray(w_uq), "w_uqp": w_uqp, "w_ukp": w_ukp, "w_ukT": w_ukT,
        "w_uv": np.asarray(inp["mla_w_uv"][0], f32), "w_out": np.asarray(inp["w_out"][0], f32),
        "cache_ckv": np.asarray(inp["cache_ckv"][0], f32), "cache_kr": np.asarray(inp["cache_krope"][0], f32),
    }
    ropeT_smp = np.zeros((128, 2, 128), f32)
    ropeT_smp[0:32, 0, :] = rope_smp[:, 0:32].T
    ropeT_smp[0:32, 1, :] = rope_smp[:, 32:64].T
    ropeT_smp[64:96] = ropeT_smp[0:32]
    common["ropeT_smp"] = ropeT_smp
    per = []
    xp = np.asarray(inp["x_prompt"], f32)
    xs = np.asarray(inp["x_sample"], f32)
    for j in range(8):
        b, jj = j // 4, j % 4
        own = [4 * g + jj for g in range(NOWN)]
        xb = xp[b].reshape(NBLK, 128, D)
        ropeo = rope_all.reshape(NBLK, 128, 64)[own].reshape(NOWN * 128, 64)
        ropeT_own = np.zeros((128, 2, NOWN * 128), f32)
        ropeT_own[64:96, 0, :] = ropeo[:, 0:32].T
        ropeT_own[64:96, 1, :] = ropeo[:, 32:64].T
        maskA = np.zeros((128, 4, 128), f32)
        for i in range(4):
            if i < jj:
                maskA[:, i, :] = 1.0
            elif i == jj:
                maskA[:, i, :] = caus
        selv = np.zeros((128, 4), f32)
        selv[:, jj] = 1.0
        sb = slice(16 * j, 16 * j + 16)
        cT = np.concatenate([np.asarray(inp["c_prompt"], f32)[b][:, None], np.asarray(inp["c_sample"], f32)[sb].T], axis=1)
        pt = np.asarray(inp["page_table"])[sb].astype(np.int32)
        ptab = np.ascontiguousarray(np.concatenate([pt.T, pt.T], axis=0))
        d = dict(common)
        d.update({
            "x_all": np.ascontiguousarray(xp[b]), "x_own": np.ascontiguousarray(xb[own].reshape(NOWN * 128, D)),
            "x_smp": np.ascontiguousarray(xs[sb].reshape(128, D)), "cT": np.ascontiguousarray(cT),
            "rope_own": np.ascontiguousarray(ropeo), "ropeT_own": ropeT_own, "sel": selv, "maskA": maskA,
            "state_C": np.ascontiguousarray(np.asarray(inp["state_C"][0], f32)[sb]),
            "state_nT": np.ascontiguousarray(np.asarray(inp["state_n"][0], f32)[sb].transpose(2, 0, 1)),
            "state_mT": np.ascontiguousarray(np.asarray(inp["state_m"][0], f32)[sb].T),
            "ptab": ptab,
        })
        per.append(d)
    return per


_CACHE = {}


def kernel(**inp):
    if "nc" not in _CACHE:
        _CACHE["nc"] = build()
    nc, used = _CACHE["nc"]
    per = _host_inputs(inp)
    in_maps = [{k: d[k] for k in used} for d in per]
    res = run_bass_kernel_spmd(nc, in_maps, core_ids=list(range(8)))
    R = res.results
    f32 = np.float32
    y_p = np.zeros((2, SEQ, D), f32)
    ckv_p = np.zeros((1, 2, SEQ, 256), f32)
    kr_p = np.zeros((1, 2, SEQ, 32), f32)
    C_p = np.zeros((1, 2, 4, 128, 128), f32)
    n_p = np.zeros((1, 2, 4, 128), f32)
    m_p = np.zeros((1, 2, 4), f32)
    y_s = np.zeros((128, 8, D), f32)
    ckv_s = np.zeros((1, 128, 8, 256), f32)
    kr_s = np.zeros((1, 128, 8, 32), f32)
    C_s = np.zeros((1, 128, 4, 128, 128), f32)
    n_s = np.zeros((1, 128, 4, 128), f32)
    m_s = np.zeros((1, 128, 4), f32)
    for j in range(8):
        b, jj = j // 4, j % 4
        r = R[j]
        own = [4 * g + jj for g in range(NOWN)]
        y_p[b].reshape(NBLK, 128, D)[own] = r["o_y"].reshape(NOWN, 128, D)
        ckv_p[0, b].reshape(NBLK, 128, 256)[own] = r["o_ckv"].reshape(NOWN, 128, 256)
        kr_p[0, b].reshape(NBLK, 128, 32)[own] = r["o_kr"].reshape(NOWN, 128, 32)
        if jj == 0:
            oc = r["o_C"]
            C_p[0, b] = oc[:, :, 0:128].transpose(1, 0, 2)
            n_p[0, b] = oc[:, :, 128].T
            m_p[0, b] = r["o_m"][0]
        sb = slice(16 * j, 16 * j + 16)
        y_s[sb] = r["o_ys"].reshape(16, 8, D)
        ckv_s[0, sb] = r["o_ckvs"].reshape(16, 8, 256)
        kr_s[0, sb] = r["o_krs"].reshape(16, 8, 32)
        C_s[0, sb] = r["o_Cs"].transpose(0, 2, 1, 3)
        n_s[0, sb] = r["o_ns"].transpose(1, 2, 0)
        m_s[0, sb] = r["o_ms"].T
    return (y_p, y_s, ckv_p, kr_p, C_p, n_p, m_p, ckv_s, kr_s, C_s, n_s, m_s)
```
